# Optimizing a Trainium2 kernel written in Bass

```python
import jax, jax.numpy as jnp
from jax import lax
import numpy as np

D_MODEL = 1024
BATCH = 8
SEQ = 2048
DEPTH = 1

HEAD_DIM = 64
N_Q_HEADS = 8
N_KV_HEADS = 2
Q_PER_KV = N_Q_HEADS // N_KV_HEADS
ATTN_WIDTH = N_Q_HEADS * HEAD_DIM
KV_WIDTH = N_KV_HEADS * HEAD_DIM
ROPE_PAIRS = HEAD_DIM // 4
ROPE_THETA = 10000.0
Q_BLOCK = 128
POOL_WINDOWS = (2, 4, 8, 16)
N_POOL_GROUPS = len(POOL_WINDOWS)
POOL_WIDTH = D_MODEL - ATTN_WIDTH
POOL_GROUP_DIM = POOL_WIDTH // N_POOL_GROUPS
MIX_WIDTH = ATTN_WIDTH + POOL_WIDTH
IN_WIDTH = ATTN_WIDTH + 2 * KV_WIDTH + POOL_WIDTH
D_FF = ((8 * D_MODEL // 3 + 255) // 256) * 256
GRID_W = 64
EPS = 1e-6

kernel_name = "hybrid_gqa_axialrope_multiscale_pool_swiglu"


def rmsnorm(x, g):
    xf = x.astype(jnp.float32)
    y = xf * lax.rsqrt(jnp.mean(xf * xf, axis=-1, keepdims=True) + EPS)
    return (y * g.astype(jnp.float32)).astype(x.dtype)


def rope_1d(x, ang):
    xf = x.astype(jnp.float32)
    x1, x2 = jnp.split(xf, 2, axis=-1)
    c = jnp.cos(ang)[None, :, None, :]
    s = jnp.sin(ang)[None, :, None, :]
    return jnp.concatenate([x1 * c - x2 * s, x2 * c + x1 * s], axis=-1).astype(x.dtype)


def axial_rope(x, ang_row, ang_col):
    xr, xc = jnp.split(x, 2, axis=-1)
    return jnp.concatenate([rope_1d(xr, ang_row), rope_1d(xc, ang_col)], axis=-1)


def attention_mixer(q, k, v, q_g, k_g, ang_row, ang_col):
    B, S, _ = q.shape
    q = q.reshape(B, S, N_Q_HEADS, HEAD_DIM)
    k = k.reshape(B, S, N_KV_HEADS, HEAD_DIM)
    v = v.reshape(B, S, N_KV_HEADS, HEAD_DIM)
    q = axial_rope(rmsnorm(q, q_g), ang_row, ang_col)
    k = axial_rope(rmsnorm(k, k_g), ang_row, ang_col)
    n_blk = S // Q_BLOCK
    qb = q.reshape(B, n_blk, Q_BLOCK, N_KV_HEADS, Q_PER_KV, HEAD_DIM).transpose(1, 0, 2, 3, 4, 5)
    scale = HEAD_DIM ** -0.5

    def block(qblk):
        s = jnp.einsum('bqkgd,bskd->bkgqs', qblk, k).astype(jnp.float32) * scale
        p = jax.nn.softmax(s, axis=-1)
        return jnp.einsum('bkgqs,bskd->bqkgd', p.astype(v.dtype), v)

    o = lax.map(block, qb)
    return o.transpose(1, 0, 2, 3, 4, 5).reshape(B, S, ATTN_WIDTH)


def pool_mixer(u, pool_w, pool_b, pool_scale):
    B, S, _ = u.shape
    ug = u.reshape(B, S, N_POOL_GROUPS, POOL_GROUP_DIM)
    uf = ug.astype(jnp.float32)
    csum = jnp.concatenate([jnp.zeros_like(uf[:, :1]), jnp.cumsum(uf, axis=1)], axis=1)
    t = jnp.arange(S)
    pooled = []
    for g, w in enumerate(POOL_WINDOWS):
        lo = jnp.clip(t - w // 2, 0, S)
        hi = jnp.clip(t - w // 2 + w, 0, S)
        cnt = (hi - lo).astype(jnp.float32)[None, :, None]
        cg = csum[:, :, g, :]
        mean = (cg[:, hi] - cg[:, lo]) / cnt
        pooled.append(mean - uf[:, :, g, :])
    pooled = jnp.stack(pooled, axis=1).astype(u.dtype)
    y = jnp.einsum('bgsc,gcd->bsgd', pooled, pool_w) + pool_b[None, None]
    return y.reshape(B, S, POOL_WIDTH) * pool_scale


def setup_inputs(seed: int = 0) -> dict:
    key = jax.random.key(seed)
    ks = jax.random.split(key, 14)
    f32 = jnp.float32

    def nrm(k, shape, fan_in):
        return jax.random.normal(k, shape, f32) * fan_in ** -0.5

    def gain(k, shape):
        return 1.0 + 0.05 * jax.random.normal(k, shape, f32)

    res_scale = (2 * DEPTH) ** -0.5
    return {
        "x": jax.random.normal(ks[0], (BATCH, SEQ, D_MODEL), f32),
        "norm1_g": gain(ks[1], (DEPTH, D_MODEL)),
        "w_in": nrm(ks[2], (DEPTH, D_MODEL, IN_WIDTH), D_MODEL),
        "q_norm_g": gain(ks[3], (DEPTH, HEAD_DIM)),
        "k_norm_g": gain(ks[4], (DEPTH, HEAD_DIM)),
        "pool_w": nrm(ks[5], (DEPTH, N_POOL_GROUPS, POOL_GROUP_DIM, POOL_GROUP_DIM), POOL_GROUP_DIM),
        "pool_b": 0.02 * jax.random.normal(ks[6], (DEPTH, N_POOL_GROUPS, POOL_GROUP_DIM), f32),
        "pool_scale": gain(ks[7], (DEPTH, POOL_WIDTH)),
        "w_out": nrm(ks[8], (DEPTH, MIX_WIDTH, D_MODEL), MIX_WIDTH) * res_scale,
        "norm2_g": gain(ks[9], (DEPTH, D_MODEL)),
        "w_gate": nrm(ks[10], (DEPTH, D_MODEL, D_FF), D_MODEL),
        "w_up": nrm(ks[11], (DEPTH, D_MODEL, D_FF), D_MODEL),
        "w_down": nrm(ks[12], (DEPTH, D_FF, D_MODEL), D_FF) * res_scale,
    }


def reference(x, norm1_g, w_in, q_norm_g, k_norm_g, pool_w, pool_b, pool_scale,
              w_out, norm2_g, w_gate, w_up, w_down):
    B, S, _ = x.shape
    rows = S // GRID_W
    row = jnp.repeat(jnp.arange(rows), GRID_W).astype(jnp.float32)
    col = (jnp.arange(S) % GRID_W).astype(jnp.float32)
    inv_freq = ROPE_THETA ** (-jnp.arange(ROPE_PAIRS, dtype=jnp.float32) / ROPE_PAIRS)
    ang_row = row[:, None] * inv_freq[None, :]
    ang_col = col[:, None] * inv_freq[None, :]

    splits = [ATTN_WIDTH, ATTN_WIDTH + KV_WIDTH, ATTN_WIDTH + 2 * KV_WIDTH]
    for l in range(DEPTH):
        h = rmsnorm(x, norm1_g[l])
        proj = h @ w_in[l]
        q, k, v, u = jnp.split(proj, splits, axis=-1)
        a = attention_mixer(q, k, v, q_norm_g[l], k_norm_g[l], ang_row, ang_col)
        p = pool_mixer(u, pool_w[l], pool_b[l], pool_scale[l])
        x = x + jnp.concatenate([a, p], axis=-1) @ w_out[l]
        h = rmsnorm(x, norm2_g[l])
        x = x + (jax.nn.silu(h @ w_gate[l]) * (h @ w_up[l])) @ w_down[l]
    return x
```

```python
import math
from contextlib import ExitStack

import numpy as np
import concourse.bass as bass
import concourse.mybir as mybir
from concourse.bass_utils import run_bass_kernel_spmd

F32 = mybir.dt.float32
BF16 = mybir.dt.bfloat16
ALU = mybir.AluOpType
AF = mybir.ActivationFunctionType
AX = mybir.AxisListType

S = 2048
D = 1024
NT = 16
NKC = 8
DFF = 2816
EPS = 1e-6
N_CORES = 8

K = 1024
R0 = 0
H0 = 64 * K
M0 = 96 * K
Q0 = 128 * K
X0 = 160 * K
ARENA_BYTES = X0 + 47 * K
NF = 224


def _dsize(dt):
    return mybir.dt.size(dt)


class _Op:
    __slots__ = ("eng", "emit", "deps", "signal", "sem", "val", "chan", "waits", "idx")


class _Chan:
    def __init__(self, sem):
        self.sem = sem
        self.count = 0
        self.last = None


class Sched:
    def __init__(self):
        self.ops = []
        self.frozen = False
        self.recs = {"SB": [], "PSUM": []}

    @staticmethod
    def ranges(ap):
        sp = str(ap.space)
        if sp not in ("SB", "PSUM"):
            return None, []
        dims = ap.ap
        pstride = dims[0][0]
        es = _dsize(ap.dtype)
        off = ap.offset % pstride if pstride > 0 else ap.offset
        free = sorted([(s, c) for s, c in dims[1:] if c > 1 and s > 0])
        run = 1
        rest = []
        for s, c in free:
            if s == run and not rest:
                run *= c
            else:
                rest.append((s, c))
        nouter = 1
        for s, c in rest:
            nouter *= c
        out = []
        if nouter <= 64:
            starts = [off]
            for s, c in rest:
                starts = [b + s * k for b in starts for k in range(c)]
            for b in starts:
                out.append((b * es, (b + run) * es))
        else:
            hi = off + run
            for s, c in rest:
                hi += s * (c - 1)
            out.append((off * es, hi * es))
        if sp == "PSUM":
            out = [((lo // 2048) * 2048, ((hi + 2047) // 2048) * 2048) for lo, hi in out]
        out.sort()
        merged = []
        for lo, hi in out:
            if merged and lo <= merged[-1][1]:
                merged[-1] = (merged[-1][0], max(hi, merged[-1][1]))
            else:
                merged.append((lo, hi))
        return sp, merged

    def op(self, eng, emit, reads=(), writes=(), chan=None, extra_deps=(), force=False):
        if self.frozen and not force:
            return None
        extra_deps = [d for d in extra_deps if d is not None]
        o = _Op()
        o.eng = eng
        o.emit = emit
        o.deps = set(extra_deps)
        o.signal = False
        o.sem = None
        o.val = None
        o.chan = chan
        o.waits = []
        o.idx = len(self.ops)
        is_dma = chan is not None
        for ap in reads:
            sp, rs = self.ranges(ap)
            if sp is None:
                continue
            recs = self.recs[sp]
            for lo, hi in rs:
                keep = []
                for r in recs:
                    if r[2] != o.idx and r[0] < hi and lo < r[1]:
                        if r[3]:
                            o.deps.add(r[2])
                        elif sp == "PSUM" and self.ops[r[2]].eng != eng:
                            o.deps.add(r[2])
                        elif (not is_dma) and self.ops[r[2]].chan is None and self.ops[r[2]].eng == eng \
                                and lo <= r[0] and r[1] <= hi:
                            continue
                    keep.append(r)
                keep.append([lo, hi, o.idx, False])
                self.recs[sp] = recs = keep
        for ap in writes:
            sp, rs = self.ranges(ap)
            if sp is None:
                continue
            recs = self.recs[sp]
            for lo, hi in rs:
                keep = []
                for r in recs:
                    if r[0] < hi and lo < r[1]:
                        if r[2] != o.idx:
                            o.deps.add(r[2])
                        if lo <= r[0] and r[1] <= hi:
                            continue
                    keep.append(r)
                keep.append([lo, hi, o.idx, True])
                self.recs[sp] = recs = keep
        if is_dma:
            if chan.last is not None:
                o.deps.add(chan.last)
            chan.last = o.idx
        o.deps.discard(o.idx)
        self.ops.append(o)
        return o.idx

    def resolve(self, eng_sems):
        ops = self.ops

        def skip(o, d):
            return o.eng == "pe" and d.eng == "pe" and o.chan is None and d.chan is None

        for o in ops:
            for di in o.deps:
                d = ops[di]
                if not skip(o, d):
                    d.signal = True
        counters = {e: 0 for e in eng_sems}
        for o in ops:
            if o.chan is not None:
                o.chan.count += 16
                o.sem = o.chan.sem
                o.val = o.chan.count
            elif o.signal:
                counters[o.eng] += 1
                o.sem = eng_sems[o.eng]
                o.val = counters[o.eng]
        waited = {}
        for o in ops:
            w = {}
            wd = waited.setdefault(o.eng, {})
            for di in o.deps:
                d = ops[di]
                if skip(o, d):
                    continue
                key = id(d.sem)
                if wd.get(key, (None, 0))[1] >= d.val:
                    continue
                if key not in w or w[key][1] < d.val:
                    w[key] = (d.sem, d.val)
            for key, sv in w.items():
                wd[key] = sv
            o.waits = list(w.values())

    def runner(self, eng):
        mine = [o for o in self.ops if o.eng == eng]

        def run(e):
            for o in mine:
                for sem, val in o.waits:
                    e.wait_ge(sem, val)
                if o.emit is None:
                    continue
                ins = o.emit(e)
                if o.chan is not None:
                    ins.then_inc(o.sem, 16)
                elif o.signal:
                    ins.then_inc(o.sem, 1)
        return run


def build_nc(debug=False, stop=99, use_ln=True, interleave=True):
    nc = bass.Bass("TRN2", target_bir_lowering=False)
    dr = {}

    def din(name, shape):
        dr[name] = nc.dram_tensor(name, list(shape), F32, kind="ExternalInput").ap()
        return dr[name]

    x_d = din("x", [S, D])
    w_in_d = din("w_in", [D, 1280])
    w_out_d = din("w_out", [D, D])
    w_gate_d = din("w_gate", [D, DFF])
    w_up_d = din("w_up", [D, DFF])
    w_down_d = din("w_down", [DFF, D])
    pool_w_d = din("pool_w", [4, 128, 128])
    constf_d = din("constf", [128, NF])
    constb_d = din("constb", [128, 384])
    rope_d = din("rope", [128, 2 * S])
    out_d = nc.dram_tensor("out", [S, D], F32, kind="ExternalOutput").ap()
    dbg_d = {}

    sch = Sched()

    with ExitStack() as es:
        es.enter_context(nc.allow_low_precision("bf16 matmul operands, fp32 accumulation"))
        arena = es.enter_context(nc.sbuf_tensor("arena", [128, ARENA_BYTES // 4], F32))
        ps = es.enter_context(nc.psum_tensor("ps", [128, 8, 512], F32))
        eng_sems = {e: es.enter_context(nc.semaphore("sem_" + e)) for e in ("pe", "act", "dve", "pool", "sp")}

        def new_chan(name):
            return _Chan(es.enter_context(nc.semaphore("ch_" + name)))

        def sbv(off, dt, *shape):
            n = 1
            for s_ in shape:
                n *= s_
            nb = n * _dsize(dt)
            assert off % 4 == 0 and nb % 4 == 0 and off + nb <= ARENA_BYTES, (off, nb)
            v = arena[:, off // 4:(off + nb) // 4]
            if dt != F32:
                v = v.bitcast(dt)
            if len(shape) == 2:
                v = v.rearrange("p (a b) -> p a b", a=shape[0], b=shape[1])
            elif len(shape) == 3:
                v = v.rearrange("p (a b c) -> p a b c", a=shape[0], b=shape[1], c=shape[2])
            return v

        def psb(bank):
            return ps[:, bank, :]

        def psb_bf(bank):
            return ps[:, bank, :].bitcast(BF16)

        _bank = [0]

        def nb_():
            b = _bank[0]
            _bank[0] = (b + 1) % 8
            return b

        def dma(q, out, in_, chan):
            return sch.op(q, lambda e: e.dma_start(out=out, in_=in_), reads=[in_], writes=[out], chan=chan)

        def mm(out, lhsT, rhs, start=True, stop=True):
            return sch.op("pe", lambda e: e.matmul(out, lhsT, rhs, start=start, stop=stop),
                          reads=[lhsT, rhs], writes=[out])

        def tr(out, in_, ident):
            return sch.op("pe", lambda e: e.transpose(out, in_, ident), reads=[in_, ident], writes=[out])

        def act(out, in_, func, scale=1.0, bias=None, accum=None):
            reads = [in_]
            writes = [out]
            kw = {}
            if not isinstance(scale, (int, float)):
                reads.append(scale)
            if bias is not None:
                kw["bias"] = bias
                if not isinstance(bias, (int, float)):
                    reads.append(bias)
            if accum is not None:
                kw["accum_out"] = accum
                writes.append(accum)
            return sch.op("act", lambda e: e.activation(out, in_, func, scale=scale, **kw),
                          reads=reads, writes=writes)

        def tt(eng, out, in0, in1, op):
            return sch.op(eng, lambda e: e.tensor_tensor(out, in0, in1, op), reads=[in0, in1], writes=[out])

        def ts(eng, out, in0, s1, s2, op0, op1=None):
            reads = [in0] + [s_ for s_ in (s1, s2) if s_ is not None and not isinstance(s_, (int, float))]
            if op1 is None:
                return sch.op(eng, lambda e: e.tensor_scalar(out, in0, s1, None, op0), reads=reads, writes=[out])
            return sch.op(eng, lambda e: e.tensor_scalar(out, in0, s1, s2, op0, op1), reads=reads, writes=[out])

        def cp(eng, out, in_):
            return sch.op(eng, lambda e: e.tensor_copy(out, in_), reads=[in_], writes=[out])

        def recip(out, in_):
            return sch.op("dve", lambda e: e.reciprocal(out, in_), reads=[in_], writes=[out])

        def memset(eng, ap, val):
            return sch.op(eng, lambda e: e.memset(ap, val), writes=[ap])

        C0 = X0 + 38 * K
        constf = sbv(C0, F32, NF)
        constb = sbv(C0 + 1024, BF16, 384)
        pw = sbv(C0 + 2048, BF16, 4, 128)
        stats = sbv(C0 + 3072, F32, 128)
        rec = [sbv(C0 + 3584 + i * 2048, F32, 512) for i in range(2)]
        g1v = constf[:, 0:8]
        g2v = constf[:, 8:16]
        gq = constf[:, 16:17]
        gk = constf[:, 17:18]
        pool_b = constf[:, 18:22]
        pool_sc = constf[:, 22:26]
        gq_row = constf[:, 26:90]
        gk_row = constf[:, 90:154]
        fixtab = constf[:, 154:218]
        gq_perm = constf[:, 218:219]
        gk_perm = constf[:, 219:220]
        ident = constb[:, 0:128]
        ones_bd = constb[:, 128:256]
        rrot = constb[:, 256:384]
        ss1 = stats[:, 0:16]
        ln1 = stats[:, 16:32]
        rstd1 = stats[:, 32:48]
        ss2 = stats[:, 48:64]
        ln2 = stats[:, 64:80]
        rstd2 = stats[:, 80:96]
        mq = stats[:, 96:97]
        mk = stats[:, 97:98]
        negc = stats[:, 98:99]
        epsq = stats[:, 99:100]
        eps1 = stats[:, 100:101]

        hT = sbv(H0, BF16, NKC, S)
        mixT = sbv(M0, BF16, 8, S)
        qT = sbv(Q0, BF16, 4, S)
        kdup = sbv(Q0 + 16 * K, BF16, 2, S)
        v_aug = sbv(Q0 + 24 * K, BF16, NT, 2, 128)
        cosg_q = sbv(X0, F32, S)
        sing_q = sbv(X0 + 8 * K, F32, S)
        cosg_k = sbv(M0, F32, S)
        sing_k = sbv(M0 + 8 * K, F32, S)
        wring = [sbv(X0 + 16 * K + i * 2048, BF16, NKC, 128) for i in range(4)]
        PT = [sbv(X0 + 32 * K + i * 2048, BF16, 1024) for i in range(3)]
        wout_sb = sbv(X0, BF16, NKC, D)
        UPAD = 16
        UW = S + 2 * UPAD
        Ubuf = [sbv(R0 + i * UW * 4, F32, UW) for i in range(2)]
        tmpAB = [sbv(R0 + (2 + i) * UW * 4, F32, UW) for i in range(2)]
        pooled = sbv(R0 + 4 * UW * 4, BF16, S)
        QK0 = R0 + 4 * UW * 4 + 4096

        class _Scr:
            pass

        scr = []
        NSCR = 3
        for i in range(NSCR):
            b = QK0 + i * 8 * K
            s_ = _Scr()
            s_.sq = sbv(b, BF16, 512)
            s_.abf = sbv(b + 1 * K, BF16, 512)
            s_.rs = sbv(b + 2 * K, F32, 512)
            s_.t2 = sbv(b + 4 * K, F32, 512)
            s_.t1 = sbv(b + 6 * K, F32, 512)
            scr.append(s_)
        assert QK0 + NSCR * 8 * K <= R0 + 64 * K
        xring = [sbv(M0 + 16 * K + i * 4 * K, F32, D) for i in range(3)]
        xn = [sbv(X0 + 32 * K + i * 2 * K, BF16, D) for i in range(3)]
        junk = sbv(M0 + 28 * K, BF16, D)
        Rt = [sbv(R0 + i * 4 * K, F32, D) for i in range(NT)]
        h2T = sbv(Q0, BF16, NKC, S)
        Gb = [sbv(H0, BF16, NKC, 512), sbv(H0 + 24 * K, BF16, NKC, 512)]
        Ub = [sbv(H0 + 8 * K, BF16, NKC, 512), sbv(X0 + 16 * K, BF16, NKC, 512)]
        Db = [sbv(H0 + 16 * K, BF16, 4, D), sbv(X0 + 24 * K, BF16, 4, D)]
        actT = [sbv(M0 + i * 4 * K, BF16, 4, 512) for i in range(3)]
        sg = [sbv(M0 + 12 * K + i * 2 * K, F32, 512) for i in range(2)]
        xn2 = [sbv(X0 + 32 * K + i * 2 * K, BF16, D) for i in range(3)]
        junk2 = sbv(X0 + 38 * K + 3584, BF16, D)

        ch_c = [new_chan("c%d" % i) for i in range(6)]
        ch_x = [new_chan("x%d" % i) for i in range(4)]
        ch_w = [new_chan("w%d" % i) for i in range(4)]
        ch_wo = [new_chan("wo%d" % i) for i in range(2)]
        ch_g = [new_chan("g%d" % i) for i in range(2)]
        ch_u = [new_chan("u%d" % i) for i in range(2)]
        ch_d = [new_chan("d%d" % i) for i in range(2)]
        ch_o = [new_chan("o%d" % i) for i in range(4)]
        ch_dbg = new_chan("dbg")

        def dbg_out(name, ap, shape, dt):
            if not debug:
                return
            d = nc.dram_tensor("dbg_" + name, [128] + list(shape), dt, kind="ExternalOutput").ap()
            dbg_d[name] = d
            dbg_ops.append(dma("sp", d, ap, ch_dbg))

        dbg_ops = []

        dma("sp", constf, constf_d, ch_c[0])
        dma("pool", constb, constb_d, ch_c[1])
        dma("pool", pw, pool_w_d.rearrange("g c d -> c g d"), ch_c[2])
        w_in_v = w_in_d.rearrange("(kc p) n -> p kc n", p=128)

        chunks = [("kd", 0), ("kd", 1), ("v", 0), ("u", 0), ("q", 0), ("u", 1), ("q", 1),
                  ("u", 2), ("q", 2), ("u", 3), ("q", 3)]
        NPRE = 4

        def load_chunk(ci):
            kind, j = chunks[ci]
            slot = ci % 4
            if kind == "q":
                dma("pool", wring[slot], w_in_v[:, :, j * 128:(j + 1) * 128], ch_w[slot])
            elif kind == "kd":
                c0 = 512 + j * 64
                dma("pool", wring[slot][:, :, 0:64], w_in_v[:, :, c0:c0 + 64], ch_w[slot])
                dma("pool", wring[slot][:, :, 64:128], w_in_v[:, :, c0:c0 + 64], ch_w[slot])
            elif kind == "v":
                dma("pool", wring[slot], w_in_v[:, :, 640:768], ch_w[slot])
            else:
                c0 = 768 + j * 128
                dma("pool", wring[slot], w_in_v[:, :, c0:c0 + 128], ch_w[slot])

        for ci in range(4):
            load_chunk(ci)

        memset("dve", epsq, 64.0 * EPS)
        memset("dve", eps1, EPS)
        memset("pool", v_aug[:, :, :, 64:128], 1.0)
        tt("dve", rec[0][:, 0:64], gq_row, gq_row, ALU.mult)
        sch.op("dve", lambda e: e.reduce_max(out=mq, in_=rec[0][:, 0:64], axis=AX.X),
               reads=[rec[0][:, 0:64]], writes=[mq])
        tt("dve", rec[0][:, 64:128], gk_row, gk_row, ALU.mult)
        sch.op("dve", lambda e: e.reduce_max(out=mk, in_=rec[0][:, 64:128], axis=AX.X),
               reads=[rec[0][:, 64:128]], writes=[mk])
        tt("dve", negc, mq, mk, ALU.mult)
        if use_ln:
            act(negc, negc, AF.Ln)
            act(negc, negc, AF.Exp, scale=0.5)
        else:
            act(negc, negc, AF.Sqrt)
        sch.op("dve", lambda e: e.tensor_scalar(negc, negc, -8.0, None, ALU.mult), reads=[negc], writes=[negc])

        def norm_stats(i, xt, ss, lnv, rstd, xnb, jk):
            act(jk, xt, AF.Square, accum=ss[:, i:i + 1])
            if use_ln:
                act(lnv[:, i:i + 1], ss[:, i:i + 1], AF.Ln, scale=1.0 / D, bias=eps1)
                act(rstd[:, i:i + 1], lnv[:, i:i + 1], AF.Exp, scale=-0.5)
            else:
                act(lnv[:, i:i + 1], ss[:, i:i + 1], AF.Sqrt, scale=1.0 / D, bias=eps1)
                recip(rstd[:, i:i + 1], lnv[:, i:i + 1])
            act(xnb, xt, AF.Copy, scale=rstd[:, i:i + 1])

        def norm_transpose(i, xnb, gv, dstT):
            bank = nb_()
            pb = psb_bf(bank)
            for kc in range(NKC):
                tr(pb[:, kc * 128:(kc + 1) * 128], xnb[:, kc * 128:(kc + 1) * 128], ident)
            g3 = gv.rearrange("p (a b) -> p a b", b=1).to_broadcast([128, NKC, 128])
            tt("dve", dstT[:, :, i * 128:(i + 1) * 128], pb.rearrange("p (a b) -> p a b", b=128), g3, ALU.mult)

        unit = [0]
        WIN = [2, 4, 8, 16]

        def proc_chunk_tb(ci, tb):
            kind, j = chunks[ci]
            W = wring[ci % 4]
            cols = slice(tb * 512, (tb + 1) * 512)
            if kind == "v":
                for i in range(tb * 4, tb * 4 + 4):
                    bank = nb_()
                    o_ = psb(bank)[:, 0:128]
                    for kc in range(NKC):
                        mm(o_, hT[:, kc, i * 128:(i + 1) * 128], W[:, kc, :], start=(kc == 0), stop=(kc == NKC - 1))
                    act(v_aug[:, i, :, 0:64], o_.rearrange("p (a b) -> p a b", a=2, b=64), AF.Copy)
                return
            bA = nb_()
            for kc in range(NKC):
                mm(psb(bA), W[:, kc, :], hT[:, kc, cols], start=(kc == 0), stop=(kc == NKC - 1))
            if kind == "u":
                act(Ubuf[j % 2][:, UPAD + tb * 512:UPAD + (tb + 1) * 512], psb(bA), AF.Copy)
                return
            sc_ = scr[unit[0] % NSCR]
            unit[0] += 1
            if kind == "q":
                cg, sgn, dest = cosg_q, sing_q, qT[:, j, cols]
            else:
                cg, sgn, dest = cosg_k, sing_k, kdup[:, j, cols]
            act(sc_.sq, psb(bA), AF.Square)
            cp("dve", sc_.abf, psb(bA))
            bB = nb_()
            bC = nb_()
            mm(psb(bB), ones_bd, sc_.sq)
            mm(psb(bC), rrot, sc_.abf)
            if use_ln:
                act(sc_.rs, psb(bB), AF.Ln, bias=epsq)
                act(sc_.rs, sc_.rs, AF.Exp, scale=-0.5)
            else:
                act(sc_.rs, psb(bB), AF.Sqrt, bias=epsq)
                recip(sc_.rs, sc_.rs)
            tt("dve", sc_.t1, psb(bA), cg[:, cols], ALU.mult)
            tt("dve", sc_.t2, psb(bC), sgn[:, cols], ALU.mult)
            tt("pool", sc_.t1, sc_.t1, sc_.t2, ALU.add)
            tt("dve", dest, sc_.t1, sc_.rs, ALU.mult)

        def finish_chunk(ci):
            kind, j = chunks[ci]
            if kind != "u":
                return
            g = j
            U = Ubuf[g % 2]
            prev = U
            exts = [8, 6, 4, 0]
            shifts = [(-1, 0), (-1, 1), (-2, 2), (-4, 4)]
            for l in range(g + 1):
                e_ = exts[l]
                dst = tmpAB[l % 2]
                lo = UPAD - e_
                n_ = S + 2 * e_
                s0, s1 = shifts[l]
                tt("pool", dst[:, lo:lo + n_], prev[:, lo + s0:lo + s0 + n_], prev[:, lo + s1:lo + s1 + n_], ALU.add)
                prev = dst
            Fv = prev
            tt("dve", Fv[:, UPAD:UPAD + 8], Fv[:, UPAD:UPAD + 8], fixtab[:, g * 16:g * 16 + 8], ALU.mult)
            tt("dve", Fv[:, UPAD + S - 8:UPAD + S], Fv[:, UPAD + S - 8:UPAD + S],
               fixtab[:, g * 16 + 8:g * 16 + 16], ALU.mult)
            fin = Fv[:, UPAD:UPAD + S]
            uin = U[:, UPAD:UPAD + S]
            winv = 1.0 / WIN[g]
            sch.op("dve", lambda e, fin=fin, uin=uin, winv=winv: e.scalar_tensor_tensor(
                pooled, fin, winv, uin, ALU.mult, ALU.subtract), reads=[fin, uin], writes=[pooled])
            for tb in range(4):
                cols = slice(tb * 512, (tb + 1) * 512)
                bk = nb_()
                mm(psb(bk), pw[:, g, :], pooled[:, cols])
                ts("dve", mixT[:, 4 + g, cols], psb(bk), pool_b[:, g:g + 1], pool_sc[:, g:g + 1], ALU.add, ALU.mult)

        for i in range(NT):
            dma("sp", xring[i % 3], x_d[i * 128:(i + 1) * 128, :], ch_x[i % 3])
            if i == 1:
                dma("sp", cosg_k, rope_d[:, 0:S], ch_c[3])
                dma("sp", sing_k, rope_d[:, S:2 * S], ch_c[4])
                ts("dve", cosg_k, cosg_k, gk, None, ALU.mult)
                ts("dve", sing_k, sing_k, gk_perm, None, ALU.mult)
                dma("sp", cosg_q, rope_d[:, 0:S], ch_c[5])
                dma("sp", sing_q, rope_d[:, S:2 * S], ch_c[3])
                ts("dve", cosg_q, cosg_q, gq, None, ALU.mult)
                ts("dve", sing_q, sing_q, gq_perm, None, ALU.mult)
            norm_stats(i, xring[i % 3], ss1, ln1, rstd1, xn[i % 3], junk)
            norm_transpose(i, xn[i % 3], g1v, hT)
            if interleave and i % 4 == 3:
                tb = i // 4
                for ci in range(NPRE):
                    proc_chunk_tb(ci, tb)
        if stop <= 0:
            sch.frozen = True
        if not interleave:
            for tb in range(4):
                for ci in range(NPRE):
                    proc_chunk_tb(ci, tb)
        for i in range(2):
            memset("pool", Ubuf[i][:, 0:UPAD], 0.0)
            memset("pool", Ubuf[i][:, UPAD + S:UW], 0.0)
        dbg_out("hT", hT, [NKC, S], BF16)
        if stop <= 1:
            sch.frozen = True
        for ci in range(NPRE):
            finish_chunk(ci)
            if ci + 4 < len(chunks):
                load_chunk(ci + 4)
        for ci in range(NPRE, len(chunks)):
            for tb in range(4):
                proc_chunk_tb(ci, tb)
            finish_chunk(ci)
            if ci + 4 < len(chunks):
                load_chunk(ci + 4)
        dbg_out("qT", qT, [4, S], BF16)
        dbg_out("kdup", kdup, [2, S], BF16)
        dbg_out("v_aug", v_aug, [NT, 2, 128], BF16)

        if stop <= 2:
            sch.frozen = True
        w_out_v = w_out_d.rearrange("(kc p) n -> p kc n", p=128)
        for hlf in range(2):
            dma("pool", wout_sb[:, hlf * 4:(hlf + 1) * 4, :], w_out_v[:, hlf * 4:(hlf + 1) * 4, :], ch_wo[hlf])
        w_gate_v = w_gate_d.rearrange("(kc p) n -> p kc n", p=128)
        w_up_v = w_up_d.rearrange("(kc p) n -> p kc n", p=128)
        pieces = [(0, 4), (4, 4), (8, 4), (12, 4), (16, 4), (20, 2)]

        def load_piece(p):
            fc0, nfc = pieces[p]
            s_ = p % 2
            c0 = fc0 * 128
            ncol = nfc * 128
            dma("pool", Gb[s_][:, :, 0:ncol], w_gate_v[:, :, c0:c0 + ncol], ch_g[s_])
            dma("pool", Ub[s_][:, :, 0:ncol], w_up_v[:, :, c0:c0 + ncol], ch_u[s_])
            dma("pool", Db[s_][:, 0:nfc, :], w_down_d[c0:c0 + ncol, :].rearrange("(fc p) n -> p fc n", p=128),
                ch_d[s_])

        load_piece(0)
        load_piece(1)
        for i in range(NT):
            dma("sp", Rt[i], x_d[i * 128:(i + 1) * 128, :], ch_x[i % 4])

        steps = [(j, qc, sc) for j in range(4) for qc in range(4) for sc in range(16)]

        def mm1(idx):
            j, qc, sc = steps[idx]
            kh = j // 2
            sb_ = idx % 2
            q0 = qc * 512
            for hb in range(2):
                pr = hb * 64
                mm(psb(2 * sb_ + hb), kdup[pr:pr + 64, kh, sc * 128:(sc + 1) * 128], qT[pr:pr + 64, j, q0:q0 + 512])

        def expo(idx):
            sb_ = idx % 2
            src = ps[:, 2 * sb_:2 * sb_ + 2, :].rearrange("p a b -> p (a b)")
            act(PT[idx % 3], src, AF.Exp, scale=8.0, bias=negc)

        def mm2(idx):
            j, qc, sc = steps[idx]
            kh = j // 2
            ob = (j * 4 + qc) % 2
            for hb in range(2):
                mm(psb(4 + 2 * ob + hb), v_aug[:, sc, kh, :], PT[idx % 3][:, hb * 512:(hb + 1) * 512],
                   start=(sc == 0), stop=(sc == 15))

        def onorm(j, qc):
            ob = (j * 4 + qc) % 2
            q0 = qc * 512
            for hb in range(2):
                bank = 4 + 2 * ob + hb
                pr = hb * 64
                recip(rec[hb][64:128, :], psb(bank)[64:128, :])
                tt("dve", mixT[pr:pr + 64, j, q0:q0 + 512], psb(bank)[0:64, :], rec[hb][64:128, :], ALU.mult)

        mm1(0)
        for idx in range(len(steps)):
            if idx + 1 < len(steps):
                mm1(idx + 1)
            expo(idx)
            mm2(idx)
            j, qc, sc = steps[idx]
            if sc == 15:
                onorm(j, qc)
        dbg_out("mixT", mixT, [8, S], BF16)
        if stop <= 3:
            sch.frozen = True

        for i in range(NT + 2):
            if i < NT:
                for n in range(2):
                    bank = nb_()
                    for kc in range(NKC):
                        mm(psb(bank), mixT[:, kc, i * 128:(i + 1) * 128], wout_sb[:, kc, n * 512:(n + 1) * 512],
                           start=(kc == 0), stop=(kc == NKC - 1))
                    tt("dve", Rt[i][:, n * 512:(n + 1) * 512], psb(bank), Rt[i][:, n * 512:(n + 1) * 512], ALU.add)
                norm_stats(i, Rt[i], ss2, ln2, rstd2, xn2[i % 3], junk2)
            if i >= 2:
                norm_transpose(i - 2, xn2[(i - 2) % 3], g2v, h2T)
        if debug:
            d = nc.dram_tensor("dbg_x1", [S, D], F32, kind="ExternalOutput").ap()
            dbg_d["x1"] = d
            for i in range(NT):
                dbg_ops.append(dma("sp", d[i * 128:(i + 1) * 128, :], Rt[i], ch_dbg))
        dbg_out("h2T", h2T, [NKC, S], BF16)

        units = [(p, tb) for p in range(len(pieces)) for tb in range(4)]
        out_ops = []
        gu_cnt = [0]

        def gateup(k):
            p, tb = units[k]
            fc0, nfc = pieces[p]
            s_ = p % 2
            slot = k % 3
            cols = slice(tb * 512, (tb + 1) * 512)
            for f in range(nfc):
                pair = gu_cnt[0] % 3
                gu_cnt[0] += 1
                bg, bu = 2 * pair, 2 * pair + 1
                for kc in range(NKC):
                    mm(psb(bg), Gb[s_][:, kc, f * 128:(f + 1) * 128], h2T[:, kc, cols], start=(kc == 0),
                       stop=(kc == NKC - 1))
                for kc in range(NKC):
                    mm(psb(bu), Ub[s_][:, kc, f * 128:(f + 1) * 128], h2T[:, kc, cols], start=(kc == 0),
                       stop=(kc == NKC - 1))
                sgb = sg[gu_cnt[0] % 2]
                act(sgb, psb(bg), AF.Silu)
                tt("dve", actT[slot][:, f, :], psb(bu), sgb, ALU.mult)

        dn_cnt = [0]

        def down(k):
            p, tb = units[k]
            fc0, nfc = pieces[p]
            s_ = p % 2
            slot = k % 3
            for ti in range(4):
                i = tb * 4 + ti
                for n in range(2):
                    bank = 6 + dn_cnt[0] % 2
                    dn_cnt[0] += 1
                    for f in range(nfc):
                        mm(psb(bank), actT[slot][:, f, ti * 128:(ti + 1) * 128], Db[s_][:, f, n * 512:(n + 1) * 512],
                           start=(f == 0), stop=(f == nfc - 1))
                    tt("dve", Rt[i][:, n * 512:(n + 1) * 512], psb(bank), Rt[i][:, n * 512:(n + 1) * 512], ALU.add)
                if p == len(pieces) - 1:
                    out_ops.append(dma("sp", out_d[i * 128:(i + 1) * 128, :], Rt[i], ch_o[i % 4]))
            if tb == 3 and p + 2 < len(pieces):
                load_piece(p + 2)

        gateup(0)
        for k in range(1, len(units)):
            gateup(k)
            down(k - 1)
        down(len(units) - 1)

        sch.op("sp", None, extra_deps=out_ops + dbg_ops, force=True)

        sch.resolve(eng_sems)
        block = es.enter_context(nc.Block())
        block.tensor(sch.runner("pe"))
        block.scalar(sch.runner("act"))
        block.vector(sch.runner("dve"))
        block.gpsimd(sch.runner("pool"))
        block.sync(sch.runner("sp"))
    return nc, dbg_d


def _host_consts():
    f32 = np.float32
    f64 = np.float64
    inv_freq = f64(10000.0) ** (-(np.arange(16, dtype=f64)) / f64(16))
    t = np.arange(S)
    row = (t // 64).astype(f64)
    col = (t % 64).astype(f64)
    ang_row = row[:, None] * inv_freq[None, :]
    ang_col = col[:, None] * inv_freq[None, :]
    ang = np.zeros((64, S), f64)
    for d in range(64):
        a = ang_row if d < 32 else ang_col
        ang[d] = a[:, d % 16]
    cos = np.cos(ang)
    sin = np.sin(ang)
    rope = np.concatenate([np.tile(cos, (2, 1)), np.tile(sin, (2, 1))], axis=1).astype(f32)
    ident = np.eye(128, dtype=f32)
    ones_bd = np.zeros((128, 128), f32)
    ones_bd[:64, :64] = 1
    ones_bd[64:, 64:] = 1
    rr = np.zeros((128, 128), f32)
    perm = np.zeros(128, np.int64)
    for m in range(128):
        if m % 32 < 16:
            rr[m + 16, m] = -1.0
            perm[m] = m + 16
        else:
            rr[m - 16, m] = 1.0
            perm[m] = m - 16
    constb = np.concatenate([ident, ones_bd, rr], axis=1).astype(f32)
    fix = np.ones((4, 16), f64)
    for g, w in enumerate([2, 4, 8, 16]):
        for jj in range(8):
            for side, tt_ in ((0, jj), (1, S - 8 + jj)):
                lo = min(max(tt_ - w // 2, 0), S)
                hi = min(max(tt_ - w // 2 + w, 0), S)
                fix[g, side * 8 + jj] = f64(w) / f64(hi - lo)
    return rope, constb, fix.astype(f32), perm


_NC_CACHE = {}


def _prep_inputs(x, norm1_g, w_in, q_norm_g, k_norm_g, pool_w, pool_b, pool_scale, w_out, norm2_g,
                 w_gate, w_up, w_down):
    f32 = np.float32
    rope, constb, fix, perm = _host_consts()
    constf = np.zeros((128, NF), f32)
    constf[:, 0:8] = np.asarray(norm1_g, f32).reshape(8, 128).T
    constf[:, 8:16] = np.asarray(norm2_g, f32).reshape(8, 128).T
    constf[:, 16] = np.tile(np.asarray(q_norm_g, f32).reshape(64), 2)
    constf[:, 17] = np.tile(np.asarray(k_norm_g, f32).reshape(64), 2)
    constf[:, 18:22] = np.asarray(pool_b, f32).reshape(4, 128).T
    constf[:, 22:26] = np.asarray(pool_scale, f32).reshape(4, 128).T
    constf[:, 26:90] = np.asarray(q_norm_g, f32).reshape(1, 64)
    constf[:, 90:154] = np.asarray(k_norm_g, f32).reshape(1, 64)
    constf[:, 154:218] = fix.reshape(1, 64)
    constf[:, 218] = np.tile(np.asarray(q_norm_g, f32).reshape(64), 2)[perm]
    constf[:, 219] = np.tile(np.asarray(k_norm_g, f32).reshape(64), 2)[perm]
    shared = {
        "w_in": np.ascontiguousarray(np.asarray(w_in, f32)[0]),
        "w_out": np.ascontiguousarray(np.asarray(w_out, f32)[0]),
        "w_gate": np.ascontiguousarray(np.asarray(w_gate, f32)[0]),
        "w_up": np.ascontiguousarray(np.asarray(w_up, f32)[0]),
        "w_down": np.ascontiguousarray(np.asarray(w_down, f32)[0]),
        "pool_w": np.ascontiguousarray(np.asarray(pool_w, f32)[0]),
        "constf": constf,
        "constb": constb,
        "rope": rope,
    }
    xs = np.asarray(x, f32)
    in_maps = []
    for c in range(N_CORES):
        m = dict(shared)
        m["x"] = np.ascontiguousarray(xs[c])
        in_maps.append(m)
    return in_maps


def kernel(x, norm1_g, w_in, q_norm_g, k_norm_g, pool_w, pool_b, pool_scale, w_out, norm2_g,
           w_gate, w_up, w_down):
    in_maps = _prep_inputs(x, norm1_g, w_in, q_norm_g, k_norm_g, pool_w, pool_b, pool_scale, w_out,
                           norm2_g, w_gate, w_up, w_down)
    if "nc" not in _NC_CACHE:
        _NC_CACHE["nc"] = build_nc(debug=False)[0]
    nc = _NC_CACHE["nc"]
    res = run_bass_kernel_spmd(nc, in_maps, core_ids=list(range(N_CORES)))
    out = np.stack([np.asarray(r["out"], np.float32) for r in res.results], axis=0)
    return out
```

```python
import math
from contextlib import ExitStack

import numpy as np
import concourse.bass as bass
import concourse.mybir as mybir
from concourse.bass_utils import run_bass_kernel_spmd

F32 = mybir.dt.float32
BF16 = mybir.dt.bfloat16
ALU = mybir.AluOpType
AF = mybir.ActivationFunctionType
AX = mybir.AxisListType

S = 2048
D = 1024
NT = 16
NKC = 8
DFF = 2816
EPS = 1e-6
N_CORES = 8

K = 1024
R0 = 0
H0 = 64 * K
M0 = 96 * K
Q0 = 128 * K
X0 = 160 * K
ARENA_BYTES = X0 + 47 * K
NF = 224


def _dsize(dt):
    return mybir.dt.size(dt)


class _Op:
    __slots__ = ("eng", "emit", "deps", "signal", "sem", "val", "chan", "waits", "idx")


class _Chan:
    def __init__(self, sem):
        self.sem = sem
        self.count = 0
        self.last = None


class Sched:
    def __init__(self):
        self.ops = []
        self.frozen = False
        self.recs = {"SB": [], "PSUM": []}

    @staticmethod
    def ranges(ap):
        sp = str(ap.space)
        if sp not in ("SB", "PSUM"):
            return None, []
        dims = ap.ap
        pstride = dims[0][0]
        es = _dsize(ap.dtype)
        off = ap.offset % pstride if pstride > 0 else ap.offset
        free = sorted([(s, c) for s, c in dims[1:] if c > 1 and s > 0])
        run = 1
        rest = []
        for s, c in free:
            if s == run and not rest:
                run *= c
            else:
                rest.append((s, c))
        nouter = 1
        for s, c in rest:
            nouter *= c
        out = []
        if nouter <= 64:
            starts = [off]
            for s, c in rest:
                starts = [b + s * k for b in starts for k in range(c)]
            for b in starts:
                out.append((b * es, (b + run) * es))
        else:
            hi = off + run
            for s, c in rest:
                hi += s * (c - 1)
            out.append((off * es, hi * es))
        if sp == "PSUM":
            out = [((lo // 2048) * 2048, ((hi + 2047) // 2048) * 2048) for lo, hi in out]
        out.sort()
        merged = []
        for lo, hi in out:
            if merged and lo <= merged[-1][1]:
                merged[-1] = (merged[-1][0], max(hi, merged[-1][1]))
            else:
                merged.append((lo, hi))
        return sp, merged

    def op(self, eng, emit, reads=(), writes=(), chan=None, extra_deps=(), force=False):
        if self.frozen and not force:
            return None
        extra_deps = [d for d in extra_deps if d is not None]
        o = _Op()
        o.eng = eng
        o.emit = emit
        o.deps = set(extra_deps)
        o.signal = False
        o.sem = None
        o.val = None
        o.chan = chan
        o.waits = []
        o.idx = len(self.ops)
        is_dma = chan is not None
        for ap in reads:
            sp, rs = self.ranges(ap)
            if sp is None:
                continue
            recs = self.recs[sp]
            for lo, hi in rs:
                keep = []
                for r in recs:
                    if r[2] != o.idx and r[0] < hi and lo < r[1]:
                        if r[3]:
                            o.deps.add(r[2])
                        elif sp == "PSUM" and self.ops[r[2]].eng != eng:
                            o.deps.add(r[2])
                        elif (not is_dma) and self.ops[r[2]].chan is None and self.ops[r[2]].eng == eng \
                                and lo <= r[0] and r[1] <= hi:
                            continue
                    keep.append(r)
                keep.append([lo, hi, o.idx, False])
                self.recs[sp] = recs = keep
        for ap in writes:
            sp, rs = self.ranges(ap)
            if sp is None:
                continue
            recs = self.recs[sp]
            for lo, hi in rs:
                keep = []
                for r in recs:
                    if r[0] < hi and lo < r[1]:
                        if r[2] != o.idx:
                            o.deps.add(r[2])
                        if lo <= r[0] and r[1] <= hi:
                            continue
                    keep.append(r)
                keep.append([lo, hi, o.idx, True])
                self.recs[sp] = recs = keep
        if is_dma:
            if chan.last is not None:
                o.deps.add(chan.last)
            chan.last = o.idx
        o.deps.discard(o.idx)
        self.ops.append(o)
        return o.idx

    def resolve(self, eng_sems):
        ops = self.ops

        def skip(o, d):
            return o.eng == "pe" and d.eng == "pe" and o.chan is None and d.chan is None

        for o in ops:
            for di in o.deps:
                d = ops[di]
                if not skip(o, d):
                    d.signal = True
        counters = {e: 0 for e in eng_sems}
        for o in ops:
            if o.chan is not None:
                o.chan.count += 16
                o.sem = o.chan.sem
                o.val = o.chan.count
            elif o.signal:
                counters[o.eng] += 1
                o.sem = eng_sems[o.eng]
                o.val = counters[o.eng]
        waited = {}
        for o in ops:
            w = {}
            wd = waited.setdefault(o.eng, {})
            for di in o.deps:
                d = ops[di]
                if skip(o, d):
                    continue
                key = id(d.sem)
                if wd.get(key, (None, 0))[1] >= d.val:
                    continue
                if key not in w or w[key][1] < d.val:
                    w[key] = (d.sem, d.val)
            for key, sv in w.items():
                wd[key] = sv
            o.waits = list(w.values())

    def runner(self, eng):
        mine = [o for o in self.ops if o.eng == eng]

        def run(e):
            for o in mine:
                for sem, val in o.waits:
                    e.wait_ge(sem, val)
                if o.emit is None:
                    continue
                ins = o.emit(e)
                if o.chan is not None:
                    ins.then_inc(o.sem, 16)
                elif o.signal:
                    ins.then_inc(o.sem, 1)
        return run


def build_nc(debug=False, stop=99, use_ln=True, interleave=True):
    nc = bass.Bass("TRN2", target_bir_lowering=False)
    dr = {}

    def din(name, shape):
        dr[name] = nc.dram_tensor(name, list(shape), F32, kind="ExternalInput").ap()
        return dr[name]

    x_d = din("x", [S, D])
    w_in_d = din("w_in", [D, 1280])
    w_out_d = din("w_out", [D, D])
    w_gate_d = din("w_gate", [D, DFF])
    w_up_d = din("w_up", [D, DFF])
    w_down_d = din("w_down", [DFF, D])
    pool_w_d = din("pool_w", [4, 128, 128])
    constf_d = din("constf", [128, NF])
    constb_d = din("constb", [128, 384])
    rope_d = din("rope", [128, 2 * S])
    out_d = nc.dram_tensor("out", [S, D], F32, kind="ExternalOutput").ap()
    dbg_d = {}

    sch = Sched()

    with ExitStack() as es:
        es.enter_context(nc.allow_low_precision("bf16 matmul operands, fp32 accumulation"))
        arena = es.enter_context(nc.sbuf_tensor("arena", [128, ARENA_BYTES // 4], F32))
        ps = es.enter_context(nc.psum_tensor("ps", [128, 8, 512], F32))
        eng_sems = {e: es.enter_context(nc.semaphore("sem_" + e)) for e in ("pe", "act", "dve", "pool", "sp")}

        def new_chan(name):
            return _Chan(es.enter_context(nc.semaphore("ch_" + name)))

        def sbv(off, dt, *shape):
            n = 1
            for s_ in shape:
                n *= s_
            nb = n * _dsize(dt)
            assert off % 4 == 0 and nb % 4 == 0 and off + nb <= ARENA_BYTES, (off, nb)
            v = arena[:, off // 4:(off + nb) // 4]
            if dt != F32:
                v = v.bitcast(dt)
            if len(shape) == 2:
                v = v.rearrange("p (a b) -> p a b", a=shape[0], b=shape[1])
            elif len(shape) == 3:
                v = v.rearrange("p (a b c) -> p a b c", a=shape[0], b=shape[1], c=shape[2])
            return v

        def psb(bank):
            return ps[:, bank, :]

        def psb_bf(bank):
            return ps[:, bank, :].bitcast(BF16)

        _bank = [0]

        def nb_():
            b = _bank[0]
            _bank[0] = (b + 1) % 8
            return b

        def dma(q, out, in_, chan):
            return sch.op(q, lambda e: e.dma_start(out=out, in_=in_), reads=[in_], writes=[out], chan=chan)

        def mm(out, lhsT, rhs, start=True, stop=True):
            return sch.op("pe", lambda e: e.matmul(out, lhsT, rhs, start=start, stop=stop),
                          reads=[lhsT, rhs], writes=[out])

        def tr(out, in_, ident):
            return sch.op("pe", lambda e: e.transpose(out, in_, ident), reads=[in_, ident], writes=[out])

        def act(out, in_, func, scale=1.0, bias=None, accum=None):
            reads = [in_]
            writes = [out]
            kw = {}
            if not isinstance(scale, (int, float)):
                reads.append(scale)
            if bias is not None:
                kw["bias"] = bias
                if not isinstance(bias, (int, float)):
                    reads.append(bias)
            if accum is not None:
                kw["accum_out"] = accum
                writes.append(accum)
            return sch.op("act", lambda e: e.activation(out, in_, func, scale=scale, **kw),
                          reads=reads, writes=writes)

        def tt(eng, out, in0, in1, op):
            return sch.op(eng, lambda e: e.tensor_tensor(out, in0, in1, op), reads=[in0, in1], writes=[out])

        def ts(eng, out, in0, s1, s2, op0, op1=None):
            reads = [in0] + [s_ for s_ in (s1, s2) if s_ is not None and not isinstance(s_, (int, float))]
            if op1 is None:
                return sch.op(eng, lambda e: e.tensor_scalar(out, in0, s1, None, op0), reads=reads, writes=[out])
            return sch.op(eng, lambda e: e.tensor_scalar(out, in0, s1, s2, op0, op1), reads=reads, writes=[out])

        def cp(eng, out, in_):
            return sch.op(eng, lambda e: e.tensor_copy(out, in_), reads=[in_], writes=[out])

        def recip(out, in_):
            return sch.op("dve", lambda e: e.reciprocal(out, in_), reads=[in_], writes=[out])

        def memset(eng, ap, val):
            return sch.op(eng, lambda e: e.memset(ap, val), writes=[ap])

        C0 = X0 + 38 * K
        constf = sbv(C0, F32, NF)
        constb = sbv(C0 + 1024, BF16, 384)
        pw = sbv(C0 + 2048, BF16, 4, 128)
        stats = sbv(C0 + 3072, F32, 128)
        rec = [sbv(C0 + 3584 + i * 2048, F32, 512) for i in range(2)]
        g1v = constf[:, 0:8]
        g2v = constf[:, 8:16]
        gq = constf[:, 16:17]
        gk = constf[:, 17:18]
        pool_b = constf[:, 18:22]
        pool_sc = constf[:, 22:26]
        gq_row = constf[:, 26:90]
        gk_row = constf[:, 90:154]
        fixtab = constf[:, 154:218]
        gq_perm = constf[:, 218:219]
        gk_perm = constf[:, 219:220]
        ident = constb[:, 0:128]
        ones_bd = constb[:, 128:256]
        rrot = constb[:, 256:384]
        ss1 = stats[:, 0:16]
        ln1 = stats[:, 16:32]
        rstd1 = stats[:, 32:48]
        ss2 = stats[:, 48:64]
        ln2 = stats[:, 64:80]
        rstd2 = stats[:, 80:96]
        mq = stats[:, 96:97]
        mk = stats[:, 97:98]
        negc = stats[:, 98:99]
        epsq = stats[:, 99:100]
        eps1 = stats[:, 100:101]
        pool_bs = stats[:, 104:108]

        hT = sbv(H0, BF16, NKC, S)
        mixT = sbv(M0, BF16, 8, S)
        qT = sbv(Q0, BF16, 4, S)
        kdup = sbv(Q0 + 16 * K, BF16, 2, S)
        v_aug = sbv(Q0 + 24 * K, BF16, NT, 2, 128)
        cosg_q = sbv(X0, F32, S)
        sing_q = sbv(X0 + 8 * K, F32, S)
        cosg_k = sbv(M0, F32, S)
        sing_k = sbv(M0 + 8 * K, F32, S)
        wring = [sbv(X0 + 16 * K + i * 2048, BF16, NKC, 128) for i in range(4)]
        PT = [sbv(X0 + 32 * K + i * 2048, BF16, 1024) for i in range(3)]
        wout_sb = sbv(X0, BF16, NKC, D)
        UPAD = 16
        UW = S + 2 * UPAD
        Ubuf = [sbv(R0 + i * UW * 4, F32, UW) for i in range(2)]
        tmpAB = [sbv(R0 + (2 + i) * UW * 4, F32, UW) for i in range(2)]
        pooled = sbv(R0 + 4 * UW * 4, BF16, S)
        QK0 = R0 + 4 * UW * 4 + 4096

        class _Scr:
            pass

        scr = []
        NSCR = 3
        for i in range(NSCR):
            b = QK0 + i * 8 * K
            s_ = _Scr()
            s_.sq = sbv(b, BF16, 512)
            s_.abf = sbv(b + 1 * K, BF16, 512)
            s_.rs = sbv(b + 2 * K, F32, 512)
            s_.t2 = sbv(b + 4 * K, F32, 512)
            s_.t1 = sbv(b + 6 * K, F32, 512)
            scr.append(s_)
        assert QK0 + NSCR * 8 * K <= R0 + 64 * K
        xring = [sbv(M0 + 16 * K + i * 4 * K, F32, D) for i in range(4)]
        xn = [sbv(X0 + 32 * K + i * 2 * K, BF16, D) for i in range(3)]
        junk = sbv(X0 + 24 * K, BF16, D)
        Rt = [sbv(R0 + i * 4 * K, F32, D) for i in range(NT)]
        h2T = sbv(Q0, BF16, NKC, S)
        Gb = [sbv(H0, BF16, NKC, 512), sbv(H0 + 24 * K, BF16, NKC, 512)]
        Ub = [sbv(H0 + 8 * K, BF16, NKC, 512), sbv(X0 + 16 * K, BF16, NKC, 512)]
        Db = [sbv(H0 + 16 * K, BF16, 4, D), sbv(X0 + 24 * K, BF16, 4, D)]
        actT = [sbv(M0 + i * 4 * K, BF16, 4, 512) for i in range(3)]
        sg = [sbv(M0 + 12 * K + i * 2 * K, F32, 512) for i in range(2)]
        xn2 = [sbv(X0 + 32 * K + i * 2 * K, BF16, D) for i in range(3)]
        junk2 = sbv(X0 + 38 * K + 3584, BF16, D)

        ch_c = [new_chan("c%d" % i) for i in range(6)]
        ch_x = [new_chan("x%d" % i) for i in range(4)]
        ch_w = [new_chan("w%d" % i) for i in range(4)]
        ch_wo = [new_chan("wo%d" % i) for i in range(2)]
        ch_g = [new_chan("g%d" % i) for i in range(2)]
        ch_u = [new_chan("u%d" % i) for i in range(2)]
        ch_d = [new_chan("d%d" % i) for i in range(2)]
        ch_o = [new_chan("o%d" % i) for i in range(4)]
        ch_dbg = new_chan("dbg")

        def dbg_out(name, ap, shape, dt):
            if not debug:
                return
            d = nc.dram_tensor("dbg_" + name, [128] + list(shape), dt, kind="ExternalOutput").ap()
            dbg_d[name] = d
            dbg_ops.append(dma("sp", d, ap, ch_dbg))

        dbg_ops = []

        dma("sp", constf, constf_d, ch_c[0])
        dma("pool", constb, constb_d, ch_c[1])
        dma("pool", pw, pool_w_d.rearrange("g c d -> c g d"), ch_c[2])
        w_in_v = w_in_d.rearrange("(kc p) n -> p kc n", p=128)

        chunks = [("kd", 0), ("kd", 1), ("v", 0), ("u", 0), ("q", 0), ("u", 1), ("q", 1),
                  ("u", 2), ("q", 2), ("u", 3), ("q", 3)]
        NPRE = 4

        def load_chunk(ci):
            kind, j = chunks[ci]
            slot = ci % 4
            if kind == "q":
                dma("pool", wring[slot], w_in_v[:, :, j * 128:(j + 1) * 128], ch_w[slot])
            elif kind == "kd":
                c0 = 512 + j * 64
                dma("pool", wring[slot][:, :, 0:64], w_in_v[:, :, c0:c0 + 64], ch_w[slot])
                dma("pool", wring[slot][:, :, 64:128], w_in_v[:, :, c0:c0 + 64], ch_w[slot])
            elif kind == "v":
                dma("pool", wring[slot], w_in_v[:, :, 640:768], ch_w[slot])
            else:
                c0 = 768 + j * 128
                dma("pool", wring[slot], w_in_v[:, :, c0:c0 + 128], ch_w[slot])

        for ci in range(4):
            load_chunk(ci)

        memset("dve", epsq, 64.0 * EPS)
        memset("dve", eps1, EPS)
        tt("dve", pool_bs, pool_b, pool_sc, ALU.mult)
        memset("pool", v_aug[:, :, :, 64:128], 1.0)
        tt("dve", rec[0][:, 0:64], gq_row, gq_row, ALU.mult)
        sch.op("dve", lambda e: e.reduce_max(out=mq, in_=rec[0][:, 0:64], axis=AX.X),
               reads=[rec[0][:, 0:64]], writes=[mq])
        tt("dve", rec[0][:, 64:128], gk_row, gk_row, ALU.mult)
        sch.op("dve", lambda e: e.reduce_max(out=mk, in_=rec[0][:, 64:128], axis=AX.X),
               reads=[rec[0][:, 64:128]], writes=[mk])
        tt("dve", negc, mq, mk, ALU.mult)
        if use_ln:
            act(negc, negc, AF.Ln)
            act(negc, negc, AF.Exp, scale=0.5)
        else:
            act(negc, negc, AF.Sqrt)
        sch.op("dve", lambda e: e.tensor_scalar(negc, negc, -8.0, None, ALU.mult), reads=[negc], writes=[negc])

        def norm_p1(i, xt, ss, jk):
            act(jk, xt, AF.Square, accum=ss[:, i:i + 1])

        def norm_p2(i, xt, ss, lnv, rstd, xnb, scale_eng="act"):
            act(lnv[:, i:i + 1], ss[:, i:i + 1], AF.Ln, scale=1.0 / D, bias=eps1)
            act(rstd[:, i:i + 1], lnv[:, i:i + 1], AF.Exp, scale=-0.5)
            if scale_eng == "act":
                act(xnb, xt, AF.Copy, scale=rstd[:, i:i + 1])
            else:
                ts(scale_eng, xnb, xt, rstd[:, i:i + 1], None, ALU.mult)

        def norm_transpose(i, xnb, gv, dstT):
            bank = nb_()
            pb = psb_bf(bank)
            for kc in range(NKC):
                tr(pb[:, kc * 128:(kc + 1) * 128], xnb[:, kc * 128:(kc + 1) * 128], ident)
            g3 = gv.rearrange("p (a b) -> p a b", b=1).to_broadcast([128, NKC, 128])
            tt("dve", dstT[:, :, i * 128:(i + 1) * 128], pb.rearrange("p (a b) -> p a b", b=128), g3, ALU.mult)

        unit = [0]
        pending = []
        pool_mm_pending = []
        WIN = [2, 4, 8, 16]

        def proc_chunk_tb(ci, tb):
            kind, j = chunks[ci]
            W = wring[ci % 4]
            cols = slice(tb * 512, (tb + 1) * 512)
            if kind == "v":
                for i in range(tb * 4, tb * 4 + 4):
                    bank = nb_()
                    o_ = psb(bank)[:, 0:128]
                    for kc in range(NKC):
                        mm(o_, hT[:, kc, i * 128:(i + 1) * 128], W[:, kc, :], start=(kc == 0), stop=(kc == NKC - 1))
                    cp("dve", v_aug[:, i, :, 0:64], o_.rearrange("p (a b) -> p a b", a=2, b=64))
                return
            bA = nb_()
            for kc in range(NKC):
                mm(psb(bA), W[:, kc, :], hT[:, kc, cols], start=(kc == 0), stop=(kc == NKC - 1))
            if kind == "u":
                cp("dve", Ubuf[j % 2][:, UPAD + tb * 512:UPAD + (tb + 1) * 512], psb(bA))
                return
            sc_ = scr[unit[0] % NSCR]
            unit[0] += 1
            if kind == "q":
                cg, sgn, dest = cosg_q, sing_q, qT[:, j, cols]
            else:
                cg, sgn, dest = cosg_k, sing_k, kdup[:, j, cols]
            act(sc_.sq, psb(bA), AF.Square)
            act(sc_.abf, psb(bA), AF.Copy)
            tt("dve", sc_.t1, psb(bA), cg[:, cols], ALU.mult)

            def part2(sc_=sc_, sgn=sgn, dest=dest, cols=cols):
                bB = nb_()
                bC = nb_()
                mm(psb(bB), ones_bd, sc_.sq)
                mm(psb(bC), rrot, sc_.abf)
                act(sc_.rs, psb(bB), AF.Ln, bias=epsq)
                act(sc_.rs, sc_.rs, AF.Exp, scale=-0.5)
                tt("dve", sc_.t2, psb(bC), sgn[:, cols], ALU.mult)
                tt("dve", sc_.t1, sc_.t1, sc_.t2, ALU.add)
                tt("dve", dest, sc_.t1, sc_.rs, ALU.mult)

            pending.append(part2)
            while len(pending) > 1:
                pending.pop(0)()

        def flush_pending():
            while pending:
                pending.pop(0)()

        def finish_chunk(ci):
            kind, j = chunks[ci]
            if kind != "u":
                return
            g = j
            U = Ubuf[g % 2]
            prev = U
            exts = [8, 6, 4, 0]
            shifts = [(-1, 0), (-1, 1), (-2, 2), (-4, 4)]
            for l in range(g + 1):
                e_ = exts[l]
                dst = tmpAB[l % 2]
                lo = UPAD - e_
                n_ = S + 2 * e_
                s0, s1 = shifts[l]
                tt("pool", dst[:, lo:lo + n_], prev[:, lo + s0:lo + s0 + n_], prev[:, lo + s1:lo + s1 + n_], ALU.add)
                prev = dst
            Fv = prev
            tt("pool", Fv[:, UPAD:UPAD + 8], Fv[:, UPAD:UPAD + 8], fixtab[:, g * 16:g * 16 + 8], ALU.mult)
            tt("pool", Fv[:, UPAD + S - 8:UPAD + S], Fv[:, UPAD + S - 8:UPAD + S],
               fixtab[:, g * 16 + 8:g * 16 + 16], ALU.mult)
            fin = Fv[:, UPAD:UPAD + S]
            uin = U[:, UPAD:UPAD + S]
            winv = 1.0 / WIN[g]
            def pool_mm(g=g, fin=fin, uin=uin, winv=winv):
                sch.op("dve", lambda e: e.scalar_tensor_tensor(pooled, fin, winv, uin, ALU.mult, ALU.subtract),
                       reads=[fin, uin], writes=[pooled])
                for tb in range(4):
                    cols = slice(tb * 512, (tb + 1) * 512)
                    bk = nb_()
                    mm(psb(bk), pw[:, g, :], pooled[:, cols])
                    act(mixT[:, 4 + g, cols], psb(bk), AF.Identity, scale=pool_sc[:, g:g + 1], bias=pool_bs[:, g:g + 1])
            pool_mm_pending.append(pool_mm)

        def flush_pool_mm():
            while pool_mm_pending:
                pool_mm_pending.pop(0)()

        NXR = len(xring)
        for i in range(NT + 2):
            if i < NT:
                dma("sp", xring[i % NXR], x_d[i * 128:(i + 1) * 128, :], ch_x[i % NXR])
            if i == 1:
                dma("sp", cosg_k, rope_d[:, 0:S], ch_c[3])
                dma("sp", sing_k, rope_d[:, S:2 * S], ch_c[4])
                act(cosg_k, cosg_k, AF.Copy, scale=gk)
                act(sing_k, sing_k, AF.Copy, scale=gk_perm)
            if i == 9:
                dma("sp", cosg_q, rope_d[:, 0:S], ch_c[5])
                dma("sp", sing_q, rope_d[:, S:2 * S], ch_c[3])
                act(cosg_q, cosg_q, AF.Copy, scale=gq)
                act(sing_q, sing_q, AF.Copy, scale=gq_perm)
            if i < NT:
                norm_p1(i, xring[i % NXR], ss1, junk)
            if 1 <= i <= NT:
                norm_p2(i - 1, xring[(i - 1) % NXR], ss1, ln1, rstd1, xn[(i - 1) % 3], scale_eng=("act" if (i % 2) else "dve"))
            if i >= 2:
                t_ = i - 2
                norm_transpose(t_, xn[t_ % 3], g1v, hT)
                if interleave and t_ % 4 == 3:
                    for ci in range(NPRE):
                        proc_chunk_tb(ci, t_ // 4)
        if stop <= 0:
            sch.frozen = True
        if not interleave:
            for tb in range(4):
                for ci in range(NPRE):
                    proc_chunk_tb(ci, tb)
        for i in range(2):
            memset("pool", Ubuf[i][:, 0:UPAD], 0.0)
            memset("pool", Ubuf[i][:, UPAD + S:UW], 0.0)
        dbg_out("hT", hT, [NKC, S], BF16)
        if stop <= 1:
            sch.frozen = True
        for ci in range(NPRE):
            finish_chunk(ci)
            if ci + 4 < len(chunks):
                load_chunk(ci + 4)
        groups = [(4, 5), (6, 7), (8, 9), (10,)]
        for grp in groups:
            for tb in range(4):
                for ci in grp:
                    proc_chunk_tb(ci, tb)
                if tb == 3:
                    flush_pool_mm()
            for ci in grp:
                finish_chunk(ci)
                if ci + 4 < len(chunks):
                    load_chunk(ci + 4)
        flush_pending()
        dbg_out("qT", qT, [4, S], BF16)
        dbg_out("kdup", kdup, [2, S], BF16)
        dbg_out("v_aug", v_aug, [NT, 2, 128], BF16)

        if stop <= 2:
            sch.frozen = True
        w_out_v = w_out_d.rearrange("(kc p) n -> p kc n", p=128)
        for hlf in range(2):
            dma("pool", wout_sb[:, hlf * 4:(hlf + 1) * 4, :], w_out_v[:, hlf * 4:(hlf + 1) * 4, :], ch_wo[hlf])
        w_gate_v = w_gate_d.rearrange("(kc p) n -> p kc n", p=128)
        w_up_v = w_up_d.rearrange("(kc p) n -> p kc n", p=128)
        pieces = [(0, 4), (4, 4), (8, 4), (12, 4), (16, 4), (20, 2)]

        def load_piece(p):
            fc0, nfc = pieces[p]
            s_ = p % 2
            c0 = fc0 * 128
            ncol = nfc * 128
            dma("pool", Gb[s_][:, :, 0:ncol], w_gate_v[:, :, c0:c0 + ncol], ch_g[s_])
            dma("pool", Ub[s_][:, :, 0:ncol], w_up_v[:, :, c0:c0 + ncol], ch_u[s_])
            dma("pool", Db[s_][:, 0:nfc, :], w_down_d[c0:c0 + ncol, :].rearrange("(fc p) n -> p fc n", p=128),
                ch_d[s_])

        load_piece(0)
        load_piece(1)
        for i in range(NT):
            dma("sp", Rt[i], x_d[i * 128:(i + 1) * 128, :], ch_x[i % 4])

        steps = [(j, qc, sc) for j in range(4) for qc in range(4) for sc in range(16)]

        def mm1(idx):
            j, qc, sc = steps[idx]
            kh = j // 2
            sb_ = idx % 2
            q0 = qc * 512
            for hb in range(2):
                pr = hb * 64
                mm(psb(2 * sb_ + hb), kdup[pr:pr + 64, kh, sc * 128:(sc + 1) * 128], qT[pr:pr + 64, j, q0:q0 + 512])

        def expo(idx):
            sb_ = idx % 2
            src = ps[:, 2 * sb_:2 * sb_ + 2, :].rearrange("p a b -> p (a b)")
            act(PT[idx % 3], src, AF.Exp, scale=8.0, bias=negc)

        def mm2(idx):
            j, qc, sc = steps[idx]
            kh = j // 2
            ob = (j * 4 + qc) % 2
            for hb in range(2):
                mm(psb(4 + 2 * ob + hb), v_aug[:, sc, kh, :], PT[idx % 3][:, hb * 512:(hb + 1) * 512],
                   start=(sc == 0), stop=(sc == 15))

        def onorm(j, qc):
            ob = (j * 4 + qc) % 2
            q0 = qc * 512
            for hb in range(2):
                bank = 4 + 2 * ob + hb
                pr = hb * 64
                recip(rec[hb][64:128, :], psb(bank)[64:128, :])
                tt("dve", mixT[pr:pr + 64, j, q0:q0 + 512], psb(bank)[0:64, :], rec[hb][64:128, :], ALU.mult)

        mm1(0)
        for idx in range(len(steps)):
            if idx + 1 < len(steps):
                mm1(idx + 1)
            expo(idx)
            mm2(idx)
            if idx == 12:
                flush_pool_mm()
            j, qc, sc = steps[idx]
            if sc == 15:
                onorm(j, qc)
        dbg_out("mixT", mixT, [8, S], BF16)
        if stop <= 3:
            sch.frozen = True

        for i in range(NT + 3):
            if i < NT:
                for n in range(2):
                    bank = nb_()
                    for kc in range(NKC):
                        mm(psb(bank), mixT[:, kc, i * 128:(i + 1) * 128], wout_sb[:, kc, n * 512:(n + 1) * 512],
                           start=(kc == 0), stop=(kc == NKC - 1))
                    tt("dve", Rt[i][:, n * 512:(n + 1) * 512], psb(bank), Rt[i][:, n * 512:(n + 1) * 512], ALU.add)
            if 1 <= i <= NT:
                norm_p1(i - 1, Rt[i - 1], ss2, junk2)
            if 2 <= i <= NT + 1:
                norm_p2(i - 2, Rt[i - 2], ss2, ln2, rstd2, xn2[(i - 2) % 3])
            if i >= 3:
                norm_transpose(i - 3, xn2[(i - 3) % 3], g2v, h2T)
        if debug:
            d = nc.dram_tensor("dbg_x1", [S, D], F32, kind="ExternalOutput").ap()
            dbg_d["x1"] = d
            for i in range(NT):
                dbg_ops.append(dma("sp", d[i * 128:(i + 1) * 128, :], Rt[i], ch_dbg))
        dbg_out("h2T", h2T, [NKC, S], BF16)

        units = [(p, tb) for p in range(len(pieces)) for tb in range(4)]
        out_ops = []
        gu_cnt = [0]

        def gateup(k):
            p, tb = units[k]
            fc0, nfc = pieces[p]
            s_ = p % 2
            slot = k % 3
            cols = slice(tb * 512, (tb + 1) * 512)
            for f in range(nfc):
                pair = gu_cnt[0] % 3
                gu_cnt[0] += 1
                bg, bu = 2 * pair, 2 * pair + 1
                for kc in range(NKC):
                    mm(psb(bg), Gb[s_][:, kc, f * 128:(f + 1) * 128], h2T[:, kc, cols], start=(kc == 0),
                       stop=(kc == NKC - 1))
                for kc in range(NKC):
                    mm(psb(bu), Ub[s_][:, kc, f * 128:(f + 1) * 128], h2T[:, kc, cols], start=(kc == 0),
                       stop=(kc == NKC - 1))
                sgb = sg[gu_cnt[0] % 2]
                act(sgb, psb(bg), AF.Silu)
                tt("dve", actT[slot][:, f, :], psb(bu), sgb, ALU.mult)

        dn_cnt = [0]

        def down(k):
            p, tb = units[k]
            fc0, nfc = pieces[p]
            s_ = p % 2
            slot = k % 3
            for ti in range(4):
                i = tb * 4 + ti
                for n in range(2):
                    bank = 6 + dn_cnt[0] % 2
                    dn_cnt[0] += 1
                    for f in range(nfc):
                        mm(psb(bank), actT[slot][:, f, ti * 128:(ti + 1) * 128], Db[s_][:, f, n * 512:(n + 1) * 512],
                           start=(f == 0), stop=(f == nfc - 1))
                    tt("dve", Rt[i][:, n * 512:(n + 1) * 512], psb(bank), Rt[i][:, n * 512:(n + 1) * 512], ALU.add)
                if p == len(pieces) - 1:
                    out_ops.append(dma("sp", out_d[i * 128:(i + 1) * 128, :], Rt[i], ch_o[i % 4]))
            if tb == 3 and p + 2 < len(pieces):
                load_piece(p + 2)

        gateup(0)
        for k in range(1, len(units)):
            gateup(k)
            down(k - 1)
        down(len(units) - 1)

        sch.op("sp", None, extra_deps=out_ops + dbg_ops, force=True)

        sch.resolve(eng_sems)
        block = es.enter_context(nc.Block())
        block.tensor(sch.runner("pe"))
        block.scalar(sch.runner("act"))
        block.vector(sch.runner("dve"))
        block.gpsimd(sch.runner("pool"))
        block.sync(sch.runner("sp"))
    return nc, dbg_d


def _host_consts():
    f32 = np.float32
    f64 = np.float64
    inv_freq = f64(10000.0) ** (-(np.arange(16, dtype=f64)) / f64(16))
    t = np.arange(S)
    row = (t // 64).astype(f64)
    col = (t % 64).astype(f64)
    ang_row = row[:, None] * inv_freq[None, :]
    ang_col = col[:, None] * inv_freq[None, :]
    ang = np.zeros((64, S), f64)
    for d in range(64):
        a = ang_row if d < 32 else ang_col
        ang[d] = a[:, d % 16]
    cos = np.cos(ang)
    sin = np.sin(ang)
    rope = np.concatenate([np.tile(cos, (2, 1)), np.tile(sin, (2, 1))], axis=1).astype(f32)
    ident = np.eye(128, dtype=f32)
    ones_bd = np.zeros((128, 128), f32)
    ones_bd[:64, :64] = 1
    ones_bd[64:, 64:] = 1
    rr = np.zeros((128, 128), f32)
    perm = np.zeros(128, np.int64)
    for m in range(128):
        if m % 32 < 16:
            rr[m + 16, m] = -1.0
            perm[m] = m + 16
        else:
            rr[m - 16, m] = 1.0
            perm[m] = m - 16
    constb = np.concatenate([ident, ones_bd, rr], axis=1).astype(f32)
    fix = np.ones((4, 16), f64)
    for g, w in enumerate([2, 4, 8, 16]):
        for jj in range(8):
            for side, tt_ in ((0, jj), (1, S - 8 + jj)):
                lo = min(max(tt_ - w // 2, 0), S)
                hi = min(max(tt_ - w // 2 + w, 0), S)
                fix[g, side * 8 + jj] = f64(w) / f64(hi - lo)
    return rope, constb, fix.astype(f32), perm


_NC_CACHE = {}


def _prep_inputs(x, norm1_g, w_in, q_norm_g, k_norm_g, pool_w, pool_b, pool_scale, w_out, norm2_g,
                 w_gate, w_up, w_down):
    f32 = np.float32
    rope, constb, fix, perm = _host_consts()
    constf = np.zeros((128, NF), f32)
    constf[:, 0:8] = np.asarray(norm1_g, f32).reshape(8, 128).T
    constf[:, 8:16] = np.asarray(norm2_g, f32).reshape(8, 128).T
    constf[:, 16] = np.tile(np.asarray(q_norm_g, f32).reshape(64), 2)
    constf[:, 17] = np.tile(np.asarray(k_norm_g, f32).reshape(64), 2)
    constf[:, 18:22] = np.asarray(pool_b, f32).reshape(4, 128).T
    constf[:, 22:26] = np.asarray(pool_scale, f32).reshape(4, 128).T
    constf[:, 26:90] = np.asarray(q_norm_g, f32).reshape(1, 64)
    constf[:, 90:154] = np.asarray(k_norm_g, f32).reshape(1, 64)
    constf[:, 154:218] = fix.reshape(1, 64)
    constf[:, 218] = np.tile(np.asarray(q_norm_g, f32).reshape(64), 2)[perm]
    constf[:, 219] = np.tile(np.asarray(k_norm_g, f32).reshape(64), 2)[perm]
    shared = {
        "w_in": np.ascontiguousarray(np.asarray(w_in, f32)[0]),
        "w_out": np.ascontiguousarray(np.asarray(w_out, f32)[0]),
        "w_gate": np.ascontiguousarray(np.asarray(w_gate, f32)[0]),
        "w_up": np.ascontiguousarray(np.asarray(w_up, f32)[0]),
        "w_down": np.ascontiguousarray(np.asarray(w_down, f32)[0]),
        "pool_w": np.ascontiguousarray(np.asarray(pool_w, f32)[0]),
        "constf": constf,
        "constb": constb,
        "rope": rope,
    }
    xs = np.asarray(x, f32)
    in_maps = []
    for c in range(N_CORES):
        m = dict(shared)
        m["x"] = np.ascontiguousarray(xs[c])
        in_maps.append(m)
    return in_maps


def kernel(x, norm1_g, w_in, q_norm_g, k_norm_g, pool_w, pool_b, pool_scale, w_out, norm2_g,
           w_gate, w_up, w_down):
    in_maps = _prep_inputs(x, norm1_g, w_in, q_norm_g, k_norm_g, pool_w, pool_b, pool_scale, w_out,
                           norm2_g, w_gate, w_up, w_down)
    if "nc" not in _NC_CACHE:
        _NC_CACHE["nc"] = build_nc(debug=False)[0]
    nc = _NC_CACHE["nc"]
    res = run_bass_kernel_spmd(nc, in_maps, core_ids=list(range(N_CORES)))
    out = np.stack([np.asarray(r["out"], np.float32) for r in res.results], axis=0)
    return out
```

```python
import math
from contextlib import ExitStack

import numpy as np
import concourse.bass as bass
import concourse.mybir as mybir
from concourse.bass_utils import run_bass_kernel_spmd

F32 = mybir.dt.float32
BF16 = mybir.dt.bfloat16
ALU = mybir.AluOpType
AF = mybir.ActivationFunctionType
AX = mybir.AxisListType

S = 2048
D = 1024
NT = 16
NKC = 8
DFF = 2816
EPS = 1e-6
N_CORES = 8

K = 1024
R0 = 0
H0 = 64 * K
M0 = 96 * K
Q0 = 128 * K
X0 = 160 * K
ARENA_BYTES = X0 + 47 * K
NF = 224


def _dsize(dt):
    return mybir.dt.size(dt)


class _Op:
    __slots__ = ("eng", "emit", "deps", "signal", "sem", "val", "chan", "waits", "idx")


class _Chan:
    def __init__(self, sem):
        self.sem = sem
        self.count = 0
        self.last = None


class Sched:
    def __init__(self):
        self.ops = []
        self.frozen = False
        self.recs = {"SB": [], "PSUM": []}

    @staticmethod
    def ranges(ap):
        sp = str(ap.space)
        if sp not in ("SB", "PSUM"):
            return None, []
        dims = ap.ap
        pstride = dims[0][0]
        es = _dsize(ap.dtype)
        off = ap.offset % pstride if pstride > 0 else ap.offset
        free = sorted([(s, c) for s, c in dims[1:] if c > 1 and s > 0])
        run = 1
        rest = []
        for s, c in free:
            if s == run and not rest:
                run *= c
            else:
                rest.append((s, c))
        nouter = 1
        for s, c in rest:
            nouter *= c
        out = []
        if nouter <= 64:
            starts = [off]
            for s, c in rest:
                starts = [b + s * k for b in starts for k in range(c)]
            for b in starts:
                out.append((b * es, (b + run) * es))
        else:
            hi = off + run
            for s, c in rest:
                hi += s * (c - 1)
            out.append((off * es, hi * es))
        if sp == "PSUM":
            out = [((lo // 2048) * 2048, ((hi + 2047) // 2048) * 2048) for lo, hi in out]
        out.sort()
        merged = []
        for lo, hi in out:
            if merged and lo <= merged[-1][1]:
                merged[-1] = (merged[-1][0], max(hi, merged[-1][1]))
            else:
                merged.append((lo, hi))
        return sp, merged

    def op(self, eng, emit, reads=(), writes=(), chan=None, extra_deps=(), force=False):
        if self.frozen and not force:
            return None
        extra_deps = [d for d in extra_deps if d is not None]
        o = _Op()
        o.eng = eng
        o.emit = emit
        o.deps = set(extra_deps)
        o.signal = False
        o.sem = None
        o.val = None
        o.chan = chan
        o.waits = []
        o.idx = len(self.ops)
        is_dma = chan is not None
        for ap in reads:
            sp, rs = self.ranges(ap)
            if sp is None:
                continue
            recs = self.recs[sp]
            for lo, hi in rs:
                keep = []
                for r in recs:
                    if r[2] != o.idx and r[0] < hi and lo < r[1]:
                        if r[3]:
                            o.deps.add(r[2])
                        elif sp == "PSUM" and self.ops[r[2]].eng != eng:
                            o.deps.add(r[2])
                        elif (not is_dma) and self.ops[r[2]].chan is None and self.ops[r[2]].eng == eng \
                                and lo <= r[0] and r[1] <= hi:
                            continue
                    keep.append(r)
                keep.append([lo, hi, o.idx, False])
                self.recs[sp] = recs = keep
        for ap in writes:
            sp, rs = self.ranges(ap)
            if sp is None:
                continue
            recs = self.recs[sp]
            for lo, hi in rs:
                keep = []
                for r in recs:
                    if r[0] < hi and lo < r[1]:
                        if r[2] != o.idx:
                            o.deps.add(r[2])
                        if lo <= r[0] and r[1] <= hi:
                            continue
                    keep.append(r)
                keep.append([lo, hi, o.idx, True])
                self.recs[sp] = recs = keep
        if is_dma:
            if chan.last is not None:
                o.deps.add(chan.last)
            chan.last = o.idx
        o.deps.discard(o.idx)
        self.ops.append(o)
        return o.idx

    def resolve(self, eng_sems):
        ops = self.ops

        def skip(o, d):
            return o.eng == "pe" and d.eng == "pe" and o.chan is None and d.chan is None

        for o in ops:
            for di in o.deps:
                d = ops[di]
                if not skip(o, d):
                    d.signal = True
        counters = {e: 0 for e in eng_sems}
        for o in ops:
            if o.chan is not None:
                o.chan.count += 16
                o.sem = o.chan.sem
                o.val = o.chan.count
            elif o.signal:
                counters[o.eng] += 1
                o.sem = eng_sems[o.eng]
                o.val = counters[o.eng]
        waited = {}
        for o in ops:
            w = {}
            wd = waited.setdefault(o.eng, {})
            for di in o.deps:
                d = ops[di]
                if skip(o, d):
                    continue
                key = id(d.sem)
                if wd.get(key, (None, 0))[1] >= d.val:
                    continue
                if key not in w or w[key][1] < d.val:
                    w[key] = (d.sem, d.val)
            for key, sv in w.items():
                wd[key] = sv
            o.waits = list(w.values())

    def runner(self, eng):
        mine = [o for o in self.ops if o.eng == eng]

        def run(e):
            for o in mine:
                for sem, val in o.waits:
                    e.wait_ge(sem, val)
                if o.emit is None:
                    continue
                ins = o.emit(e)
                if o.chan is not None:
                    ins.then_inc(o.sem, 16)
                elif o.signal:
                    ins.then_inc(o.sem, 1)
        return run


def build_nc(debug=False, stop=99, use_ln=True, interleave=True):
    nc = bass.Bass("TRN2", target_bir_lowering=False)
    dr = {}

    def din(name, shape):
        dr[name] = nc.dram_tensor(name, list(shape), F32, kind="ExternalInput").ap()
        return dr[name]

    x_d = din("x", [S, D])
    w_in_d = din("w_in", [D, 1280])
    w_out_d = din("w_out", [D, D])
    w_gate_d = din("w_gate", [D, DFF])
    w_up_d = din("w_up", [D, DFF])
    w_down_d = din("w_down", [DFF, D])
    pool_w_d = din("pool_w", [4, 128, 128])
    constf_d = din("constf", [128, NF])
    constb_d = din("constb", [128, 384])
    rope_d = din("rope", [128, 2 * S])
    out_d = nc.dram_tensor("out", [S, D], F32, kind="ExternalOutput").ap()
    dbg_d = {}

    sch = Sched()

    with ExitStack() as es:
        es.enter_context(nc.allow_low_precision("bf16 matmul operands, fp32 accumulation"))
        arena = es.enter_context(nc.sbuf_tensor("arena", [128, ARENA_BYTES // 4], F32))
        ps = es.enter_context(nc.psum_tensor("ps", [128, 8, 512], F32))
        eng_sems = {e: es.enter_context(nc.semaphore("sem_" + e)) for e in ("pe", "act", "dve", "pool", "sp")}

        def new_chan(name):
            return _Chan(es.enter_context(nc.semaphore("ch_" + name)))

        def sbv(off, dt, *shape):
            n = 1
            for s_ in shape:
                n *= s_
            nb = n * _dsize(dt)
            assert off % 4 == 0 and nb % 4 == 0 and off + nb <= ARENA_BYTES, (off, nb)
            v = arena[:, off // 4:(off + nb) // 4]
            if dt != F32:
                v = v.bitcast(dt)
            if len(shape) == 2:
                v = v.rearrange("p (a b) -> p a b", a=shape[0], b=shape[1])
            elif len(shape) == 3:
                v = v.rearrange("p (a b c) -> p a b c", a=shape[0], b=shape[1], c=shape[2])
            return v

        def psb(bank):
            return ps[:, bank, :]

        def psb_bf(bank):
            return ps[:, bank, :].bitcast(BF16)

        _bank = [0]

        def nb_():
            b = _bank[0]
            _bank[0] = (b + 1) % 8
            return b

        def dma(q, out, in_, chan):
            return sch.op(q, lambda e: e.dma_start(out=out, in_=in_), reads=[in_], writes=[out], chan=chan)

        def mm(out, lhsT, rhs, start=True, stop=True):
            return sch.op("pe", lambda e: e.matmul(out, lhsT, rhs, start=start, stop=stop),
                          reads=[lhsT, rhs], writes=[out])

        def tr(out, in_, ident):
            return sch.op("pe", lambda e: e.transpose(out, in_, ident), reads=[in_, ident], writes=[out])

        def act(out, in_, func, scale=1.0, bias=None, accum=None):
            reads = [in_]
            writes = [out]
            kw = {}
            if not isinstance(scale, (int, float)):
                reads.append(scale)
            if bias is not None:
                kw["bias"] = bias
                if not isinstance(bias, (int, float)):
                    reads.append(bias)
            if accum is not None:
                kw["accum_out"] = accum
                writes.append(accum)
            return sch.op("act", lambda e: e.activation(out, in_, func, scale=scale, **kw),
                          reads=reads, writes=writes)

        def tt(eng, out, in0, in1, op):
            return sch.op(eng, lambda e: e.tensor_tensor(out, in0, in1, op), reads=[in0, in1], writes=[out])

        def ts(eng, out, in0, s1, s2, op0, op1=None):
            reads = [in0] + [s_ for s_ in (s1, s2) if s_ is not None and not isinstance(s_, (int, float))]
            if op1 is None:
                return sch.op(eng, lambda e: e.tensor_scalar(out, in0, s1, None, op0), reads=reads, writes=[out])
            return sch.op(eng, lambda e: e.tensor_scalar(out, in0, s1, s2, op0, op1), reads=reads, writes=[out])

        def cp(eng, out, in_):
            return sch.op(eng, lambda e: e.tensor_copy(out, in_), reads=[in_], writes=[out])

        def recip(out, in_):
            return sch.op("dve", lambda e: e.reciprocal(out, in_), reads=[in_], writes=[out])

        def memset(eng, ap, val):
            return sch.op(eng, lambda e: e.memset(ap, val), writes=[ap])

        C0 = X0 + 38 * K
        constf = sbv(C0, F32, NF)
        constb = sbv(C0 + 1024, BF16, 384)
        pw = sbv(C0 + 2048, BF16, 4, 128)
        stats = sbv(C0 + 3072, F32, 128)
        rec = [sbv(C0 + 3584 + i * 2048, F32, 512) for i in range(2)]
        g1v = constf[:, 0:8]
        g2v = constf[:, 8:16]
        gq = constf[:, 16:17]
        gk = constf[:, 17:18]
        pool_b = constf[:, 18:22]
        pool_sc = constf[:, 22:26]
        gq_row = constf[:, 26:90]
        gk_row = constf[:, 90:154]
        fixtab = constf[:, 154:218]
        gq_perm = constf[:, 218:219]
        gk_perm = constf[:, 219:220]
        ident = constb[:, 0:128]
        ones_bd = constb[:, 128:256]
        rrot = constb[:, 256:384]
        ss1 = stats[:, 0:16]
        ln1 = stats[:, 16:32]
        rstd1 = stats[:, 32:48]
        ss2 = stats[:, 48:64]
        ln2 = stats[:, 64:80]
        rstd2 = stats[:, 80:96]
        mq = stats[:, 96:97]
        mk = stats[:, 97:98]
        negc = stats[:, 98:99]
        epsq = stats[:, 99:100]
        eps1 = stats[:, 100:101]
        pool_bs = stats[:, 104:108]

        hT = sbv(H0, BF16, NKC, S)
        mixT = sbv(M0, BF16, 8, S)
        qT = sbv(Q0, BF16, 4, S)
        kdup = sbv(Q0 + 16 * K, BF16, 2, S)
        v_aug = sbv(Q0 + 24 * K, BF16, NT, 2, 128)
        cosg_q = sbv(X0, F32, S)
        sing_q = sbv(X0 + 8 * K, F32, S)
        cosg_k = sbv(M0, F32, S)
        sing_k = sbv(M0 + 8 * K, F32, S)
        wring = [sbv(X0 + 16 * K + i * 2048, BF16, NKC, 128) for i in range(4)]
        PT = [sbv(X0 + 32 * K + i * 2048, BF16, 1024) for i in range(3)]
        wout_sb = sbv(X0, BF16, NKC, D)
        UPAD = 16
        UW = S + 2 * UPAD
        Ubuf = [sbv(R0 + i * UW * 4, F32, UW) for i in range(2)]
        tmpAB = [sbv(R0 + (2 + i) * UW * 4, F32, UW) for i in range(2)]
        pooled = sbv(R0 + 4 * UW * 4, BF16, S)
        QK0 = R0 + 4 * UW * 4 + 4096

        class _Scr:
            pass

        scr = []
        NSCR = 3
        for i in range(NSCR):
            b = QK0 + i * 8 * K
            s_ = _Scr()
            s_.sq = sbv(b, BF16, 512)
            s_.abf = sbv(b + 1 * K, BF16, 512)
            s_.rs = sbv(b + 2 * K, F32, 512)
            s_.t2 = sbv(b + 4 * K, F32, 512)
            s_.t1 = sbv(b + 6 * K, F32, 512)
            scr.append(s_)
        assert QK0 + NSCR * 8 * K <= R0 + 64 * K
        xring = [sbv(M0 + 16 * K + i * 4 * K, F32, D) for i in range(4)]
        xn = [sbv(X0 + 32 * K + i * 2 * K, BF16, D) for i in range(3)]
        junk = sbv(X0 + 24 * K, BF16, D)
        Rt = [sbv(R0 + i * 4 * K, F32, D) for i in range(NT)]
        h2T = sbv(Q0, BF16, NKC, S)
        Gb = [sbv(H0, BF16, NKC, 512), sbv(H0 + 24 * K, BF16, NKC, 512)]
        Ub = [sbv(H0 + 8 * K, BF16, NKC, 512), sbv(X0 + 16 * K, BF16, NKC, 512)]
        Db = [sbv(H0 + 16 * K, BF16, 4, D), sbv(X0 + 24 * K, BF16, 4, D)]
        actT = [sbv(M0 + i * 4 * K, BF16, 4, 512) for i in range(3)]
        sg = [sbv(M0 + 12 * K + i * 2 * K, F32, 512) for i in range(2)]
        xn2 = [sbv(X0 + 32 * K + i * 2 * K, BF16, D) for i in range(3)]
        junk2 = sbv(X0 + 38 * K + 3584, BF16, D)

        ch_c = [new_chan("c%d" % i) for i in range(6)]
        ch_x = [new_chan("x%d" % i) for i in range(4)]
        ch_w = [new_chan("w%d" % i) for i in range(4)]
        ch_wo = [new_chan("wo%d" % i) for i in range(2)]
        ch_g = [new_chan("g%d" % i) for i in range(2)]
        ch_u = [new_chan("u%d" % i) for i in range(2)]
        ch_d = [new_chan("d%d" % i) for i in range(2)]
        ch_o = [new_chan("o%d" % i) for i in range(4)]
        ch_dbg = new_chan("dbg")

        def dbg_out(name, ap, shape, dt):
            if not debug:
                return
            d = nc.dram_tensor("dbg_" + name, [128] + list(shape), dt, kind="ExternalOutput").ap()
            dbg_d[name] = d
            dbg_ops.append(dma("sp", d, ap, ch_dbg))

        dbg_ops = []

        dma("sp", constf, constf_d, ch_c[0])
        dma("pool", constb, constb_d, ch_c[1])
        dma("pool", pw, pool_w_d.rearrange("g c d -> c g d"), ch_c[2])
        w_in_v = w_in_d.rearrange("(kc p) n -> p kc n", p=128)

        chunks = [("kd", 0), ("kd", 1), ("v", 0), ("u", 0), ("q", 0), ("u", 1), ("q", 1),
                  ("u", 2), ("q", 2), ("u", 3), ("q", 3)]
        NPRE = 4

        def load_chunk(ci):
            kind, j = chunks[ci]
            slot = ci % 4
            if kind == "q":
                dma("pool", wring[slot], w_in_v[:, :, j * 128:(j + 1) * 128], ch_w[slot])
            elif kind == "kd":
                c0 = 512 + j * 64
                dma("pool", wring[slot][:, :, 0:64], w_in_v[:, :, c0:c0 + 64], ch_w[slot])
                dma("pool", wring[slot][:, :, 64:128], w_in_v[:, :, c0:c0 + 64], ch_w[slot])
            elif kind == "v":
                dma("pool", wring[slot], w_in_v[:, :, 640:768], ch_w[slot])
            else:
                c0 = 768 + j * 128
                dma("pool", wring[slot], w_in_v[:, :, c0:c0 + 128], ch_w[slot])

        for ci in range(4):
            load_chunk(ci)

        memset("dve", epsq, 64.0 * EPS)
        memset("dve", eps1, EPS)
        tt("dve", pool_bs, pool_b, pool_sc, ALU.mult)
        memset("pool", v_aug[:, :, :, 64:128], 1.0)
        tt("dve", rec[0][:, 0:64], gq_row, gq_row, ALU.mult)
        sch.op("dve", lambda e: e.reduce_max(out=mq, in_=rec[0][:, 0:64], axis=AX.X),
               reads=[rec[0][:, 0:64]], writes=[mq])
        tt("dve", rec[0][:, 64:128], gk_row, gk_row, ALU.mult)
        sch.op("dve", lambda e: e.reduce_max(out=mk, in_=rec[0][:, 64:128], axis=AX.X),
               reads=[rec[0][:, 64:128]], writes=[mk])
        tt("dve", negc, mq, mk, ALU.mult)
        if use_ln:
            act(negc, negc, AF.Ln)
            act(negc, negc, AF.Exp, scale=0.5)
        else:
            act(negc, negc, AF.Sqrt)
        sch.op("dve", lambda e: e.tensor_scalar(negc, negc, -8.0, None, ALU.mult), reads=[negc], writes=[negc])

        def norm_p1(i, xt, ss, jk):
            act(jk, xt, AF.Square, accum=ss[:, i:i + 1])

        def norm_p2(i, xt, ss, lnv, rstd, xnb, scale_eng="act"):
            act(lnv[:, i:i + 1], ss[:, i:i + 1], AF.Ln, scale=1.0 / D, bias=eps1)
            act(rstd[:, i:i + 1], lnv[:, i:i + 1], AF.Exp, scale=-0.5)
            if scale_eng == "act":
                act(xnb, xt, AF.Copy, scale=rstd[:, i:i + 1])
            else:
                ts(scale_eng, xnb, xt, rstd[:, i:i + 1], None, ALU.mult)

        def norm_transpose(i, xnb, gv, dstT):
            bank = nb_()
            pb = psb_bf(bank)
            for kc in range(NKC):
                tr(pb[:, kc * 128:(kc + 1) * 128], xnb[:, kc * 128:(kc + 1) * 128], ident)
            g3 = gv.rearrange("p (a b) -> p a b", b=1).to_broadcast([128, NKC, 128])
            tt("dve", dstT[:, :, i * 128:(i + 1) * 128], pb.rearrange("p (a b) -> p a b", b=128), g3, ALU.mult)

        unit = [0]
        pending = []
        pool_mm_pending = []
        WIN = [2, 4, 8, 16]

        def proc_chunk_tb(ci, tb):
            kind, j = chunks[ci]
            W = wring[ci % 4]
            cols = slice(tb * 512, (tb + 1) * 512)
            if kind == "v":
                for i in range(tb * 4, tb * 4 + 4):
                    bank = nb_()
                    o_ = psb(bank)[:, 0:128]
                    for kc in range(NKC):
                        mm(o_, hT[:, kc, i * 128:(i + 1) * 128], W[:, kc, :], start=(kc == 0), stop=(kc == NKC - 1))
                    cp("dve", v_aug[:, i, :, 0:64], o_.rearrange("p (a b) -> p a b", a=2, b=64))
                return
            bA = nb_()
            for kc in range(NKC):
                mm(psb(bA), W[:, kc, :], hT[:, kc, cols], start=(kc == 0), stop=(kc == NKC - 1))
            if kind == "u":
                act(Ubuf[j % 2][:, UPAD + tb * 512:UPAD + (tb + 1) * 512], psb(bA), AF.Copy)
                return
            sc_ = scr[unit[0] % NSCR]
            unit[0] += 1
            if kind == "q":
                cg, sgn, dest = cosg_q, sing_q, qT[:, j, cols]
            else:
                cg, sgn, dest = cosg_k, sing_k, kdup[:, j, cols]
            act(sc_.sq, psb(bA), AF.Square)
            act(sc_.abf, psb(bA), AF.Copy)
            tt("dve", sc_.t1, psb(bA), cg[:, cols], ALU.mult)

            def part2(sc_=sc_, sgn=sgn, dest=dest, cols=cols):
                bB = nb_()
                bC = nb_()
                mm(psb(bB), ones_bd, sc_.sq)
                mm(psb(bC), rrot, sc_.abf)
                act(sc_.rs, psb(bB), AF.Ln, bias=epsq)
                act(sc_.rs, sc_.rs, AF.Exp, scale=-0.5)
                tt("dve", sc_.t2, psb(bC), sgn[:, cols], ALU.mult)
                tt("dve", sc_.t1, sc_.t1, sc_.t2, ALU.add)
                tt("dve", dest, sc_.t1, sc_.rs, ALU.mult)

            pending.append(part2)
            while len(pending) > 1:
                pending.pop(0)()

        def flush_pending():
            while pending:
                pending.pop(0)()

        def finish_chunk(ci):
            kind, j = chunks[ci]
            if kind != "u":
                return
            g = j
            U = Ubuf[g % 2]
            prev = U
            exts = [8, 6, 4, 0]
            shifts = [(-1, 0), (-1, 1), (-2, 2), (-4, 4)]
            for l in range(g + 1):
                e_ = exts[l]
                dst = tmpAB[l % 2]
                lo = UPAD - e_
                n_ = S + 2 * e_
                s0, s1 = shifts[l]
                tt("dve", dst[:, lo:lo + n_], prev[:, lo + s0:lo + s0 + n_], prev[:, lo + s1:lo + s1 + n_], ALU.add)
                prev = dst
            Fv = prev
            tt("dve", Fv[:, UPAD:UPAD + 8], Fv[:, UPAD:UPAD + 8], fixtab[:, g * 16:g * 16 + 8], ALU.mult)
            tt("dve", Fv[:, UPAD + S - 8:UPAD + S], Fv[:, UPAD + S - 8:UPAD + S],
               fixtab[:, g * 16 + 8:g * 16 + 16], ALU.mult)
            fin = Fv[:, UPAD:UPAD + S]
            uin = U[:, UPAD:UPAD + S]
            winv = 1.0 / WIN[g]
            sch.op("dve", lambda e, fin=fin, uin=uin, winv=winv: e.scalar_tensor_tensor(
                pooled, fin, winv, uin, ALU.mult, ALU.subtract), reads=[fin, uin], writes=[pooled])

            def pool_mm(g=g):
                for tb in range(4):
                    cols = slice(tb * 512, (tb + 1) * 512)
                    bk = nb_()
                    mm(psb(bk), pw[:, g, :], pooled[:, cols])
                    act(mixT[:, 4 + g, cols], psb(bk), AF.Identity, scale=pool_sc[:, g:g + 1], bias=pool_bs[:, g:g + 1])
            pool_mm_pending.append(pool_mm)

        def flush_pool_mm():
            while pool_mm_pending:
                pool_mm_pending.pop(0)()

        NXR = len(xring)
        for i in range(NT + 2):
            if i < NT:
                dma("sp", xring[i % NXR], x_d[i * 128:(i + 1) * 128, :], ch_x[i % NXR])
            if i == 1:
                dma("sp", cosg_k, rope_d[:, 0:S], ch_c[3])
                dma("sp", sing_k, rope_d[:, S:2 * S], ch_c[4])
            if i == 4:
                act(cosg_k, cosg_k, AF.Copy, scale=gk)
                act(sing_k, sing_k, AF.Copy, scale=gk_perm)
            if i == 9:
                dma("sp", cosg_q, rope_d[:, 0:S], ch_c[5])
                dma("sp", sing_q, rope_d[:, S:2 * S], ch_c[3])
                act(cosg_q, cosg_q, AF.Copy, scale=gq)
                act(sing_q, sing_q, AF.Copy, scale=gq_perm)
            if i < NT:
                norm_p1(i, xring[i % NXR], ss1, junk)
            if 1 <= i <= NT:
                norm_p2(i - 1, xring[(i - 1) % NXR], ss1, ln1, rstd1, xn[(i - 1) % 3], scale_eng=("act" if (i % 2) else "dve"))
            if i >= 2:
                t_ = i - 2
                norm_transpose(t_, xn[t_ % 3], g1v, hT)
                if interleave and t_ % 4 == 3:
                    for ci in range(NPRE):
                        proc_chunk_tb(ci, t_ // 4)
        if stop <= 0:
            sch.frozen = True
        if not interleave:
            for tb in range(4):
                for ci in range(NPRE):
                    proc_chunk_tb(ci, tb)
        for i in range(2):
            memset("pool", Ubuf[i][:, 0:UPAD], 0.0)
            memset("pool", Ubuf[i][:, UPAD + S:UW], 0.0)
        dbg_out("hT", hT, [NKC, S], BF16)
        if stop <= 1:
            sch.frozen = True
        for ci in range(NPRE):
            finish_chunk(ci)
            if ci + 4 < len(chunks):
                load_chunk(ci + 4)
        groups = [(4, 5), (6, 7), (8, 9), (10,)]
        for grp in groups:
            for tb in range(4):
                for ci in grp:
                    proc_chunk_tb(ci, tb)
                if tb == 3:
                    flush_pool_mm()
            for ci in grp:
                finish_chunk(ci)
                if ci + 4 < len(chunks):
                    load_chunk(ci + 4)
        flush_pending()
        dbg_out("qT", qT, [4, S], BF16)
        dbg_out("kdup", kdup, [2, S], BF16)
        dbg_out("v_aug", v_aug, [NT, 2, 128], BF16)

        if stop <= 2:
            sch.frozen = True
        w_out_v = w_out_d.rearrange("(kc p) n -> p kc n", p=128)
        for hlf in range(2):
            dma("pool", wout_sb[:, hlf * 4:(hlf + 1) * 4, :], w_out_v[:, hlf * 4:(hlf + 1) * 4, :], ch_wo[hlf])
        w_gate_v = w_gate_d.rearrange("(kc p) n -> p kc n", p=128)
        w_up_v = w_up_d.rearrange("(kc p) n -> p kc n", p=128)
        pieces = [(0, 4), (4, 4), (8, 4), (12, 4), (16, 4), (20, 2)]

        def load_piece(p):
            fc0, nfc = pieces[p]
            s_ = p % 2
            c0 = fc0 * 128
            ncol = nfc * 128
            dma("pool", Gb[s_][:, :, 0:ncol], w_gate_v[:, :, c0:c0 + ncol], ch_g[s_])
            dma("pool", Ub[s_][:, :, 0:ncol], w_up_v[:, :, c0:c0 + ncol], ch_u[s_])
            dma("pool", Db[s_][:, 0:nfc, :], w_down_d[c0:c0 + ncol, :].rearrange("(fc p) n -> p fc n", p=128),
                ch_d[s_])

        load_piece(0)
        load_piece(1)
        for i in range(NT):
            dma("sp", Rt[i], x_d[i * 128:(i + 1) * 128, :], ch_x[i % 4])

        steps = [(j, qc, sc) for j in range(4) for qc in range(4) for sc in range(16)]

        def mm1(idx):
            j, qc, sc = steps[idx]
            kh = j // 2
            sb_ = idx % 2
            q0 = qc * 512
            for hb in range(2):
                pr = hb * 64
                mm(psb(2 * sb_ + hb), kdup[pr:pr + 64, kh, sc * 128:(sc + 1) * 128], qT[pr:pr + 64, j, q0:q0 + 512])

        def expo(idx):
            sb_ = idx % 2
            src = ps[:, 2 * sb_:2 * sb_ + 2, :].rearrange("p a b -> p (a b)")
            act(PT[idx % 3], src, AF.Exp, scale=8.0, bias=negc)

        def mm2(idx):
            j, qc, sc = steps[idx]
            kh = j // 2
            ob = (j * 4 + qc) % 2
            for hb in range(2):
                mm(psb(4 + 2 * ob + hb), v_aug[:, sc, kh, :], PT[idx % 3][:, hb * 512:(hb + 1) * 512],
                   start=(sc == 0), stop=(sc == 15))

        def onorm(j, qc):
            ob = (j * 4 + qc) % 2
            q0 = qc * 512
            for hb in range(2):
                bank = 4 + 2 * ob + hb
                pr = hb * 64
                recip(rec[hb][64:128, :], psb(bank)[64:128, :])
                tt("dve", mixT[pr:pr + 64, j, q0:q0 + 512], psb(bank)[0:64, :], rec[hb][64:128, :], ALU.mult)

        mm1(0)
        for idx in range(len(steps)):
            if idx + 1 < len(steps):
                mm1(idx + 1)
            expo(idx)
            mm2(idx)
            if idx == 12:
                flush_pool_mm()
            j, qc, sc = steps[idx]
            if sc == 15:
                onorm(j, qc)
        dbg_out("mixT", mixT, [8, S], BF16)
        if stop <= 3:
            sch.frozen = True

        for i in range(NT + 3):
            if i < NT:
                for n in range(2):
                    bank = nb_()
                    for kc in range(NKC):
                        mm(psb(bank), mixT[:, kc, i * 128:(i + 1) * 128], wout_sb[:, kc, n * 512:(n + 1) * 512],
                           start=(kc == 0), stop=(kc == NKC - 1))
                    tt("dve", Rt[i][:, n * 512:(n + 1) * 512], psb(bank), Rt[i][:, n * 512:(n + 1) * 512], ALU.add)
            if 1 <= i <= NT:
                norm_p1(i - 1, Rt[i - 1], ss2, junk2)
            if 2 <= i <= NT + 1:
                norm_p2(i - 2, Rt[i - 2], ss2, ln2, rstd2, xn2[(i - 2) % 3])
            if i >= 3:
                norm_transpose(i - 3, xn2[(i - 3) % 3], g2v, h2T)
        if debug:
            d = nc.dram_tensor("dbg_x1", [S, D], F32, kind="ExternalOutput").ap()
            dbg_d["x1"] = d
            for i in range(NT):
                dbg_ops.append(dma("sp", d[i * 128:(i + 1) * 128, :], Rt[i], ch_dbg))
        dbg_out("h2T", h2T, [NKC, S], BF16)

        units = [(p, tb) for p in range(len(pieces)) for tb in range(4)]
        out_ops = []
        gu_cnt = [0]

        def gateup(k):
            p, tb = units[k]
            fc0, nfc = pieces[p]
            s_ = p % 2
            slot = k % 3
            cols = slice(tb * 512, (tb + 1) * 512)
            for f in range(nfc):
                pair = gu_cnt[0] % 3
                gu_cnt[0] += 1
                bg, bu = 2 * pair, 2 * pair + 1
                for kc in range(NKC):
                    mm(psb(bg), Gb[s_][:, kc, f * 128:(f + 1) * 128], h2T[:, kc, cols], start=(kc == 0),
                       stop=(kc == NKC - 1))
                for kc in range(NKC):
                    mm(psb(bu), Ub[s_][:, kc, f * 128:(f + 1) * 128], h2T[:, kc, cols], start=(kc == 0),
                       stop=(kc == NKC - 1))
                sgb = sg[gu_cnt[0] % 2]
                act(sgb, psb(bg), AF.Silu)
                tt("dve", actT[slot][:, f, :], psb(bu), sgb, ALU.mult)

        dn_cnt = [0]

        def down(k):
            p, tb = units[k]
            fc0, nfc = pieces[p]
            s_ = p % 2
            slot = k % 3
            for ti in range(4):
                i = tb * 4 + ti
                for n in range(2):
                    bank = 6 + dn_cnt[0] % 2
                    dn_cnt[0] += 1
                    for f in range(nfc):
                        mm(psb(bank), actT[slot][:, f, ti * 128:(ti + 1) * 128], Db[s_][:, f, n * 512:(n + 1) * 512],
                           start=(f == 0), stop=(f == nfc - 1))
                    tt("dve", Rt[i][:, n * 512:(n + 1) * 512], psb(bank), Rt[i][:, n * 512:(n + 1) * 512], ALU.add)
                if p == len(pieces) - 1:
                    out_ops.append(dma("sp", out_d[i * 128:(i + 1) * 128, :], Rt[i], ch_o[i % 4]))
            if tb == 3 and p + 2 < len(pieces):
                load_piece(p + 2)

        gateup(0)
        for k in range(1, len(units)):
            gateup(k)
            down(k - 1)
        down(len(units) - 1)

        sch.op("sp", None, extra_deps=out_ops + dbg_ops, force=True)

        sch.resolve(eng_sems)
        block = es.enter_context(nc.Block())
        block.tensor(sch.runner("pe"))
        block.scalar(sch.runner("act"))
        block.vector(sch.runner("dve"))
        block.gpsimd(sch.runner("pool"))
        block.sync(sch.runner("sp"))
    return nc, dbg_d


def _host_consts():
    f32 = np.float32
    f64 = np.float64
    inv_freq = f64(10000.0) ** (-(np.arange(16, dtype=f64)) / f64(16))
    t = np.arange(S)
    row = (t // 64).astype(f64)
    col = (t % 64).astype(f64)
    ang_row = row[:, None] * inv_freq[None, :]
    ang_col = col[:, None] * inv_freq[None, :]
    ang = np.zeros((64, S), f64)
    for d in range(64):
        a = ang_row if d < 32 else ang_col
        ang[d] = a[:, d % 16]
    cos = np.cos(ang)
    sin = np.sin(ang)
    rope = np.concatenate([np.tile(cos, (2, 1)), np.tile(sin, (2, 1))], axis=1).astype(f32)
    ident = np.eye(128, dtype=f32)
    ones_bd = np.zeros((128, 128), f32)
    ones_bd[:64, :64] = 1
    ones_bd[64:, 64:] = 1
    rr = np.zeros((128, 128), f32)
    perm = np.zeros(128, np.int64)
    for m in range(128):
        if m % 32 < 16:
            rr[m + 16, m] = -1.0
            perm[m] = m + 16
        else:
            rr[m - 16, m] = 1.0
            perm[m] = m - 16
    constb = np.concatenate([ident, ones_bd, rr], axis=1).astype(f32)
    fix = np.ones((4, 16), f64)
    for g, w in enumerate([2, 4, 8, 16]):
        for jj in range(8):
            for side, tt_ in ((0, jj), (1, S - 8 + jj)):
                lo = min(max(tt_ - w // 2, 0), S)
                hi = min(max(tt_ - w // 2 + w, 0), S)
                fix[g, side * 8 + jj] = f64(w) / f64(hi - lo)
    return rope, constb, fix.astype(f32), perm


_NC_CACHE = {}


def _prep_inputs(x, norm1_g, w_in, q_norm_g, k_norm_g, pool_w, pool_b, pool_scale, w_out, norm2_g,
                 w_gate, w_up, w_down):
    f32 = np.float32
    rope, constb, fix, perm = _host_consts()
    constf = np.zeros((128, NF), f32)
    constf[:, 0:8] = np.asarray(norm1_g, f32).reshape(8, 128).T
    constf[:, 8:16] = np.asarray(norm2_g, f32).reshape(8, 128).T
    constf[:, 16] = np.tile(np.asarray(q_norm_g, f32).reshape(64), 2)
    constf[:, 17] = np.tile(np.asarray(k_norm_g, f32).reshape(64), 2)
    constf[:, 18:22] = np.asarray(pool_b, f32).reshape(4, 128).T
    constf[:, 22:26] = np.asarray(pool_scale, f32).reshape(4, 128).T
    constf[:, 26:90] = np.asarray(q_norm_g, f32).reshape(1, 64)
    constf[:, 90:154] = np.asarray(k_norm_g, f32).reshape(1, 64)
    constf[:, 154:218] = fix.reshape(1, 64)
    constf[:, 218] = np.tile(np.asarray(q_norm_g, f32).reshape(64), 2)[perm]
    constf[:, 219] = np.tile(np.asarray(k_norm_g, f32).reshape(64), 2)[perm]
    shared = {
        "w_in": np.ascontiguousarray(np.asarray(w_in, f32)[0]),
        "w_out": np.ascontiguousarray(np.asarray(w_out, f32)[0]),
        "w_gate": np.ascontiguousarray(np.asarray(w_gate, f32)[0]),
        "w_up": np.ascontiguousarray(np.asarray(w_up, f32)[0]),
        "w_down": np.ascontiguousarray(np.asarray(w_down, f32)[0]),
        "pool_w": np.ascontiguousarray(np.asarray(pool_w, f32)[0]),
        "constf": constf,
        "constb": constb,
        "rope": rope,
    }
    xs = np.asarray(x, f32)
    in_maps = []
    for c in range(N_CORES):
        m = dict(shared)
        m["x"] = np.ascontiguousarray(xs[c])
        in_maps.append(m)
    return in_maps


def kernel(x, norm1_g, w_in, q_norm_g, k_norm_g, pool_w, pool_b, pool_scale, w_out, norm2_g,
           w_gate, w_up, w_down):
    in_maps = _prep_inputs(x, norm1_g, w_in, q_norm_g, k_norm_g, pool_w, pool_b, pool_scale, w_out,
                           norm2_g, w_gate, w_up, w_down)
    if "nc" not in _NC_CACHE:
        _NC_CACHE["nc"] = build_nc(debug=False)[0]
    nc = _NC_CACHE["nc"]
    res = run_bass_kernel_spmd(nc, in_maps, core_ids=list(range(N_CORES)))
    out = np.stack([np.asarray(r["out"], np.float32) for r in res.results], axis=0)
    return out
```

```python
import math
from contextlib import ExitStack

import numpy as np
import concourse.bass as bass
import concourse.mybir as mybir
from concourse.bass_utils import run_bass_kernel_spmd

F32 = mybir.dt.float32
BF16 = mybir.dt.bfloat16
ALU = mybir.AluOpType
AF = mybir.ActivationFunctionType
AX = mybir.AxisListType

S = 2048
D = 1024
NT = 16
NKC = 8
DFF = 2816
EPS = 1e-6
N_CORES = 8

K = 1024
R0 = 0
H0 = 64 * K
M0 = 96 * K
Q0 = 128 * K
X0 = 160 * K
ARENA_BYTES = X0 + 47 * K
NF = 224


def _dsize(dt):
    return mybir.dt.size(dt)


class _Op:
    __slots__ = ("eng", "emit", "deps", "signal", "sem", "val", "chan", "waits", "idx")


class _Chan:
    def __init__(self, sem):
        self.sem = sem
        self.count = 0
        self.last = None


class Sched:
    def __init__(self):
        self.ops = []
        self.frozen = False
        self.recs = {"SB": [], "PSUM": []}

    @staticmethod
    def ranges(ap):
        sp = str(ap.space)
        if sp not in ("SB", "PSUM"):
            return None, []
        dims = ap.ap
        pstride = dims[0][0]
        es = _dsize(ap.dtype)
        off = ap.offset % pstride if pstride > 0 else ap.offset
        free = sorted([(s, c) for s, c in dims[1:] if c > 1 and s > 0])
        run = 1
        rest = []
        for s, c in free:
            if s == run and not rest:
                run *= c
            else:
                rest.append((s, c))
        nouter = 1
        for s, c in rest:
            nouter *= c
        out = []
        if nouter <= 64:
            starts = [off]
            for s, c in rest:
                starts = [b + s * k for b in starts for k in range(c)]
            for b in starts:
                out.append((b * es, (b + run) * es))
        else:
            hi = off + run
            for s, c in rest:
                hi += s * (c - 1)
            out.append((off * es, hi * es))
        if sp == "PSUM":
            out = [((lo // 2048) * 2048, ((hi + 2047) // 2048) * 2048) for lo, hi in out]
        out.sort()
        merged = []
        for lo, hi in out:
            if merged and lo <= merged[-1][1]:
                merged[-1] = (merged[-1][0], max(hi, merged[-1][1]))
            else:
                merged.append((lo, hi))
        return sp, merged

    def op(self, eng, emit, reads=(), writes=(), chan=None, extra_deps=(), force=False):
        if self.frozen and not force:
            return None
        extra_deps = [d for d in extra_deps if d is not None]
        o = _Op()
        o.eng = eng
        o.emit = emit
        o.deps = set(extra_deps)
        o.signal = False
        o.sem = None
        o.val = None
        o.chan = chan
        o.waits = []
        o.idx = len(self.ops)
        is_dma = chan is not None
        for ap in reads:
            sp, rs = self.ranges(ap)
            if sp is None:
                continue
            recs = self.recs[sp]
            for lo, hi in rs:
                keep = []
                for r in recs:
                    if r[2] != o.idx and r[0] < hi and lo < r[1]:
                        if r[3]:
                            o.deps.add(r[2])
                        elif sp == "PSUM" and self.ops[r[2]].eng != eng:
                            o.deps.add(r[2])
                        elif (not is_dma) and self.ops[r[2]].chan is None and self.ops[r[2]].eng == eng \
                                and lo <= r[0] and r[1] <= hi:
                            continue
                    keep.append(r)
                keep.append([lo, hi, o.idx, False])
                self.recs[sp] = recs = keep
        for ap in writes:
            sp, rs = self.ranges(ap)
            if sp is None:
                continue
            recs = self.recs[sp]
            for lo, hi in rs:
                keep = []
                for r in recs:
                    if r[0] < hi and lo < r[1]:
                        if r[2] != o.idx:
                            o.deps.add(r[2])
                        if lo <= r[0] and r[1] <= hi:
                            continue
                    keep.append(r)
                keep.append([lo, hi, o.idx, True])
                self.recs[sp] = recs = keep
        if is_dma:
            if chan.last is not None:
                o.deps.add(chan.last)
            chan.last = o.idx
        o.deps.discard(o.idx)
        self.ops.append(o)
        return o.idx

    def resolve(self, eng_sems):
        ops = self.ops

        def skip(o, d):
            return o.eng == "pe" and d.eng == "pe" and o.chan is None and d.chan is None

        for o in ops:
            for di in o.deps:
                d = ops[di]
                if not skip(o, d):
                    d.signal = True
        counters = {e: 0 for e in eng_sems}
        for o in ops:
            if o.chan is not None:
                o.chan.count += 16
                o.sem = o.chan.sem
                o.val = o.chan.count
            elif o.signal:
                counters[o.eng] += 1
                o.sem = eng_sems[o.eng]
                o.val = counters[o.eng]
        waited = {}
        for o in ops:
            w = {}
            wd = waited.setdefault(o.eng, {})
            for di in o.deps:
                d = ops[di]
                if skip(o, d):
                    continue
                key = id(d.sem)
                if wd.get(key, (None, 0))[1] >= d.val:
                    continue
                if key not in w or w[key][1] < d.val:
                    w[key] = (d.sem, d.val)
            for key, sv in w.items():
                wd[key] = sv
            o.waits = list(w.values())

    def runner(self, eng):
        mine = [o for o in self.ops if o.eng == eng]

        def run(e):
            for o in mine:
                for sem, val in o.waits:
                    e.wait_ge(sem, val)
                if o.emit is None:
                    continue
                ins = o.emit(e)
                if o.chan is not None:
                    ins.then_inc(o.sem, 16)
                elif o.signal:
                    ins.then_inc(o.sem, 1)
        return run


def build_nc(debug=False, stop=99, use_ln=True, interleave=True):
    nc = bass.Bass("TRN2", target_bir_lowering=False)
    dr = {}

    def din(name, shape):
        dr[name] = nc.dram_tensor(name, list(shape), F32, kind="ExternalInput").ap()
        return dr[name]

    x_d = din("x", [S, D])
    w_in_d = din("w_in", [D, 1280])
    w_out_d = din("w_out", [D, D])
    w_gate_d = din("w_gate", [D, DFF])
    w_up_d = din("w_up", [D, DFF])
    w_down_d = din("w_down", [DFF, D])
    pool_w_d = din("pool_w", [4, 128, 128])
    constf_d = din("constf", [128, NF])
    constb_d = din("constb", [128, 384])
    rope_d = din("rope", [128, 2 * S])
    out_d = nc.dram_tensor("out", [S, D], F32, kind="ExternalOutput").ap()
    dbg_d = {}

    sch = Sched()

    with ExitStack() as es:
        es.enter_context(nc.allow_low_precision("bf16 matmul operands, fp32 accumulation"))
        arena = es.enter_context(nc.sbuf_tensor("arena", [128, ARENA_BYTES // 4], F32))
        ps = es.enter_context(nc.psum_tensor("ps", [128, 8, 512], F32))
        eng_sems = {e: es.enter_context(nc.semaphore("sem_" + e)) for e in ("pe", "act", "dve", "pool", "sp")}

        def new_chan(name):
            return _Chan(es.enter_context(nc.semaphore("ch_" + name)))

        def sbv(off, dt, *shape):
            n = 1
            for s_ in shape:
                n *= s_
            nb = n * _dsize(dt)
            assert off % 4 == 0 and nb % 4 == 0 and off + nb <= ARENA_BYTES, (off, nb)
            v = arena[:, off // 4:(off + nb) // 4]
            if dt != F32:
                v = v.bitcast(dt)
            if len(shape) == 2:
                v = v.rearrange("p (a b) -> p a b", a=shape[0], b=shape[1])
            elif len(shape) == 3:
                v = v.rearrange("p (a b c) -> p a b c", a=shape[0], b=shape[1], c=shape[2])
            return v

        def psb(bank):
            return ps[:, bank, :]

        def psb_bf(bank):
            return ps[:, bank, :].bitcast(BF16)

        _bank = [0]

        def nb_():
            b = _bank[0]
            _bank[0] = (b + 1) % 8
            return b

        def dma(q, out, in_, chan):
            return sch.op(q, lambda e: e.dma_start(out=out, in_=in_), reads=[in_], writes=[out], chan=chan)

        def mm(out, lhsT, rhs, start=True, stop=True):
            return sch.op("pe", lambda e: e.matmul(out, lhsT, rhs, start=start, stop=stop),
                          reads=[lhsT, rhs], writes=[out])

        def tr(out, in_, ident):
            return sch.op("pe", lambda e: e.transpose(out, in_, ident), reads=[in_, ident], writes=[out])

        def act(out, in_, func, scale=1.0, bias=None, accum=None):
            reads = [in_]
            writes = [out]
            kw = {}
            if not isinstance(scale, (int, float)):
                reads.append(scale)
            if bias is not None:
                kw["bias"] = bias
                if not isinstance(bias, (int, float)):
                    reads.append(bias)
            if accum is not None:
                kw["accum_out"] = accum
                writes.append(accum)
            return sch.op("act", lambda e: e.activation(out, in_, func, scale=scale, **kw),
                          reads=reads, writes=writes)

        def tt(eng, out, in0, in1, op):
            return sch.op(eng, lambda e: e.tensor_tensor(out, in0, in1, op), reads=[in0, in1], writes=[out])

        def ts(eng, out, in0, s1, s2, op0, op1=None):
            reads = [in0] + [s_ for s_ in (s1, s2) if s_ is not None and not isinstance(s_, (int, float))]
            if op1 is None:
                return sch.op(eng, lambda e: e.tensor_scalar(out, in0, s1, None, op0), reads=reads, writes=[out])
            return sch.op(eng, lambda e: e.tensor_scalar(out, in0, s1, s2, op0, op1), reads=reads, writes=[out])

        def cp(eng, out, in_):
            return sch.op(eng, lambda e: e.tensor_copy(out, in_), reads=[in_], writes=[out])

        def recip(out, in_):
            return sch.op("dve", lambda e: e.reciprocal(out, in_), reads=[in_], writes=[out])

        def memset(eng, ap, val):
            return sch.op(eng, lambda e: e.memset(ap, val), writes=[ap])

        C0 = X0 + 38 * K
        constf = sbv(C0, F32, NF)
        constb = sbv(C0 + 1024, BF16, 384)
        pw = sbv(C0 + 2048, BF16, 4, 128)
        stats = sbv(C0 + 3072, F32, 128)
        rec = [sbv(C0 + 3584 + i * 2048, F32, 512) for i in range(2)]
        g1v = constf[:, 0:8]
        g2v = constf[:, 8:16]
        gq = constf[:, 16:17]
        gk = constf[:, 17:18]
        pool_b = constf[:, 18:22]
        pool_sc = constf[:, 22:26]
        gq_row = constf[:, 26:90]
        gk_row = constf[:, 90:154]
        fixtab = constf[:, 154:218]
        gq_perm = constf[:, 218:219]
        gk_perm = constf[:, 219:220]
        ident = constb[:, 0:128]
        ones_bd = constb[:, 128:256]
        rrot = constb[:, 256:384]
        ss1 = stats[:, 0:16]
        ln1 = stats[:, 16:32]
        rstd1 = stats[:, 32:48]
        ss2 = stats[:, 48:64]
        ln2 = stats[:, 64:80]
        rstd2 = stats[:, 80:96]
        mq = stats[:, 96:97]
        mk = stats[:, 97:98]
        negc = stats[:, 98:99]
        epsq = stats[:, 99:100]
        eps1 = stats[:, 100:101]
        pool_bs = stats[:, 104:108]

        hT = sbv(H0, BF16, NKC, S)
        mixT = sbv(M0, BF16, 8, S)
        qT = sbv(Q0, BF16, 4, S)
        kdup = sbv(Q0 + 16 * K, BF16, 2, S)
        v_aug = sbv(Q0 + 24 * K, BF16, NT, 2, 128)
        cosg_q = sbv(X0, F32, S)
        sing_q = sbv(X0 + 8 * K, F32, S)
        cosg_k = sbv(M0, F32, S)
        sing_k = sbv(M0 + 8 * K, F32, S)
        wring = [sbv(X0 + 16 * K + i * 2048, BF16, NKC, 128) for i in range(4)]
        PT = [sbv(X0 + 32 * K + i * 2048, BF16, 1024) for i in range(3)]
        wout_sb = sbv(X0, BF16, NKC, D)
        UPAD = 16
        UW = S + 2 * UPAD
        Ubuf = [sbv(R0 + i * UW * 4, F32, UW) for i in range(2)]
        tmpAB = [sbv(R0 + (2 + i) * UW * 4, F32, UW) for i in range(2)]
        pooled = sbv(R0 + 4 * UW * 4, BF16, S)
        QK0 = R0 + 4 * UW * 4 + 4096

        class _Scr:
            pass

        scr = []
        NSCR = 3
        for i in range(NSCR):
            b = QK0 + i * 8 * K
            s_ = _Scr()
            s_.sq = sbv(b, BF16, 512)
            s_.abf = sbv(b + 1 * K, BF16, 512)
            s_.rs = sbv(b + 2 * K, F32, 512)
            s_.t2 = sbv(b + 4 * K, F32, 512)
            s_.t1 = sbv(b + 6 * K, F32, 512)
            scr.append(s_)
        assert QK0 + NSCR * 8 * K <= R0 + 64 * K
        xring = [sbv(M0 + 16 * K + i * 4 * K, F32, D) for i in range(4)]
        xn = [sbv(X0 + 32 * K + i * 2 * K, BF16, D) for i in range(3)]
        junk = sbv(X0 + 24 * K, BF16, D)
        Rt = [sbv(R0 + i * 4 * K, F32, D) for i in range(NT)]
        h2T = sbv(Q0, BF16, NKC, S)
        Gb = [sbv(H0, BF16, NKC, 512), sbv(H0 + 24 * K, BF16, NKC, 512)]
        Ub = [sbv(H0 + 8 * K, BF16, NKC, 512), sbv(X0 + 16 * K, BF16, NKC, 512)]
        Db = [sbv(H0 + 16 * K, BF16, 4, D), sbv(X0 + 24 * K, BF16, 4, D)]
        actT = [sbv(M0 + i * 4 * K, BF16, 4, 512) for i in range(3)]
        sg = [sbv(M0 + 12 * K + i * 2 * K, F32, 512) for i in range(2)]
        xn2 = [sbv(X0 + 32 * K + i * 2 * K, BF16, D) for i in range(3)]
        junk2 = sbv(X0 + 38 * K + 3584, BF16, D)

        ch_c = [new_chan("c%d" % i) for i in range(6)]
        ch_x = [new_chan("x%d" % i) for i in range(4)]
        ch_w = [new_chan("w%d" % i) for i in range(4)]
        ch_wo = [new_chan("wo%d" % i) for i in range(2)]
        ch_g = [new_chan("g%d" % i) for i in range(2)]
        ch_u = [new_chan("u%d" % i) for i in range(2)]
        ch_d = [new_chan("d%d" % i) for i in range(2)]
        ch_o = [new_chan("o%d" % i) for i in range(4)]
        ch_dbg = new_chan("dbg")

        def dbg_out(name, ap, shape, dt):
            if not debug:
                return
            d = nc.dram_tensor("dbg_" + name, [128] + list(shape), dt, kind="ExternalOutput").ap()
            dbg_d[name] = d
            dbg_ops.append(dma("sp", d, ap, ch_dbg))

        dbg_ops = []

        dma("sp", constf, constf_d, ch_c[0])
        dma("pool", constb, constb_d, ch_c[1])
        dma("pool", pw, pool_w_d.rearrange("g c d -> c g d"), ch_c[2])
        w_in_v = w_in_d.rearrange("(kc p) n -> p kc n", p=128)

        chunks = [("kd", 0), ("kd", 1), ("v", 0), ("u", 0), ("q", 0), ("u", 1), ("q", 1),
                  ("u", 2), ("q", 2), ("u", 3), ("q", 3)]
        NPRE = 4

        def load_chunk(ci):
            kind, j = chunks[ci]
            slot = ci % 4
            if kind == "q":
                dma("pool", wring[slot], w_in_v[:, :, j * 128:(j + 1) * 128], ch_w[slot])
            elif kind == "kd":
                c0 = 512 + j * 64
                dma("pool", wring[slot][:, :, 0:64], w_in_v[:, :, c0:c0 + 64], ch_w[slot])
                dma("pool", wring[slot][:, :, 64:128], w_in_v[:, :, c0:c0 + 64], ch_w[slot])
            elif kind == "v":
                dma("pool", wring[slot], w_in_v[:, :, 640:768], ch_w[slot])
            else:
                c0 = 768 + j * 128
                dma("pool", wring[slot], w_in_v[:, :, c0:c0 + 128], ch_w[slot])

        for ci in range(4):
            load_chunk(ci)

        memset("dve", epsq, 64.0 * EPS)
        memset("dve", eps1, EPS)
        tt("dve", pool_bs, pool_b, pool_sc, ALU.mult)
        memset("pool", v_aug[:, :, :, 64:128], 1.0)
        tt("dve", rec[0][:, 0:64], gq_row, gq_row, ALU.mult)
        sch.op("dve", lambda e: e.reduce_max(out=mq, in_=rec[0][:, 0:64], axis=AX.X),
               reads=[rec[0][:, 0:64]], writes=[mq])
        tt("dve", rec[0][:, 64:128], gk_row, gk_row, ALU.mult)
        sch.op("dve", lambda e: e.reduce_max(out=mk, in_=rec[0][:, 64:128], axis=AX.X),
               reads=[rec[0][:, 64:128]], writes=[mk])
        tt("dve", negc, mq, mk, ALU.mult)
        if use_ln:
            act(negc, negc, AF.Ln)
            act(negc, negc, AF.Exp, scale=0.5)
        else:
            act(negc, negc, AF.Sqrt)
        sch.op("dve", lambda e: e.tensor_scalar(negc, negc, -8.0, None, ALU.mult), reads=[negc], writes=[negc])

        def norm_p1(i, xt, ss, jk):
            act(jk, xt, AF.Square, accum=ss[:, i:i + 1])

        def norm_p2(i, xt, ss, lnv, rstd, xnb, scale_eng="act"):
            act(lnv[:, i:i + 1], ss[:, i:i + 1], AF.Ln, scale=1.0 / D, bias=eps1)
            act(rstd[:, i:i + 1], lnv[:, i:i + 1], AF.Exp, scale=-0.5)
            if scale_eng == "act":
                act(xnb, xt, AF.Copy, scale=rstd[:, i:i + 1])
            else:
                ts(scale_eng, xnb, xt, rstd[:, i:i + 1], None, ALU.mult)

        def norm_transpose(i, xnb, gv, dstT):
            bank = nb_()
            pb = psb_bf(bank)
            for kc in range(NKC):
                tr(pb[:, kc * 128:(kc + 1) * 128], xnb[:, kc * 128:(kc + 1) * 128], ident)
            g3 = gv.rearrange("p (a b) -> p a b", b=1).to_broadcast([128, NKC, 128])
            tt("dve", dstT[:, :, i * 128:(i + 1) * 128], pb.rearrange("p (a b) -> p a b", b=128), g3, ALU.mult)

        unit = [0]
        pending = []
        pool_mm_pending = []
        WIN = [2, 4, 8, 16]

        def proc_chunk_tb(ci, tb):
            kind, j = chunks[ci]
            W = wring[ci % 4]
            cols = slice(tb * 512, (tb + 1) * 512)
            if kind == "v":
                for i in range(tb * 4, tb * 4 + 4):
                    bank = nb_()
                    o_ = psb(bank)[:, 0:128]
                    for kc in range(NKC):
                        mm(o_, hT[:, kc, i * 128:(i + 1) * 128], W[:, kc, :], start=(kc == 0), stop=(kc == NKC - 1))
                    cp("dve", v_aug[:, i, :, 0:64], o_.rearrange("p (a b) -> p a b", a=2, b=64))
                return
            bA = nb_()
            for kc in range(NKC):
                mm(psb(bA), W[:, kc, :], hT[:, kc, cols], start=(kc == 0), stop=(kc == NKC - 1))
            if kind == "u":
                act(Ubuf[j % 2][:, UPAD + tb * 512:UPAD + (tb + 1) * 512], psb(bA), AF.Copy)
                return
            sc_ = scr[unit[0] % NSCR]
            unit[0] += 1
            if kind == "q":
                cg, sgn, dest = cosg_q, sing_q, qT[:, j, cols]
            else:
                cg, sgn, dest = cosg_k, sing_k, kdup[:, j, cols]
            act(sc_.sq, psb(bA), AF.Square)
            act(sc_.abf, psb(bA), AF.Copy)
            tt("dve", sc_.t1, psb(bA), cg[:, cols], ALU.mult)

            def part2(sc_=sc_, sgn=sgn, dest=dest, cols=cols):
                bB = nb_()
                bC = nb_()
                mm(psb(bB), ones_bd, sc_.sq)
                mm(psb(bC), rrot, sc_.abf)
                act(sc_.rs, psb(bB), AF.Ln, bias=epsq)
                act(sc_.rs, sc_.rs, AF.Exp, scale=-0.5)
                tt("dve", sc_.t2, psb(bC), sgn[:, cols], ALU.mult)
                tt("dve", sc_.t1, sc_.t1, sc_.t2, ALU.add)
                tt("dve", dest, sc_.t1, sc_.rs, ALU.mult)

            pending.append(part2)
            while len(pending) > 1:
                pending.pop(0)()

        def flush_pending():
            while pending:
                pending.pop(0)()

        def finish_chunk(ci):
            kind, j = chunks[ci]
            if kind != "u":
                return
            g = j
            U = Ubuf[g % 2]
            prev = U
            exts = [8, 6, 4, 0]
            shifts = [(-1, 0), (-1, 1), (-2, 2), (-4, 4)]
            for l in range(g + 1):
                e_ = exts[l]
                dst = tmpAB[l % 2]
                lo = UPAD - e_
                n_ = S + 2 * e_
                s0, s1 = shifts[l]
                tt("dve", dst[:, lo:lo + n_], prev[:, lo + s0:lo + s0 + n_], prev[:, lo + s1:lo + s1 + n_], ALU.add)
                prev = dst
            Fv = prev
            tt("dve", Fv[:, UPAD:UPAD + 8], Fv[:, UPAD:UPAD + 8], fixtab[:, g * 16:g * 16 + 8], ALU.mult)
            tt("dve", Fv[:, UPAD + S - 8:UPAD + S], Fv[:, UPAD + S - 8:UPAD + S],
               fixtab[:, g * 16 + 8:g * 16 + 16], ALU.mult)
            fin = Fv[:, UPAD:UPAD + S]
            uin = U[:, UPAD:UPAD + S]
            winv = 1.0 / WIN[g]
            sch.op("dve", lambda e, fin=fin, uin=uin, winv=winv: e.scalar_tensor_tensor(
                pooled, fin, winv, uin, ALU.mult, ALU.subtract), reads=[fin, uin], writes=[pooled])

            def pool_mm(g=g, banks=None):
                for tb in range(4):
                    cols = slice(tb * 512, (tb + 1) * 512)
                    bk = banks[tb % len(banks)] if banks else nb_()
                    mm(psb(bk), pw[:, g, :], pooled[:, cols])
                    act(mixT[:, 4 + g, cols], psb(bk), AF.Identity, scale=pool_sc[:, g:g + 1], bias=pool_bs[:, g:g + 1])
            pool_mm_pending.append(pool_mm)

        def flush_pool_mm(banks=None):
            while pool_mm_pending:
                pool_mm_pending.pop(0)(banks=banks)

        NXR = len(xring)
        for i in range(NT + 2):
            if i < NT:
                dma("sp", xring[i % NXR], x_d[i * 128:(i + 1) * 128, :], ch_x[i % NXR])
            if i == 1:
                dma("sp", cosg_k, rope_d[:, 0:S], ch_c[3])
                dma("sp", sing_k, rope_d[:, S:2 * S], ch_c[4])
            if i == 4:
                act(cosg_k, cosg_k, AF.Copy, scale=gk)
                act(sing_k, sing_k, AF.Copy, scale=gk_perm)
            if i == 9:
                dma("sp", cosg_q, rope_d[:, 0:S], ch_c[5])
                dma("sp", sing_q, rope_d[:, S:2 * S], ch_c[3])
                act(cosg_q, cosg_q, AF.Copy, scale=gq)
                act(sing_q, sing_q, AF.Copy, scale=gq_perm)
            if i < NT:
                norm_p1(i, xring[i % NXR], ss1, junk)
            if 1 <= i <= NT:
                norm_p2(i - 1, xring[(i - 1) % NXR], ss1, ln1, rstd1, xn[(i - 1) % 3], scale_eng=("act" if (i % 2) else "dve"))
            if i >= 2:
                t_ = i - 2
                norm_transpose(t_, xn[t_ % 3], g1v, hT)
                if interleave and t_ % 4 == 3:
                    for ci in range(NPRE):
                        proc_chunk_tb(ci, t_ // 4)
        if stop <= 0:
            sch.frozen = True
        if not interleave:
            for tb in range(4):
                for ci in range(NPRE):
                    proc_chunk_tb(ci, tb)
        for i in range(2):
            memset("pool", Ubuf[i][:, 0:UPAD], 0.0)
            memset("pool", Ubuf[i][:, UPAD + S:UW], 0.0)
        dbg_out("hT", hT, [NKC, S], BF16)
        if stop <= 1:
            sch.frozen = True
        for ci in range(NPRE):
            finish_chunk(ci)
            if ci + 4 < len(chunks):
                load_chunk(ci + 4)
        groups = [(4, 5), (6, 7), (8, 9), (10,)]
        for grp in groups:
            for tb in range(4):
                for ci in grp:
                    proc_chunk_tb(ci, tb)
                if tb == 3:
                    flush_pool_mm()
            for ci in grp:
                finish_chunk(ci)
                if ci + 4 < len(chunks):
                    load_chunk(ci + 4)
        flush_pending()
        dbg_out("qT", qT, [4, S], BF16)
        dbg_out("kdup", kdup, [2, S], BF16)
        dbg_out("v_aug", v_aug, [NT, 2, 128], BF16)

        if stop <= 2:
            sch.frozen = True
        w_out_v = w_out_d.rearrange("(kc p) n -> p kc n", p=128)
        for hlf in range(2):
            dma("pool", wout_sb[:, hlf * 4:(hlf + 1) * 4, :], w_out_v[:, hlf * 4:(hlf + 1) * 4, :], ch_wo[hlf])
        w_gate_v = w_gate_d.rearrange("(kc p) n -> p kc n", p=128)
        w_up_v = w_up_d.rearrange("(kc p) n -> p kc n", p=128)
        pieces = [(0, 4), (4, 4), (8, 4), (12, 4), (16, 4), (20, 2)]

        def load_piece(p):
            fc0, nfc = pieces[p]
            s_ = p % 2
            c0 = fc0 * 128
            ncol = nfc * 128
            dma("pool", Gb[s_][:, :, 0:ncol], w_gate_v[:, :, c0:c0 + ncol], ch_g[s_])
            dma("pool", Ub[s_][:, :, 0:ncol], w_up_v[:, :, c0:c0 + ncol], ch_u[s_])
            dma("pool", Db[s_][:, 0:nfc, :], w_down_d[c0:c0 + ncol, :].rearrange("(fc p) n -> p fc n", p=128),
                ch_d[s_])

        load_piece(0)
        load_piece(1)
        for i in range(NT):
            dma("sp", Rt[i], x_d[i * 128:(i + 1) * 128, :], ch_x[i % 4])

        steps = [(j, qc, sc) for j in range(4) for qc in range(4) for sc in range(16)]

        def mm1(idx):
            j, qc, sc = steps[idx]
            kh = j // 2
            sb_ = idx % 2
            q0 = qc * 512
            for hb in range(2):
                pr = hb * 64
                mm(psb(2 * sb_ + hb), kdup[pr:pr + 64, kh, sc * 128:(sc + 1) * 128], qT[pr:pr + 64, j, q0:q0 + 512])

        def expo(idx):
            sb_ = idx % 2
            src = ps[:, 2 * sb_:2 * sb_ + 2, :].rearrange("p a b -> p (a b)")
            act(PT[idx % 3], src, AF.Exp, scale=8.0, bias=negc)

        def mm2(idx):
            j, qc, sc = steps[idx]
            kh = j // 2
            ob = (j * 4 + qc) % 2
            for hb in range(2):
                mm(psb(4 + 2 * ob + hb), v_aug[:, sc, kh, :], PT[idx % 3][:, hb * 512:(hb + 1) * 512],
                   start=(sc == 0), stop=(sc == 15))

        def onorm(j, qc):
            ob = (j * 4 + qc) % 2
            q0 = qc * 512
            for hb in range(2):
                bank = 4 + 2 * ob + hb
                pr = hb * 64
                recip(rec[hb][64:128, :], psb(bank)[64:128, :])
                tt("dve", mixT[pr:pr + 64, j, q0:q0 + 512], psb(bank)[0:64, :], rec[hb][64:128, :], ALU.mult)

        mm1(0)
        for idx in range(len(steps)):
            if idx + 1 < len(steps):
                mm1(idx + 1)
            expo(idx)
            mm2(idx)
            if idx == 12:
                flush_pool_mm(banks=(6, 7))
            j, qc, sc = steps[idx]
            if sc == 15:
                onorm(j, qc)
        dbg_out("mixT", mixT, [8, S], BF16)
        if stop <= 3:
            sch.frozen = True

        for i in range(NT + 3):
            if i < NT:
                for n in range(2):
                    bank = nb_()
                    for kc in range(NKC):
                        mm(psb(bank), mixT[:, kc, i * 128:(i + 1) * 128], wout_sb[:, kc, n * 512:(n + 1) * 512],
                           start=(kc == 0), stop=(kc == NKC - 1))
                    tt("dve", Rt[i][:, n * 512:(n + 1) * 512], psb(bank), Rt[i][:, n * 512:(n + 1) * 512], ALU.add)
            if 1 <= i <= NT:
                norm_p1(i - 1, Rt[i - 1], ss2, junk2)
            if 2 <= i <= NT + 1:
                norm_p2(i - 2, Rt[i - 2], ss2, ln2, rstd2, xn2[(i - 2) % 3])
            if i >= 3:
                norm_transpose(i - 3, xn2[(i - 3) % 3], g2v, h2T)
        if debug:
            d = nc.dram_tensor("dbg_x1", [S, D], F32, kind="ExternalOutput").ap()
            dbg_d["x1"] = d
            for i in range(NT):
                dbg_ops.append(dma("sp", d[i * 128:(i + 1) * 128, :], Rt[i], ch_dbg))
        dbg_out("h2T", h2T, [NKC, S], BF16)

        units = [(p, tb) for p in range(len(pieces)) for tb in range(4)]
        out_ops = []
        gu_cnt = [0]

        def gateup(k):
            p, tb = units[k]
            fc0, nfc = pieces[p]
            s_ = p % 2
            slot = k % 3
            cols = slice(tb * 512, (tb + 1) * 512)
            for f in range(nfc):
                pair = gu_cnt[0] % 3
                gu_cnt[0] += 1
                bg, bu = 2 * pair, 2 * pair + 1
                for kc in range(NKC):
                    mm(psb(bg), Gb[s_][:, kc, f * 128:(f + 1) * 128], h2T[:, kc, cols], start=(kc == 0),
                       stop=(kc == NKC - 1))
                for kc in range(NKC):
                    mm(psb(bu), Ub[s_][:, kc, f * 128:(f + 1) * 128], h2T[:, kc, cols], start=(kc == 0),
                       stop=(kc == NKC - 1))
                sgb = sg[gu_cnt[0] % 2]
                act(sgb, psb(bg), AF.Silu)
                tt("dve", actT[slot][:, f, :], psb(bu), sgb, ALU.mult)

        dn_cnt = [0]

        def down(k):
            p, tb = units[k]
            fc0, nfc = pieces[p]
            s_ = p % 2
            slot = k % 3
            for ti in range(4):
                i = tb * 4 + ti
                for n in range(2):
                    bank = 6 + dn_cnt[0] % 2
                    dn_cnt[0] += 1
                    for f in range(nfc):
                        mm(psb(bank), actT[slot][:, f, ti * 128:(ti + 1) * 128], Db[s_][:, f, n * 512:(n + 1) * 512],
                           start=(f == 0), stop=(f == nfc - 1))
                    tt("dve", Rt[i][:, n * 512:(n + 1) * 512], psb(bank), Rt[i][:, n * 512:(n + 1) * 512], ALU.add)
                if p == len(pieces) - 1:
                    out_ops.append(dma("sp", out_d[i * 128:(i + 1) * 128, :], Rt[i], ch_o[i % 4]))
            if tb == 3 and p + 2 < len(pieces):
                load_piece(p + 2)

        gateup(0)
        for k in range(1, len(units)):
            gateup(k)
            down(k - 1)
        down(len(units) - 1)

        sch.op("sp", None, extra_deps=out_ops + dbg_ops, force=True)

        sch.resolve(eng_sems)
        block = es.enter_context(nc.Block())
        block.tensor(sch.runner("pe"))
        block.scalar(sch.runner("act"))
        block.vector(sch.runner("dve"))
        block.gpsimd(sch.runner("pool"))
        block.sync(sch.runner("sp"))
    return nc, dbg_d


def _host_consts():
    f32 = np.float32
    f64 = np.float64
    inv_freq = f64(10000.0) ** (-(np.arange(16, dtype=f64)) / f64(16))
    t = np.arange(S)
    row = (t // 64).astype(f64)
    col = (t % 64).astype(f64)
    ang_row = row[:, None] * inv_freq[None, :]
    ang_col = col[:, None] * inv_freq[None, :]
    ang = np.zeros((64, S), f64)
    for d in range(64):
        a = ang_row if d < 32 else ang_col
        ang[d] = a[:, d % 16]
    cos = np.cos(ang)
    sin = np.sin(ang)
    rope = np.concatenate([np.tile(cos, (2, 1)), np.tile(sin, (2, 1))], axis=1).astype(f32)
    ident = np.eye(128, dtype=f32)
    ones_bd = np.zeros((128, 128), f32)
    ones_bd[:64, :64] = 1
    ones_bd[64:, 64:] = 1
    rr = np.zeros((128, 128), f32)
    perm = np.zeros(128, np.int64)
    for m in range(128):
        if m % 32 < 16:
            rr[m + 16, m] = -1.0
            perm[m] = m + 16
        else:
            rr[m - 16, m] = 1.0
            perm[m] = m - 16
    constb = np.concatenate([ident, ones_bd, rr], axis=1).astype(f32)
    fix = np.ones((4, 16), f64)
    for g, w in enumerate([2, 4, 8, 16]):
        for jj in range(8):
            for side, tt_ in ((0, jj), (1, S - 8 + jj)):
                lo = min(max(tt_ - w // 2, 0), S)
                hi = min(max(tt_ - w // 2 + w, 0), S)
                fix[g, side * 8 + jj] = f64(w) / f64(hi - lo)
    return rope, constb, fix.astype(f32), perm


_NC_CACHE = {}


def _prep_inputs(x, norm1_g, w_in, q_norm_g, k_norm_g, pool_w, pool_b, pool_scale, w_out, norm2_g,
                 w_gate, w_up, w_down):
    f32 = np.float32
    rope, constb, fix, perm = _host_consts()
    constf = np.zeros((128, NF), f32)
    constf[:, 0:8] = np.asarray(norm1_g, f32).reshape(8, 128).T
    constf[:, 8:16] = np.asarray(norm2_g, f32).reshape(8, 128).T
    constf[:, 16] = np.tile(np.asarray(q_norm_g, f32).reshape(64), 2)
    constf[:, 17] = np.tile(np.asarray(k_norm_g, f32).reshape(64), 2)
    constf[:, 18:22] = np.asarray(pool_b, f32).reshape(4, 128).T
    constf[:, 22:26] = np.asarray(pool_scale, f32).reshape(4, 128).T
    constf[:, 26:90] = np.asarray(q_norm_g, f32).reshape(1, 64)
    constf[:, 90:154] = np.asarray(k_norm_g, f32).reshape(1, 64)
    constf[:, 154:218] = fix.reshape(1, 64)
    constf[:, 218] = np.tile(np.asarray(q_norm_g, f32).reshape(64), 2)[perm]
    constf[:, 219] = np.tile(np.asarray(k_norm_g, f32).reshape(64), 2)[perm]
    shared = {
        "w_in": np.ascontiguousarray(np.asarray(w_in, f32)[0]),
        "w_out": np.ascontiguousarray(np.asarray(w_out, f32)[0]),
        "w_gate": np.ascontiguousarray(np.asarray(w_gate, f32)[0]),
        "w_up": np.ascontiguousarray(np.asarray(w_up, f32)[0]),
        "w_down": np.ascontiguousarray(np.asarray(w_down, f32)[0]),
        "pool_w": np.ascontiguousarray(np.asarray(pool_w, f32)[0]),
        "constf": constf,
        "constb": constb,
        "rope": rope,
    }
    xs = np.asarray(x, f32)
    in_maps = []
    for c in range(N_CORES):
        m = dict(shared)
        m["x"] = np.ascontiguousarray(xs[c])
        in_maps.append(m)
    return in_maps


def kernel(x, norm1_g, w_in, q_norm_g, k_norm_g, pool_w, pool_b, pool_scale, w_out, norm2_g,
           w_gate, w_up, w_down):
    in_maps = _prep_inputs(x, norm1_g, w_in, q_norm_g, k_norm_g, pool_w, pool_b, pool_scale, w_out,
                           norm2_g, w_gate, w_up, w_down)
    if "nc" not in _NC_CACHE:
        _NC_CACHE["nc"] = build_nc(debug=False)[0]
    nc = _NC_CACHE["nc"]
    res = run_bass_kernel_spmd(nc, in_maps, core_ids=list(range(N_CORES)))
    out = np.stack([np.asarray(r["out"], np.float32) for r in res.results], axis=0)
    return out
```

```python
import math
from contextlib import ExitStack

import numpy as np
import concourse.bass as bass
import concourse.mybir as mybir
from concourse.bass_utils import run_bass_kernel_spmd

F32 = mybir.dt.float32
BF16 = mybir.dt.bfloat16
ALU = mybir.AluOpType
AF = mybir.ActivationFunctionType
AX = mybir.AxisListType

S = 2048
D = 1024
NT = 16
NKC = 8
DFF = 2816
EPS = 1e-6
N_CORES = 8

K = 1024
R0 = 0
H0 = 64 * K
M0 = 96 * K
Q0 = 128 * K
X0 = 160 * K
ARENA_BYTES = X0 + 47 * K
NF = 224


def _dsize(dt):
    return mybir.dt.size(dt)


class _Op:
    __slots__ = ("eng", "emit", "deps", "signal", "sem", "val", "chan", "waits", "idx")


class _Chan:
    def __init__(self, sem):
        self.sem = sem
        self.count = 0
        self.last = None


class Sched:
    def __init__(self):
        self.ops = []
        self.frozen = False
        self.recs = {"SB": [], "PSUM": []}

    @staticmethod
    def ranges(ap):
        sp = str(ap.space)
        if sp not in ("SB", "PSUM"):
            return None, []
        dims = ap.ap
        pstride = dims[0][0]
        es = _dsize(ap.dtype)
        off = ap.offset % pstride if pstride > 0 else ap.offset
        free = sorted([(s, c) for s, c in dims[1:] if c > 1 and s > 0])
        run = 1
        rest = []
        for s, c in free:
            if s == run and not rest:
                run *= c
            else:
                rest.append((s, c))
        nouter = 1
        for s, c in rest:
            nouter *= c
        out = []
        if nouter <= 64:
            starts = [off]
            for s, c in rest:
                starts = [b + s * k for b in starts for k in range(c)]
            for b in starts:
                out.append((b * es, (b + run) * es))
        else:
            hi = off + run
            for s, c in rest:
                hi += s * (c - 1)
            out.append((off * es, hi * es))
        if sp == "PSUM":
            out = [((lo // 2048) * 2048, ((hi + 2047) // 2048) * 2048) for lo, hi in out]
        out.sort()
        merged = []
        for lo, hi in out:
            if merged and lo <= merged[-1][1]:
                merged[-1] = (merged[-1][0], max(hi, merged[-1][1]))
            else:
                merged.append((lo, hi))
        return sp, merged

    def op(self, eng, emit, reads=(), writes=(), chan=None, extra_deps=(), force=False):
        if self.frozen and not force:
            return None
        extra_deps = [d for d in extra_deps if d is not None]
        o = _Op()
        o.eng = eng
        o.emit = emit
        o.deps = set(extra_deps)
        o.signal = False
        o.sem = None
        o.val = None
        o.chan = chan
        o.waits = []
        o.idx = len(self.ops)
        is_dma = chan is not None
        for ap in reads:
            sp, rs = self.ranges(ap)
            if sp is None:
                continue
            recs = self.recs[sp]
            for lo, hi in rs:
                keep = []
                for r in recs:
                    if r[2] != o.idx and r[0] < hi and lo < r[1]:
                        if r[3]:
                            o.deps.add(r[2])
                        elif sp == "PSUM" and self.ops[r[2]].eng != eng:
                            o.deps.add(r[2])
                        elif (not is_dma) and self.ops[r[2]].chan is None and self.ops[r[2]].eng == eng \
                                and lo <= r[0] and r[1] <= hi:
                            continue
                    keep.append(r)
                keep.append([lo, hi, o.idx, False])
                self.recs[sp] = recs = keep
        for ap in writes:
            sp, rs = self.ranges(ap)
            if sp is None:
                continue
            recs = self.recs[sp]
            for lo, hi in rs:
                keep = []
                for r in recs:
                    if r[0] < hi and lo < r[1]:
                        if r[2] != o.idx:
                            o.deps.add(r[2])
                        if lo <= r[0] and r[1] <= hi:
                            continue
                    keep.append(r)
                keep.append([lo, hi, o.idx, True])
                self.recs[sp] = recs = keep
        if is_dma:
            if chan.last is not None:
                o.deps.add(chan.last)
            chan.last = o.idx
        o.deps.discard(o.idx)
        self.ops.append(o)
        return o.idx

    def resolve(self, eng_sems):
        ops = self.ops

        def skip(o, d):
            return o.eng == "pe" and d.eng == "pe" and o.chan is None and d.chan is None

        for o in ops:
            for di in o.deps:
                d = ops[di]
                if not skip(o, d):
                    d.signal = True
        counters = {e: 0 for e in eng_sems}
        for o in ops:
            if o.chan is not None:
                o.chan.count += 16
                o.sem = o.chan.sem
                o.val = o.chan.count
            elif o.signal:
                counters[o.eng] += 1
                o.sem = eng_sems[o.eng]
                o.val = counters[o.eng]
        waited = {}
        for o in ops:
            w = {}
            wd = waited.setdefault(o.eng, {})
            for di in o.deps:
                d = ops[di]
                if skip(o, d):
                    continue
                key = id(d.sem)
                if wd.get(key, (None, 0))[1] >= d.val:
                    continue
                if key not in w or w[key][1] < d.val:
                    w[key] = (d.sem, d.val)
            for key, sv in w.items():
                wd[key] = sv
            o.waits = list(w.values())

    def runner(self, eng):
        mine = [o for o in self.ops if o.eng == eng]

        def run(e):
            for o in mine:
                for sem, val in o.waits:
                    e.wait_ge(sem, val)
                if o.emit is None:
                    continue
                ins = o.emit(e)
                if o.chan is not None:
                    ins.then_inc(o.sem, 16)
                elif o.signal:
                    ins.then_inc(o.sem, 1)
        return run


def build_nc(debug=False, stop=99, use_ln=True, interleave=True):
    nc = bass.Bass("TRN2", target_bir_lowering=False)
    dr = {}

    def din(name, shape):
        dr[name] = nc.dram_tensor(name, list(shape), F32, kind="ExternalInput").ap()
        return dr[name]

    x_d = din("x", [S, D])
    w_in_d = din("w_in", [D, 1280])
    w_out_d = din("w_out", [D, D])
    w_gate_d = din("w_gate", [D, DFF])
    w_up_d = din("w_up", [D, DFF])
    w_down_d = din("w_down", [DFF, D])
    pool_w_d = din("pool_w", [4, 128, 128])
    constf_d = din("constf", [128, NF])
    constb_d = din("constb", [128, 384])
    rope_d = din("rope", [128, 2 * S])
    out_d = nc.dram_tensor("out", [S, D], F32, kind="ExternalOutput").ap()
    dbg_d = {}

    sch = Sched()

    with ExitStack() as es:
        es.enter_context(nc.allow_low_precision("bf16 matmul operands, fp32 accumulation"))
        arena = es.enter_context(nc.sbuf_tensor("arena", [128, ARENA_BYTES // 4], F32))
        ps = es.enter_context(nc.psum_tensor("ps", [128, 8, 512], F32))
        eng_sems = {e: es.enter_context(nc.semaphore("sem_" + e)) for e in ("pe", "act", "dve", "pool", "sp")}

        def new_chan(name):
            return _Chan(es.enter_context(nc.semaphore("ch_" + name)))

        def sbv(off, dt, *shape):
            n = 1
            for s_ in shape:
                n *= s_
            nb = n * _dsize(dt)
            assert off % 4 == 0 and nb % 4 == 0 and off + nb <= ARENA_BYTES, (off, nb)
            v = arena[:, off // 4:(off + nb) // 4]
            if dt != F32:
                v = v.bitcast(dt)
            if len(shape) == 2:
                v = v.rearrange("p (a b) -> p a b", a=shape[0], b=shape[1])
            elif len(shape) == 3:
                v = v.rearrange("p (a b c) -> p a b c", a=shape[0], b=shape[1], c=shape[2])
            return v

        def psb(bank):
            return ps[:, bank, :]

        def psb_bf(bank):
            return ps[:, bank, :].bitcast(BF16)

        _bank = [0]

        def nb_():
            b = _bank[0]
            _bank[0] = (b + 1) % 8
            return b

        def dma(q, out, in_, chan):
            return sch.op(q, lambda e: e.dma_start(out=out, in_=in_), reads=[in_], writes=[out], chan=chan)

        def mm(out, lhsT, rhs, start=True, stop=True):
            return sch.op("pe", lambda e: e.matmul(out, lhsT, rhs, start=start, stop=stop),
                          reads=[lhsT, rhs], writes=[out])

        def tr(out, in_, ident):
            return sch.op("pe", lambda e: e.transpose(out, in_, ident), reads=[in_, ident], writes=[out])

        def act(out, in_, func, scale=1.0, bias=None, accum=None):
            reads = [in_]
            writes = [out]
            kw = {}
            if not isinstance(scale, (int, float)):
                reads.append(scale)
            if bias is not None:
                kw["bias"] = bias
                if not isinstance(bias, (int, float)):
                    reads.append(bias)
            if accum is not None:
                kw["accum_out"] = accum
                writes.append(accum)
            return sch.op("act", lambda e: e.activation(out, in_, func, scale=scale, **kw),
                          reads=reads, writes=writes)

        def tt(eng, out, in0, in1, op):
            return sch.op(eng, lambda e: e.tensor_tensor(out, in0, in1, op), reads=[in0, in1], writes=[out])

        def ts(eng, out, in0, s1, s2, op0, op1=None):
            reads = [in0] + [s_ for s_ in (s1, s2) if s_ is not None and not isinstance(s_, (int, float))]
            if op1 is None:
                return sch.op(eng, lambda e: e.tensor_scalar(out, in0, s1, None, op0), reads=reads, writes=[out])
            return sch.op(eng, lambda e: e.tensor_scalar(out, in0, s1, s2, op0, op1), reads=reads, writes=[out])

        def cp(eng, out, in_):
            return sch.op(eng, lambda e: e.tensor_copy(out, in_), reads=[in_], writes=[out])

        def recip(out, in_):
            return sch.op("dve", lambda e: e.reciprocal(out, in_), reads=[in_], writes=[out])

        def memset(eng, ap, val):
            return sch.op(eng, lambda e: e.memset(ap, val), writes=[ap])

        C0 = X0 + 38 * K
        constf = sbv(C0, F32, NF)
        constb = sbv(C0 + 1024, BF16, 384)
        pw = sbv(C0 + 2048, BF16, 4, 128)
        stats = sbv(C0 + 3072, F32, 128)
        rec = [sbv(C0 + 3584 + i * 2048, F32, 512) for i in range(2)]
        g1v = constf[:, 0:8]
        g2v = constf[:, 8:16]
        gq = constf[:, 16:17]
        gk = constf[:, 17:18]
        pool_b = constf[:, 18:22]
        pool_sc = constf[:, 22:26]
        gq_row = constf[:, 26:90]
        gk_row = constf[:, 90:154]
        fixtab = constf[:, 154:218]
        gq_perm = constf[:, 218:219]
        gk_perm = constf[:, 219:220]
        ident = constb[:, 0:128]
        ones_bd = constb[:, 128:256]
        rrot = constb[:, 256:384]
        ss1 = stats[:, 0:16]
        ln1 = stats[:, 16:32]
        rstd1 = stats[:, 32:48]
        ss2 = stats[:, 48:64]
        ln2 = stats[:, 64:80]
        rstd2 = stats[:, 80:96]
        mq = stats[:, 96:97]
        mk = stats[:, 97:98]
        negc = stats[:, 98:99]
        epsq = stats[:, 99:100]
        eps1 = stats[:, 100:101]
        pool_bs = stats[:, 104:108]

        hT = sbv(H0, BF16, NKC, S)
        mixT = sbv(M0, BF16, 8, S)
        qT = sbv(Q0, BF16, 4, S)
        kdup = sbv(Q0 + 16 * K, BF16, 2, S)
        v_aug = sbv(Q0 + 24 * K, BF16, NT, 2, 128)
        cosg_q = sbv(X0, F32, S)
        sing_q = sbv(X0 + 8 * K, F32, S)
        cosg_k = sbv(M0, F32, S)
        sing_k = sbv(M0 + 8 * K, F32, S)
        wring = [sbv(X0 + 16 * K + i * 2048, BF16, NKC, 128) for i in range(4)]
        PT = [sbv(X0 + 32 * K + i * 2048, BF16, 1024) for i in range(3)]
        wout_sb = sbv(X0, BF16, NKC, D)
        UPAD = 16
        UW = S + 2 * UPAD
        Ubuf = [sbv(R0 + i * UW * 4, F32, UW) for i in range(2)]
        tmpAB = [sbv(R0 + (2 + i) * UW * 4, F32, UW) for i in range(2)]
        pooled = sbv(R0 + 4 * UW * 4, BF16, S)
        QK0 = R0 + 4 * UW * 4 + 4096

        class _Scr:
            pass

        scr = []
        NSCR = 3
        for i in range(NSCR):
            b = QK0 + i * 8 * K
            s_ = _Scr()
            s_.sq = sbv(b, BF16, 512)
            s_.abf = sbv(b + 1 * K, BF16, 512)
            s_.rs = sbv(b + 2 * K, F32, 512)
            s_.t2 = sbv(b + 4 * K, F32, 512)
            s_.t1 = sbv(b + 6 * K, F32, 512)
            scr.append(s_)
        assert QK0 + NSCR * 8 * K <= R0 + 64 * K
        xring = [sbv(M0 + 16 * K + i * 4 * K, F32, D) for i in range(4)]
        xn = [sbv(X0 + 32 * K + i * 2 * K, BF16, D) for i in range(3)]
        junk = sbv(X0 + 24 * K, BF16, D)
        Rt = [sbv(R0 + i * 4 * K, F32, D) for i in range(NT)]
        h2T = sbv(Q0, BF16, NKC, S)
        Gb = [sbv(H0, BF16, NKC, 512), sbv(H0 + 24 * K, BF16, NKC, 512)]
        Ub = [sbv(H0 + 8 * K, BF16, NKC, 512), sbv(X0 + 16 * K, BF16, NKC, 512)]
        Db = [sbv(H0 + 16 * K, BF16, 4, D), sbv(X0 + 24 * K, BF16, 4, D)]
        actT = [sbv(M0 + i * 4 * K, BF16, 4, 512) for i in range(3)]
        sg = [sbv(M0 + 12 * K + i * 2 * K, F32, 512) for i in range(2)]
        xn2 = [sbv(X0 + 32 * K + i * 2 * K, BF16, D) for i in range(3)]
        junk2 = sbv(X0 + 38 * K + 3584, BF16, D)

        ch_c = [new_chan("c%d" % i) for i in range(6)]
        ch_x = [new_chan("x%d" % i) for i in range(4)]
        ch_w = [new_chan("w%d" % i) for i in range(4)]
        ch_wo = [new_chan("wo%d" % i) for i in range(2)]
        ch_g = [new_chan("g%d" % i) for i in range(2)]
        ch_u = [new_chan("u%d" % i) for i in range(2)]
        ch_d = [new_chan("d%d" % i) for i in range(2)]
        ch_o = [new_chan("o%d" % i) for i in range(4)]
        ch_dbg = new_chan("dbg")

        def dbg_out(name, ap, shape, dt):
            if not debug:
                return
            d = nc.dram_tensor("dbg_" + name, [128] + list(shape), dt, kind="ExternalOutput").ap()
            dbg_d[name] = d
            dbg_ops.append(dma("sp", d, ap, ch_dbg))

        dbg_ops = []

        dma("sp", constf, constf_d, ch_c[0])
        dma("pool", constb, constb_d, ch_c[1])
        dma("pool", pw, pool_w_d.rearrange("g c d -> c g d"), ch_c[2])
        w_in_v = w_in_d.rearrange("(kc p) n -> p kc n", p=128)

        chunks = [("kd", 0), ("kd", 1), ("v", 0), ("u", 0), ("q", 0), ("u", 1), ("q", 1),
                  ("u", 2), ("q", 2), ("u", 3), ("q", 3)]
        NPRE = 4

        def load_chunk(ci):
            kind, j = chunks[ci]
            slot = ci % 4
            if kind == "q":
                dma("pool", wring[slot], w_in_v[:, :, j * 128:(j + 1) * 128], ch_w[slot])
            elif kind == "kd":
                c0 = 512 + j * 64
                dma("pool", wring[slot][:, :, 0:64], w_in_v[:, :, c0:c0 + 64], ch_w[slot])
                dma("pool", wring[slot][:, :, 64:128], w_in_v[:, :, c0:c0 + 64], ch_w[slot])
            elif kind == "v":
                dma("pool", wring[slot], w_in_v[:, :, 640:768], ch_w[slot])
            else:
                c0 = 768 + j * 128
                dma("pool", wring[slot], w_in_v[:, :, c0:c0 + 128], ch_w[slot])

        for ci in range(4):
            load_chunk(ci)

        memset("dve", epsq, 64.0 * EPS)
        memset("dve", eps1, EPS)
        tt("dve", pool_bs, pool_b, pool_sc, ALU.mult)
        memset("pool", v_aug[:, :, :, 64:128], 1.0)
        tt("dve", rec[0][:, 0:64], gq_row, gq_row, ALU.mult)
        sch.op("dve", lambda e: e.reduce_max(out=mq, in_=rec[0][:, 0:64], axis=AX.X),
               reads=[rec[0][:, 0:64]], writes=[mq])
        tt("dve", rec[0][:, 64:128], gk_row, gk_row, ALU.mult)
        sch.op("dve", lambda e: e.reduce_max(out=mk, in_=rec[0][:, 64:128], axis=AX.X),
               reads=[rec[0][:, 64:128]], writes=[mk])
        tt("dve", negc, mq, mk, ALU.mult)
        if use_ln:
            act(negc, negc, AF.Ln)
            act(negc, negc, AF.Exp, scale=0.5)
        else:
            act(negc, negc, AF.Sqrt)
        sch.op("dve", lambda e: e.tensor_scalar(negc, negc, -8.0, None, ALU.mult), reads=[negc], writes=[negc])

        def norm_p1(i, xt, ss, jk):
            act(jk, xt, AF.Square, accum=ss[:, i:i + 1])

        def norm_p2(i, xt, ss, lnv, rstd, xnb, scale_eng="act"):
            act(lnv[:, i:i + 1], ss[:, i:i + 1], AF.Ln, scale=1.0 / D, bias=eps1)
            act(rstd[:, i:i + 1], lnv[:, i:i + 1], AF.Exp, scale=-0.5)
            if scale_eng == "act":
                act(xnb, xt, AF.Copy, scale=rstd[:, i:i + 1])
            else:
                ts(scale_eng, xnb, xt, rstd[:, i:i + 1], None, ALU.mult)

        def norm_transpose(i, xnb, gv, dstT):
            bank = nb_()
            pb = psb_bf(bank)
            for kc in range(NKC):
                tr(pb[:, kc * 128:(kc + 1) * 128], xnb[:, kc * 128:(kc + 1) * 128], ident)
            g3 = gv.rearrange("p (a b) -> p a b", b=1).to_broadcast([128, NKC, 128])
            tt("dve", dstT[:, :, i * 128:(i + 1) * 128], pb.rearrange("p (a b) -> p a b", b=128), g3, ALU.mult)

        unit = [0]
        pending = []
        pool_mm_pending = []
        WIN = [2, 4, 8, 16]

        def proc_chunk_tb(ci, tb):
            kind, j = chunks[ci]
            W = wring[ci % 4]
            cols = slice(tb * 512, (tb + 1) * 512)
            if kind == "v":
                for i in range(tb * 4, tb * 4 + 4):
                    bank = nb_()
                    o_ = psb(bank)[:, 0:128]
                    for kc in range(NKC):
                        mm(o_, hT[:, kc, i * 128:(i + 1) * 128], W[:, kc, :], start=(kc == 0), stop=(kc == NKC - 1))
                    cp("dve", v_aug[:, i, :, 0:64], o_.rearrange("p (a b) -> p a b", a=2, b=64))
                return
            bA = nb_()
            for kc in range(NKC):
                mm(psb(bA), W[:, kc, :], hT[:, kc, cols], start=(kc == 0), stop=(kc == NKC - 1))
            if kind == "u":
                act(Ubuf[j % 2][:, UPAD + tb * 512:UPAD + (tb + 1) * 512], psb(bA), AF.Copy)
                return
            sc_ = scr[unit[0] % NSCR]
            unit[0] += 1
            if kind == "q":
                cg, sgn, dest = cosg_q, sing_q, qT[:, j, cols]
            else:
                cg, sgn, dest = cosg_k, sing_k, kdup[:, j, cols]
            act(sc_.sq, psb(bA), AF.Square)
            act(sc_.abf, psb(bA), AF.Copy)
            tt("dve", sc_.t1, psb(bA), cg[:, cols], ALU.mult)

            def part2(sc_=sc_, sgn=sgn, dest=dest, cols=cols):
                bB = nb_()
                bC = nb_()
                mm(psb(bB), ones_bd, sc_.sq)
                mm(psb(bC), rrot, sc_.abf)
                act(sc_.rs, psb(bB), AF.Ln, bias=epsq)
                act(sc_.rs, sc_.rs, AF.Exp, scale=-0.5)
                tt("dve", sc_.t2, psb(bC), sgn[:, cols], ALU.mult)
                tt("dve", sc_.t1, sc_.t1, sc_.t2, ALU.add)
                tt("dve", dest, sc_.t1, sc_.rs, ALU.mult)

            pending.append(part2)
            while len(pending) > 1:
                pending.pop(0)()

        def flush_pending():
            while pending:
                pending.pop(0)()

        def finish_chunk(ci):
            kind, j = chunks[ci]
            if kind != "u":
                return
            g = j
            U = Ubuf[g % 2]
            prev = U
            exts = [8, 6, 4, 0]
            shifts = [(-1, 0), (-1, 1), (-2, 2), (-4, 4)]
            for l in range(g + 1):
                e_ = exts[l]
                dst = tmpAB[l % 2]
                lo = UPAD - e_
                n_ = S + 2 * e_
                s0, s1 = shifts[l]
                tt("dve", dst[:, lo:lo + n_], prev[:, lo + s0:lo + s0 + n_], prev[:, lo + s1:lo + s1 + n_], ALU.add)
                prev = dst
            Fv = prev
            tt("dve", Fv[:, UPAD:UPAD + 8], Fv[:, UPAD:UPAD + 8], fixtab[:, g * 16:g * 16 + 8], ALU.mult)
            tt("dve", Fv[:, UPAD + S - 8:UPAD + S], Fv[:, UPAD + S - 8:UPAD + S],
               fixtab[:, g * 16 + 8:g * 16 + 16], ALU.mult)
            fin = Fv[:, UPAD:UPAD + S]
            uin = U[:, UPAD:UPAD + S]
            winv = 1.0 / WIN[g]
            sch.op("dve", lambda e, fin=fin, uin=uin, winv=winv: e.scalar_tensor_tensor(
                pooled, fin, winv, uin, ALU.mult, ALU.subtract), reads=[fin, uin], writes=[pooled])

            def pool_mm(g=g, banks=None):
                for tb in range(4):
                    cols = slice(tb * 512, (tb + 1) * 512)
                    bk = banks[tb % len(banks)] if banks else nb_()
                    mm(psb(bk), pw[:, g, :], pooled[:, cols])
                    act(mixT[:, 4 + g, cols], psb(bk), AF.Identity, scale=pool_sc[:, g:g + 1], bias=pool_bs[:, g:g + 1])
            pool_mm_pending.append(pool_mm)

        def flush_pool_mm(banks=None):
            while pool_mm_pending:
                pool_mm_pending.pop(0)(banks=banks)

        NXR = len(xring)
        for i in range(NT + 2):
            if i < NT:
                dma("sp", xring[i % NXR], x_d[i * 128:(i + 1) * 128, :], ch_x[i % NXR])
            if i == 1:
                dma("sp", cosg_k, rope_d[:, 0:S], ch_c[3])
                dma("sp", sing_k, rope_d[:, S:2 * S], ch_c[4])
            if i == 4:
                act(cosg_k, cosg_k, AF.Copy, scale=gk)
                act(sing_k, sing_k, AF.Copy, scale=gk_perm)
            if i == 9:
                dma("sp", cosg_q, rope_d[:, 0:S], ch_c[5])
                dma("sp", sing_q, rope_d[:, S:2 * S], ch_c[3])
                act(cosg_q, cosg_q, AF.Copy, scale=gq)
                act(sing_q, sing_q, AF.Copy, scale=gq_perm)
            if i < NT:
                norm_p1(i, xring[i % NXR], ss1, junk)
            if 1 <= i <= NT:
                norm_p2(i - 1, xring[(i - 1) % NXR], ss1, ln1, rstd1, xn[(i - 1) % 3], scale_eng=("act" if (i % 2) else "dve"))
            if i >= 2:
                t_ = i - 2
                norm_transpose(t_, xn[t_ % 3], g1v, hT)
                if interleave and t_ % 4 == 3:
                    for ci in range(NPRE):
                        proc_chunk_tb(ci, t_ // 4)
        if stop <= 0:
            sch.frozen = True
        if not interleave:
            for tb in range(4):
                for ci in range(NPRE):
                    proc_chunk_tb(ci, tb)
        for i in range(2):
            memset("pool", Ubuf[i][:, 0:UPAD], 0.0)
            memset("pool", Ubuf[i][:, UPAD + S:UW], 0.0)
        dbg_out("hT", hT, [NKC, S], BF16)
        if stop <= 1:
            sch.frozen = True
        for ci in range(NPRE):
            finish_chunk(ci)
            if ci + 4 < len(chunks):
                load_chunk(ci + 4)
        groups = [(4, 5), (6, 7), (8, 9), (10,)]
        for grp in groups:
            for tb in range(4):
                for ci in grp:
                    proc_chunk_tb(ci, tb)
                if tb == 3:
                    flush_pool_mm()
            for ci in grp:
                finish_chunk(ci)
                if ci + 4 < len(chunks):
                    load_chunk(ci + 4)
        flush_pending()
        dbg_out("qT", qT, [4, S], BF16)
        dbg_out("kdup", kdup, [2, S], BF16)
        dbg_out("v_aug", v_aug, [NT, 2, 128], BF16)

        if stop <= 2:
            sch.frozen = True
        w_out_v = w_out_d.rearrange("(kc p) n -> p kc n", p=128)
        for hlf in range(2):
            dma("pool", wout_sb[:, hlf * 4:(hlf + 1) * 4, :], w_out_v[:, hlf * 4:(hlf + 1) * 4, :], ch_wo[hlf])
        w_gate_v = w_gate_d.rearrange("(kc p) n -> p kc n", p=128)
        w_up_v = w_up_d.rearrange("(kc p) n -> p kc n", p=128)
        pieces = [(0, 4), (4, 4), (8, 4), (12, 4), (16, 4), (20, 2)]

        def load_piece(p):
            fc0, nfc = pieces[p]
            s_ = p % 2
            c0 = fc0 * 128
            ncol = nfc * 128
            dma("pool", Gb[s_][:, :, 0:ncol], w_gate_v[:, :, c0:c0 + ncol], ch_g[s_])
            dma("pool", Ub[s_][:, :, 0:ncol], w_up_v[:, :, c0:c0 + ncol], ch_u[s_])
            dma("pool", Db[s_][:, 0:nfc, :], w_down_d[c0:c0 + ncol, :].rearrange("(fc p) n -> p fc n", p=128),
                ch_d[s_])

        load_piece(0)
        for i in range(NT):
            dma("sp", Rt[i], x_d[i * 128:(i + 1) * 128, :], ch_x[i % 4])

        flush_pool_mm()
        steps = [(j, qc, sc) for j in range(4) for qc in range(4) for sc in range(16)]
        ocp = [sbv(X0 + 16 * K + i * 2 * K, F32, 512) for i in range(2)]

        def mm1(idx):
            j, qc, sc = steps[idx]
            kh = j // 2
            sb_ = idx % 3
            q0 = qc * 512
            for hb in range(2):
                pr = hb * 64
                mm(psb(2 * sb_ + hb), kdup[pr:pr + 64, kh, sc * 128:(sc + 1) * 128], qT[pr:pr + 64, j, q0:q0 + 512])

        def expo(idx):
            sb_ = idx % 3
            src = ps[:, 2 * sb_:2 * sb_ + 2, :].rearrange("p a b -> p (a b)")
            act(PT[idx % 3], src, AF.Exp, scale=8.0, bias=negc)

        def mm2(idx):
            j, qc, sc = steps[idx]
            kh = j // 2
            for hb in range(2):
                mm(psb(6 + hb), v_aug[:, sc, kh, :], PT[idx % 3][:, hb * 512:(hb + 1) * 512],
                   start=(sc == 0), stop=(sc == 15))

        def onorm(j, qc):
            q0 = qc * 512
            for hb in range(2):
                cp("dve", ocp[hb], psb(6 + hb))
            for hb in range(2):
                pr = hb * 64
                recip(rec[hb][0:64, :], ocp[hb][64:128, :])
                tt("dve", mixT[pr:pr + 64, j, q0:q0 + 512], ocp[hb][0:64, :], rec[hb][0:64, :], ALU.mult)

        mm1(0)
        mm1(1)
        for idx in range(len(steps)):
            if idx + 2 < len(steps):
                mm1(idx + 2)
            expo(idx)
            mm2(idx)
            j, qc, sc = steps[idx]
            if sc == 15:
                onorm(j, qc)
        dbg_out("mixT", mixT, [8, S], BF16)
        if stop <= 3:
            sch.frozen = True
        load_piece(1)

        for i in range(NT + 3):
            if i < NT:
                for n in range(2):
                    bank = nb_()
                    for kc in range(NKC):
                        mm(psb(bank), mixT[:, kc, i * 128:(i + 1) * 128], wout_sb[:, kc, n * 512:(n + 1) * 512],
                           start=(kc == 0), stop=(kc == NKC - 1))
                    tt("dve", Rt[i][:, n * 512:(n + 1) * 512], psb(bank), Rt[i][:, n * 512:(n + 1) * 512], ALU.add)
            if 1 <= i <= NT:
                norm_p1(i - 1, Rt[i - 1], ss2, junk2)
            if 2 <= i <= NT + 1:
                norm_p2(i - 2, Rt[i - 2], ss2, ln2, rstd2, xn2[(i - 2) % 3])
            if i >= 3:
                norm_transpose(i - 3, xn2[(i - 3) % 3], g2v, h2T)
        if debug:
            d = nc.dram_tensor("dbg_x1", [S, D], F32, kind="ExternalOutput").ap()
            dbg_d["x1"] = d
            for i in range(NT):
                dbg_ops.append(dma("sp", d[i * 128:(i + 1) * 128, :], Rt[i], ch_dbg))
        dbg_out("h2T", h2T, [NKC, S], BF16)

        units = [(p, tb) for p in range(len(pieces)) for tb in range(4)]
        out_ops = []
        gu_cnt = [0]

        def gateup(k):
            p, tb = units[k]
            fc0, nfc = pieces[p]
            s_ = p % 2
            slot = k % 3
            cols = slice(tb * 512, (tb + 1) * 512)
            for f in range(nfc):
                pair = gu_cnt[0] % 3
                gu_cnt[0] += 1
                bg, bu = 2 * pair, 2 * pair + 1
                for kc in range(NKC):
                    mm(psb(bg), Gb[s_][:, kc, f * 128:(f + 1) * 128], h2T[:, kc, cols], start=(kc == 0),
                       stop=(kc == NKC - 1))
                for kc in range(NKC):
                    mm(psb(bu), Ub[s_][:, kc, f * 128:(f + 1) * 128], h2T[:, kc, cols], start=(kc == 0),
                       stop=(kc == NKC - 1))
                sgb = sg[gu_cnt[0] % 2]
                act(sgb, psb(bg), AF.Silu)
                tt("dve", actT[slot][:, f, :], psb(bu), sgb, ALU.mult)

        dn_cnt = [0]

        def down(k):
            p, tb = units[k]
            fc0, nfc = pieces[p]
            s_ = p % 2
            slot = k % 3
            for ti in range(4):
                i = tb * 4 + ti
                for n in range(2):
                    bank = 6 + dn_cnt[0] % 2
                    dn_cnt[0] += 1
                    for f in range(nfc):
                        mm(psb(bank), actT[slot][:, f, ti * 128:(ti + 1) * 128], Db[s_][:, f, n * 512:(n + 1) * 512],
                           start=(f == 0), stop=(f == nfc - 1))
                    tt("dve", Rt[i][:, n * 512:(n + 1) * 512], psb(bank), Rt[i][:, n * 512:(n + 1) * 512], ALU.add)
                if p == len(pieces) - 1:
                    out_ops.append(dma("sp", out_d[i * 128:(i + 1) * 128, :], Rt[i], ch_o[i % 4]))
            if tb == 3 and p + 2 < len(pieces):
                load_piece(p + 2)

        gateup(0)
        for k in range(1, len(units)):
            gateup(k)
            down(k - 1)
        down(len(units) - 1)

        sch.op("sp", None, extra_deps=out_ops + dbg_ops, force=True)

        sch.resolve(eng_sems)
        block = es.enter_context(nc.Block())
        block.tensor(sch.runner("pe"))
        block.scalar(sch.runner("act"))
        block.vector(sch.runner("dve"))
        block.gpsimd(sch.runner("pool"))
        block.sync(sch.runner("sp"))
    return nc, dbg_d


def _host_consts():
    f32 = np.float32
    f64 = np.float64
    inv_freq = f64(10000.0) ** (-(np.arange(16, dtype=f64)) / f64(16))
    t = np.arange(S)
    row = (t // 64).astype(f64)
    col = (t % 64).astype(f64)
    ang_row = row[:, None] * inv_freq[None, :]
    ang_col = col[:, None] * inv_freq[None, :]
    ang = np.zeros((64, S), f64)
    for d in range(64):
        a = ang_row if d < 32 else ang_col
        ang[d] = a[:, d % 16]
    cos = np.cos(ang)
    sin = np.sin(ang)
    rope = np.concatenate([np.tile(cos, (2, 1)), np.tile(sin, (2, 1))], axis=1).astype(f32)
    ident = np.eye(128, dtype=f32)
    ones_bd = np.zeros((128, 128), f32)
    ones_bd[:64, :64] = 1
    ones_bd[64:, 64:] = 1
    rr = np.zeros((128, 128), f32)
    perm = np.zeros(128, np.int64)
    for m in range(128):
        if m % 32 < 16:
            rr[m + 16, m] = -1.0
            perm[m] = m + 16
        else:
            rr[m - 16, m] = 1.0
            perm[m] = m - 16
    constb = np.concatenate([ident, ones_bd, rr], axis=1).astype(f32)
    fix = np.ones((4, 16), f64)
    for g, w in enumerate([2, 4, 8, 16]):
        for jj in range(8):
            for side, tt_ in ((0, jj), (1, S - 8 + jj)):
                lo = min(max(tt_ - w // 2, 0), S)
                hi = min(max(tt_ - w // 2 + w, 0), S)
                fix[g, side * 8 + jj] = f64(w) / f64(hi - lo)
    return rope, constb, fix.astype(f32), perm


_NC_CACHE = {}


def _prep_inputs(x, norm1_g, w_in, q_norm_g, k_norm_g, pool_w, pool_b, pool_scale, w_out, norm2_g,
                 w_gate, w_up, w_down):
    f32 = np.float32
    rope, constb, fix, perm = _host_consts()
    constf = np.zeros((128, NF), f32)
    constf[:, 0:8] = np.asarray(norm1_g, f32).reshape(8, 128).T
    constf[:, 8:16] = np.asarray(norm2_g, f32).reshape(8, 128).T
    constf[:, 16] = np.tile(np.asarray(q_norm_g, f32).reshape(64), 2)
    constf[:, 17] = np.tile(np.asarray(k_norm_g, f32).reshape(64), 2)
    constf[:, 18:22] = np.asarray(pool_b, f32).reshape(4, 128).T
    constf[:, 22:26] = np.asarray(pool_scale, f32).reshape(4, 128).T
    constf[:, 26:90] = np.asarray(q_norm_g, f32).reshape(1, 64)
    constf[:, 90:154] = np.asarray(k_norm_g, f32).reshape(1, 64)
    constf[:, 154:218] = fix.reshape(1, 64)
    constf[:, 218] = np.tile(np.asarray(q_norm_g, f32).reshape(64), 2)[perm]
    constf[:, 219] = np.tile(np.asarray(k_norm_g, f32).reshape(64), 2)[perm]
    shared = {
        "w_in": np.ascontiguousarray(np.asarray(w_in, f32)[0]),
        "w_out": np.ascontiguousarray(np.asarray(w_out, f32)[0]),
        "w_gate": np.ascontiguousarray(np.asarray(w_gate, f32)[0]),
        "w_up": np.ascontiguousarray(np.asarray(w_up, f32)[0]),
        "w_down": np.ascontiguousarray(np.asarray(w_down, f32)[0]),
        "pool_w": np.ascontiguousarray(np.asarray(pool_w, f32)[0]),
        "constf": constf,
        "constb": constb,
        "rope": rope,
    }
    xs = np.asarray(x, f32)
    in_maps = []
    for c in range(N_CORES):
        m = dict(shared)
        m["x"] = np.ascontiguousarray(xs[c])
        in_maps.append(m)
    return in_maps


def kernel(x, norm1_g, w_in, q_norm_g, k_norm_g, pool_w, pool_b, pool_scale, w_out, norm2_g,
           w_gate, w_up, w_down):
    in_maps = _prep_inputs(x, norm1_g, w_in, q_norm_g, k_norm_g, pool_w, pool_b, pool_scale, w_out,
                           norm2_g, w_gate, w_up, w_down)
    if "nc" not in _NC_CACHE:
        _NC_CACHE["nc"] = build_nc(debug=False)[0]
    nc = _NC_CACHE["nc"]
    res = run_bass_kernel_spmd(nc, in_maps, core_ids=list(range(N_CORES)))
    out = np.stack([np.asarray(r["out"], np.float32) for r in res.results], axis=0)
    return out
```

```python
import math
from contextlib import ExitStack

import numpy as np
import concourse.bass as bass
import concourse.mybir as mybir
from concourse.bass_utils import run_bass_kernel_spmd

F32 = mybir.dt.float32
BF16 = mybir.dt.bfloat16
ALU = mybir.AluOpType
AF = mybir.ActivationFunctionType
AX = mybir.AxisListType

S = 2048
D = 1024
NT = 16
NKC = 8
DFF = 2816
EPS = 1e-6
N_CORES = 8

K = 1024
R0 = 0
H0 = 64 * K
M0 = 96 * K
Q0 = 128 * K
X0 = 160 * K
ARENA_BYTES = X0 + 47 * K
NF = 224


def _dsize(dt):
    return mybir.dt.size(dt)


class _Op:
    __slots__ = ("eng", "emit", "deps", "signal", "sem", "val", "chan", "waits", "idx")


class _Chan:
    def __init__(self, sem):
        self.sem = sem
        self.count = 0
        self.last = None


class Sched:
    def __init__(self):
        self.ops = []
        self.frozen = False
        self.recs = {"SB": [], "PSUM": []}

    @staticmethod
    def ranges(ap):
        sp = str(ap.space)
        if sp not in ("SB", "PSUM"):
            return None, []
        dims = ap.ap
        pstride = dims[0][0]
        es = _dsize(ap.dtype)
        off = ap.offset % pstride if pstride > 0 else ap.offset
        free = sorted([(s, c) for s, c in dims[1:] if c > 1 and s > 0])
        run = 1
        rest = []
        for s, c in free:
            if s == run and not rest:
                run *= c
            else:
                rest.append((s, c))
        nouter = 1
        for s, c in rest:
            nouter *= c
        out = []
        if nouter <= 64:
            starts = [off]
            for s, c in rest:
                starts = [b + s * k for b in starts for k in range(c)]
            for b in starts:
                out.append((b * es, (b + run) * es))
        else:
            hi = off + run
            for s, c in rest:
                hi += s * (c - 1)
            out.append((off * es, hi * es))
        if sp == "PSUM":
            out = [((lo // 2048) * 2048, ((hi + 2047) // 2048) * 2048) for lo, hi in out]
        out.sort()
        merged = []
        for lo, hi in out:
            if merged and lo <= merged[-1][1]:
                merged[-1] = (merged[-1][0], max(hi, merged[-1][1]))
            else:
                merged.append((lo, hi))
        return sp, merged

    def op(self, eng, emit, reads=(), writes=(), chan=None, extra_deps=(), force=False):
        if self.frozen and not force:
            return None
        extra_deps = [d for d in extra_deps if d is not None]
        o = _Op()
        o.eng = eng
        o.emit = emit
        o.deps = set(extra_deps)
        o.signal = False
        o.sem = None
        o.val = None
        o.chan = chan
        o.waits = []
        o.idx = len(self.ops)
        is_dma = chan is not None
        for ap in reads:
            sp, rs = self.ranges(ap)
            if sp is None:
                continue
            recs = self.recs[sp]
            for lo, hi in rs:
                keep = []
                for r in recs:
                    if r[2] != o.idx and r[0] < hi and lo < r[1]:
                        if r[3]:
                            o.deps.add(r[2])
                        elif sp == "PSUM" and self.ops[r[2]].eng != eng:
                            o.deps.add(r[2])
                        elif (not is_dma) and self.ops[r[2]].chan is None and self.ops[r[2]].eng == eng \
                                and lo <= r[0] and r[1] <= hi:
                            continue
                    keep.append(r)
                keep.append([lo, hi, o.idx, False])
                self.recs[sp] = recs = keep
        for ap in writes:
            sp, rs = self.ranges(ap)
            if sp is None:
                continue
            recs = self.recs[sp]
            for lo, hi in rs:
                keep = []
                for r in recs:
                    if r[0] < hi and lo < r[1]:
                        if r[2] != o.idx:
                            o.deps.add(r[2])
                        if lo <= r[0] and r[1] <= hi:
                            continue
                    keep.append(r)
                keep.append([lo, hi, o.idx, True])
                self.recs[sp] = recs = keep
        if is_dma:
            if chan.last is not None:
                o.deps.add(chan.last)
            chan.last = o.idx
        o.deps.discard(o.idx)
        self.ops.append(o)
        return o.idx

    def resolve(self, eng_sems):
        ops = self.ops

        def skip(o, d):
            return o.eng == "pe" and d.eng == "pe" and o.chan is None and d.chan is None

        for o in ops:
            for di in o.deps:
                d = ops[di]
                if not skip(o, d):
                    d.signal = True
        counters = {e: 0 for e in eng_sems}
        for o in ops:
            if o.chan is not None:
                o.chan.count += 16
                o.sem = o.chan.sem
                o.val = o.chan.count
            elif o.signal:
                counters[o.eng] += 1
                o.sem = eng_sems[o.eng]
                o.val = counters[o.eng]
        waited = {}
        for o in ops:
            w = {}
            wd = waited.setdefault(o.eng, {})
            for di in o.deps:
                d = ops[di]
                if skip(o, d):
                    continue
                key = id(d.sem)
                if wd.get(key, (None, 0))[1] >= d.val:
                    continue
                if key not in w or w[key][1] < d.val:
                    w[key] = (d.sem, d.val)
            for key, sv in w.items():
                wd[key] = sv
            o.waits = list(w.values())

    def runner(self, eng):
        mine = [o for o in self.ops if o.eng == eng]

        def run(e):
            for o in mine:
                for sem, val in o.waits:
                    e.wait_ge(sem, val)
                if o.emit is None:
                    continue
                ins = o.emit(e)
                if o.chan is not None:
                    ins.then_inc(o.sem, 16)
                elif o.signal:
                    ins.then_inc(o.sem, 1)
        return run


def build_nc(debug=False, stop=99, use_ln=True, interleave=True):
    nc = bass.Bass("TRN2", target_bir_lowering=False)
    dr = {}

    def din(name, shape):
        dr[name] = nc.dram_tensor(name, list(shape), F32, kind="ExternalInput").ap()
        return dr[name]

    x_d = din("x", [S, D])
    w_in_d = din("w_in", [D, 1280])
    w_out_d = din("w_out", [D, D])
    w_gate_d = din("w_gate", [D, DFF])
    w_up_d = din("w_up", [D, DFF])
    w_down_d = din("w_down", [DFF, D])
    pool_w_d = din("pool_w", [4, 128, 128])
    constf_d = din("constf", [128, NF])
    constb_d = din("constb", [128, 384])
    rope_d = din("rope", [128, 2 * S])
    out_d = nc.dram_tensor("out", [S, D], F32, kind="ExternalOutput").ap()
    dbg_d = {}

    sch = Sched()

    with ExitStack() as es:
        es.enter_context(nc.allow_low_precision("bf16 matmul operands, fp32 accumulation"))
        arena = es.enter_context(nc.sbuf_tensor("arena", [128, ARENA_BYTES // 4], F32))
        ps = es.enter_context(nc.psum_tensor("ps", [128, 8, 512], F32))
        eng_sems = {e: es.enter_context(nc.semaphore("sem_" + e)) for e in ("pe", "act", "dve", "pool", "sp")}

        def new_chan(name):
            return _Chan(es.enter_context(nc.semaphore("ch_" + name)))

        def sbv(off, dt, *shape):
            n = 1
            for s_ in shape:
                n *= s_
            nb = n * _dsize(dt)
            assert off % 4 == 0 and nb % 4 == 0 and off + nb <= ARENA_BYTES, (off, nb)
            v = arena[:, off // 4:(off + nb) // 4]
            if dt != F32:
                v = v.bitcast(dt)
            if len(shape) == 2:
                v = v.rearrange("p (a b) -> p a b", a=shape[0], b=shape[1])
            elif len(shape) == 3:
                v = v.rearrange("p (a b c) -> p a b c", a=shape[0], b=shape[1], c=shape[2])
            return v

        def psb(bank):
            return ps[:, bank, :]

        def psb_bf(bank):
            return ps[:, bank, :].bitcast(BF16)

        _bank = [0]

        def nb_():
            b = _bank[0]
            _bank[0] = (b + 1) % 8
            return b

        def dma(q, out, in_, chan):
            return sch.op(q, lambda e: e.dma_start(out=out, in_=in_), reads=[in_], writes=[out], chan=chan)

        def mm(out, lhsT, rhs, start=True, stop=True):
            return sch.op("pe", lambda e: e.matmul(out, lhsT, rhs, start=start, stop=stop),
                          reads=[lhsT, rhs], writes=[out])

        def tr(out, in_, ident):
            return sch.op("pe", lambda e: e.transpose(out, in_, ident), reads=[in_, ident], writes=[out])

        def act(out, in_, func, scale=1.0, bias=None, accum=None):
            reads = [in_]
            writes = [out]
            kw = {}
            if not isinstance(scale, (int, float)):
                reads.append(scale)
            if bias is not None:
                kw["bias"] = bias
                if not isinstance(bias, (int, float)):
                    reads.append(bias)
            if accum is not None:
                kw["accum_out"] = accum
                writes.append(accum)
            return sch.op("act", lambda e: e.activation(out, in_, func, scale=scale, **kw),
                          reads=reads, writes=writes)

        def tt(eng, out, in0, in1, op):
            return sch.op(eng, lambda e: e.tensor_tensor(out, in0, in1, op), reads=[in0, in1], writes=[out])

        def ts(eng, out, in0, s1, s2, op0, op1=None):
            reads = [in0] + [s_ for s_ in (s1, s2) if s_ is not None and not isinstance(s_, (int, float))]
            if op1 is None:
                return sch.op(eng, lambda e: e.tensor_scalar(out, in0, s1, None, op0), reads=reads, writes=[out])
            return sch.op(eng, lambda e: e.tensor_scalar(out, in0, s1, s2, op0, op1), reads=reads, writes=[out])

        def cp(eng, out, in_):
            return sch.op(eng, lambda e: e.tensor_copy(out, in_), reads=[in_], writes=[out])

        def recip(out, in_):
            return sch.op("dve", lambda e: e.reciprocal(out, in_), reads=[in_], writes=[out])

        def memset(eng, ap, val):
            return sch.op(eng, lambda e: e.memset(ap, val), writes=[ap])

        C0 = X0 + 38 * K
        constf = sbv(C0, F32, NF)
        constb = sbv(C0 + 1024, BF16, 384)
        pw = sbv(C0 + 2048, BF16, 4, 128)
        stats = sbv(C0 + 3072, F32, 128)
        rec = [sbv(C0 + 3584 + i * 2048, F32, 512) for i in range(2)]
        g1v = constf[:, 0:8]
        g2v = constf[:, 8:16]
        gq = constf[:, 16:17]
        gk = constf[:, 17:18]
        pool_b = constf[:, 18:22]
        pool_sc = constf[:, 22:26]
        gq_row = constf[:, 26:90]
        gk_row = constf[:, 90:154]
        fixtab = constf[:, 154:218]
        gq_perm = constf[:, 218:219]
        gk_perm = constf[:, 219:220]
        ident = constb[:, 0:128]
        ones_bd = constb[:, 128:256]
        rrot = constb[:, 256:384]
        ss1 = stats[:, 0:16]
        ln1 = stats[:, 16:32]
        rstd1 = stats[:, 32:48]
        ss2 = stats[:, 48:64]
        ln2 = stats[:, 64:80]
        rstd2 = stats[:, 80:96]
        mq = stats[:, 96:97]
        mk = stats[:, 97:98]
        negc = stats[:, 98:99]
        epsq = stats[:, 99:100]
        eps1 = stats[:, 100:101]
        pool_bs = stats[:, 104:108]

        hT = sbv(H0, BF16, NKC, S)
        mixT = sbv(M0, BF16, 8, S)
        qT = sbv(Q0, BF16, 4, S)
        kdup = sbv(Q0 + 16 * K, BF16, 2, S)
        v_aug = sbv(Q0 + 24 * K, BF16, NT, 2, 128)
        cosg_q = sbv(X0, F32, S)
        sing_q = sbv(X0 + 8 * K, F32, S)
        cosg_k = sbv(M0, F32, S)
        sing_k = sbv(M0 + 8 * K, F32, S)
        wring = [sbv(X0 + 16 * K + i * 2048, BF16, NKC, 128) for i in range(4)]
        PT = [sbv(X0 + 32 * K + i * 2048, BF16, 1024) for i in range(3)]
        wout_sb = sbv(X0, BF16, NKC, D)
        UPAD = 16
        UW = S + 2 * UPAD
        Ubuf = [sbv(R0 + i * UW * 4, F32, UW) for i in range(2)]
        tmpAB = [sbv(R0 + (2 + i) * UW * 4, F32, UW) for i in range(2)]
        pooled = sbv(R0 + 4 * UW * 4, BF16, S)
        QK0 = R0 + 4 * UW * 4 + 4096

        class _Scr:
            pass

        scr = []
        NSCR = 3
        for i in range(NSCR):
            b = QK0 + i * 8 * K
            s_ = _Scr()
            s_.sq = sbv(b, BF16, 512)
            s_.abf = sbv(b + 1 * K, BF16, 512)
            s_.rs = sbv(b + 2 * K, F32, 512)
            s_.t2 = sbv(b + 4 * K, F32, 512)
            s_.t1 = sbv(b + 6 * K, F32, 512)
            scr.append(s_)
        assert QK0 + NSCR * 8 * K <= R0 + 64 * K
        xring = [sbv(M0 + 16 * K + i * 4 * K, F32, D) for i in range(4)]
        xn = [sbv(X0 + 32 * K + i * 2 * K, BF16, D) for i in range(3)]
        junk = sbv(X0 + 24 * K, BF16, D)
        Rt = [sbv(R0 + i * 4 * K, F32, D) for i in range(NT)]
        h2T = sbv(Q0, BF16, NKC, S)
        Gb = [sbv(H0, BF16, NKC, 512), sbv(H0 + 24 * K, BF16, NKC, 512)]
        Ub = [sbv(H0 + 8 * K, BF16, NKC, 512), sbv(X0 + 16 * K, BF16, NKC, 512)]
        Db = [sbv(H0 + 16 * K, BF16, 4, D), sbv(X0 + 24 * K, BF16, 4, D)]
        actT = [sbv(M0 + i * 4 * K, BF16, 4, 512) for i in range(3)]
        sg = [sbv(M0 + 12 * K + i * 2 * K, F32, 512) for i in range(2)]
        xn2 = [sbv(X0 + 32 * K + i * 2 * K, BF16, D) for i in range(3)]
        junk2 = sbv(X0 + 38 * K + 3584, BF16, D)

        ch_c = [new_chan("c%d" % i) for i in range(6)]
        ch_x = [new_chan("x%d" % i) for i in range(4)]
        ch_w = [new_chan("w%d" % i) for i in range(4)]
        ch_wo = [new_chan("wo%d" % i) for i in range(2)]
        ch_g = [new_chan("g%d" % i) for i in range(2)]
        ch_u = [new_chan("u%d" % i) for i in range(2)]
        ch_d = [new_chan("d%d" % i) for i in range(2)]
        ch_o = [new_chan("o%d" % i) for i in range(4)]
        ch_dbg = new_chan("dbg")

        def dbg_out(name, ap, shape, dt):
            if not debug:
                return
            d = nc.dram_tensor("dbg_" + name, [128] + list(shape), dt, kind="ExternalOutput").ap()
            dbg_d[name] = d
            dbg_ops.append(dma("sp", d, ap, ch_dbg))

        dbg_ops = []

        dma("sp", constf, constf_d, ch_c[0])
        dma("pool", constb, constb_d, ch_c[1])
        dma("pool", pw, pool_w_d.rearrange("g c d -> c g d"), ch_c[2])
        w_in_v = w_in_d.rearrange("(kc p) n -> p kc n", p=128)

        chunks = [("kd", 0), ("kd", 1), ("v", 0), ("u", 0), ("q", 0), ("u", 1), ("q", 1),
                  ("u", 2), ("q", 2), ("u", 3), ("q", 3)]
        NPRE = 4

        def load_chunk(ci):
            kind, j = chunks[ci]
            slot = ci % 4
            if kind == "q":
                dma("pool", wring[slot], w_in_v[:, :, j * 128:(j + 1) * 128], ch_w[slot])
            elif kind == "kd":
                c0 = 512 + j * 64
                dma("pool", wring[slot][:, :, 0:64], w_in_v[:, :, c0:c0 + 64], ch_w[slot])
                dma("pool", wring[slot][:, :, 64:128], w_in_v[:, :, c0:c0 + 64], ch_w[slot])
            elif kind == "v":
                dma("pool", wring[slot], w_in_v[:, :, 640:768], ch_w[slot])
            else:
                c0 = 768 + j * 128
                dma("pool", wring[slot], w_in_v[:, :, c0:c0 + 128], ch_w[slot])

        for ci in range(4):
            load_chunk(ci)

        memset("dve", epsq, 64.0 * EPS)
        memset("dve", eps1, EPS)
        tt("dve", pool_bs, pool_b, pool_sc, ALU.mult)
        memset("pool", v_aug[:, :, :, 64:128], 1.0)
        tt("dve", rec[0][:, 0:64], gq_row, gq_row, ALU.mult)
        sch.op("dve", lambda e: e.reduce_max(out=mq, in_=rec[0][:, 0:64], axis=AX.X),
               reads=[rec[0][:, 0:64]], writes=[mq])
        tt("dve", rec[0][:, 64:128], gk_row, gk_row, ALU.mult)
        sch.op("dve", lambda e: e.reduce_max(out=mk, in_=rec[0][:, 64:128], axis=AX.X),
               reads=[rec[0][:, 64:128]], writes=[mk])
        tt("dve", negc, mq, mk, ALU.mult)
        if use_ln:
            act(negc, negc, AF.Ln)
            act(negc, negc, AF.Exp, scale=0.5)
        else:
            act(negc, negc, AF.Sqrt)
        sch.op("dve", lambda e: e.tensor_scalar(negc, negc, -8.0, None, ALU.mult), reads=[negc], writes=[negc])

        def norm_p1(i, xt, ss, jk):
            act(jk, xt, AF.Square, accum=ss[:, i:i + 1])

        def norm_p2(i, xt, ss, lnv, rstd, xnb, scale_eng="act"):
            act(lnv[:, i:i + 1], ss[:, i:i + 1], AF.Ln, scale=1.0 / D, bias=eps1)
            act(rstd[:, i:i + 1], lnv[:, i:i + 1], AF.Exp, scale=-0.5)
            if scale_eng == "act":
                act(xnb, xt, AF.Copy, scale=rstd[:, i:i + 1])
            else:
                ts(scale_eng, xnb, xt, rstd[:, i:i + 1], None, ALU.mult)

        def norm_transpose(i, xnb, gv, dstT):
            bank = nb_()
            pb = psb_bf(bank)
            for kc in range(NKC):
                tr(pb[:, kc * 128:(kc + 1) * 128], xnb[:, kc * 128:(kc + 1) * 128], ident)
            g3 = gv.rearrange("p (a b) -> p a b", b=1).to_broadcast([128, NKC, 128])
            tt("dve", dstT[:, :, i * 128:(i + 1) * 128], pb.rearrange("p (a b) -> p a b", b=128), g3, ALU.mult)

        unit = [0]
        pending = []
        pool_mm_pending = []
        WIN = [2, 4, 8, 16]

        def proc_chunk_tb(ci, tb):
            kind, j = chunks[ci]
            W = wring[ci % 4]
            cols = slice(tb * 512, (tb + 1) * 512)
            if kind == "v":
                for i in range(tb * 4, tb * 4 + 4):
                    bank = nb_()
                    o_ = psb(bank)[:, 0:128]
                    for kc in range(NKC):
                        mm(o_, hT[:, kc, i * 128:(i + 1) * 128], W[:, kc, :], start=(kc == 0), stop=(kc == NKC - 1))
                    cp("dve", v_aug[:, i, :, 0:64], o_.rearrange("p (a b) -> p a b", a=2, b=64))
                return
            bA = nb_()
            for kc in range(NKC):
                mm(psb(bA), W[:, kc, :], hT[:, kc, cols], start=(kc == 0), stop=(kc == NKC - 1))
            if kind == "u":
                act(Ubuf[j % 2][:, UPAD + tb * 512:UPAD + (tb + 1) * 512], psb(bA), AF.Copy)
                return
            sc_ = scr[unit[0] % NSCR]
            unit[0] += 1
            if kind == "q":
                cg, sgn, dest = cosg_q, sing_q, qT[:, j, cols]
            else:
                cg, sgn, dest = cosg_k, sing_k, kdup[:, j, cols]
            act(sc_.sq, psb(bA), AF.Square)
            act(sc_.abf, psb(bA), AF.Copy)
            tt("dve", sc_.t1, psb(bA), cg[:, cols], ALU.mult)

            def part2(sc_=sc_, sgn=sgn, dest=dest, cols=cols):
                bB = nb_()
                bC = nb_()
                mm(psb(bB), ones_bd, sc_.sq)
                mm(psb(bC), rrot, sc_.abf)
                act(sc_.rs, psb(bB), AF.Ln, bias=epsq)
                act(sc_.rs, sc_.rs, AF.Exp, scale=-0.5)
                tt("dve", sc_.t2, psb(bC), sgn[:, cols], ALU.mult)
                tt("dve", sc_.t1, sc_.t1, sc_.t2, ALU.add)
                tt("dve", dest, sc_.t1, sc_.rs, ALU.mult)

            pending.append(part2)
            while len(pending) > 1:
                pending.pop(0)()

        def flush_pending():
            while pending:
                pending.pop(0)()

        def finish_chunk(ci):
            kind, j = chunks[ci]
            if kind != "u":
                return
            g = j
            U = Ubuf[g % 2]
            prev = U
            exts = [8, 6, 4, 0]
            shifts = [(-1, 0), (-1, 1), (-2, 2), (-4, 4)]
            for l in range(g + 1):
                e_ = exts[l]
                dst = tmpAB[l % 2]
                lo = UPAD - e_
                n_ = S + 2 * e_
                s0, s1 = shifts[l]
                tt("dve", dst[:, lo:lo + n_], prev[:, lo + s0:lo + s0 + n_], prev[:, lo + s1:lo + s1 + n_], ALU.add)
                prev = dst
            Fv = prev
            tt("dve", Fv[:, UPAD:UPAD + 8], Fv[:, UPAD:UPAD + 8], fixtab[:, g * 16:g * 16 + 8], ALU.mult)
            tt("dve", Fv[:, UPAD + S - 8:UPAD + S], Fv[:, UPAD + S - 8:UPAD + S],
               fixtab[:, g * 16 + 8:g * 16 + 16], ALU.mult)
            fin = Fv[:, UPAD:UPAD + S]
            uin = U[:, UPAD:UPAD + S]
            winv = 1.0 / WIN[g]
            sch.op("dve", lambda e, fin=fin, uin=uin, winv=winv: e.scalar_tensor_tensor(
                pooled, fin, winv, uin, ALU.mult, ALU.subtract), reads=[fin, uin], writes=[pooled])

            def pool_mm(g=g, banks=None):
                for tb in range(4):
                    cols = slice(tb * 512, (tb + 1) * 512)
                    bk = banks[tb % len(banks)] if banks else nb_()
                    mm(psb(bk), pw[:, g, :], pooled[:, cols])
                    act(mixT[:, 4 + g, cols], psb(bk), AF.Identity, scale=pool_sc[:, g:g + 1], bias=pool_bs[:, g:g + 1])
            pool_mm_pending.append(pool_mm)

        def flush_pool_mm(banks=None):
            while pool_mm_pending:
                pool_mm_pending.pop(0)(banks=banks)

        NXR = len(xring)
        pre_q = []
        for i in range(NT + 2):
            if i < NT:
                dma("sp", xring[i % NXR], x_d[i * 128:(i + 1) * 128, :], ch_x[i % NXR])
            if i == 1:
                dma("sp", cosg_k, rope_d[:, 0:S], ch_c[3])
                dma("sp", sing_k, rope_d[:, S:2 * S], ch_c[4])
            if i == 4:
                act(cosg_k, cosg_k, AF.Copy, scale=gk)
                act(sing_k, sing_k, AF.Copy, scale=gk_perm)
            if i == 9:
                dma("sp", cosg_q, rope_d[:, 0:S], ch_c[5])
                dma("sp", sing_q, rope_d[:, S:2 * S], ch_c[3])
                act(cosg_q, cosg_q, AF.Copy, scale=gq)
                act(sing_q, sing_q, AF.Copy, scale=gq_perm)
            if i < NT:
                norm_p1(i, xring[i % NXR], ss1, junk)
            if 1 <= i <= NT:
                norm_p2(i - 1, xring[(i - 1) % NXR], ss1, ln1, rstd1, xn[(i - 1) % 3], scale_eng=("act" if (i % 2) else "dve"))
            if i >= 2:
                t_ = i - 2
                norm_transpose(t_, xn[t_ % 3], g1v, hT)
                if interleave and t_ % 4 == 3:
                    pre_q.extend((ci, t_ // 4) for ci in range(NPRE))
            if pre_q:
                proc_chunk_tb(*pre_q.pop(0))
        while pre_q:
            proc_chunk_tb(*pre_q.pop(0))
        if stop <= 0:
            sch.frozen = True
        if not interleave:
            for tb in range(4):
                for ci in range(NPRE):
                    proc_chunk_tb(ci, tb)
        for i in range(2):
            memset("pool", Ubuf[i][:, 0:UPAD], 0.0)
            memset("pool", Ubuf[i][:, UPAD + S:UW], 0.0)
        dbg_out("hT", hT, [NKC, S], BF16)
        if stop <= 1:
            sch.frozen = True
        for ci in range(NPRE):
            finish_chunk(ci)
            if ci + 4 < len(chunks):
                load_chunk(ci + 4)
        groups = [(4, 5), (6, 7), (8, 9), (10,)]
        for grp in groups:
            for tb in range(4):
                for ci in grp:
                    proc_chunk_tb(ci, tb)
                if tb == 3:
                    flush_pool_mm()
            for ci in grp:
                finish_chunk(ci)
                if ci + 4 < len(chunks):
                    load_chunk(ci + 4)
        flush_pending()
        dbg_out("qT", qT, [4, S], BF16)
        dbg_out("kdup", kdup, [2, S], BF16)
        dbg_out("v_aug", v_aug, [NT, 2, 128], BF16)

        if stop <= 2:
            sch.frozen = True
        w_out_v = w_out_d.rearrange("(kc p) n -> p kc n", p=128)
        for hlf in range(2):
            dma("pool", wout_sb[:, hlf * 4:(hlf + 1) * 4, :], w_out_v[:, hlf * 4:(hlf + 1) * 4, :], ch_wo[hlf])
        w_gate_v = w_gate_d.rearrange("(kc p) n -> p kc n", p=128)
        w_up_v = w_up_d.rearrange("(kc p) n -> p kc n", p=128)
        pieces = [(0, 4), (4, 4), (8, 4), (12, 4), (16, 4), (20, 2)]

        def load_piece(p):
            fc0, nfc = pieces[p]
            s_ = p % 2
            c0 = fc0 * 128
            ncol = nfc * 128
            dma("pool", Gb[s_][:, :, 0:ncol], w_gate_v[:, :, c0:c0 + ncol], ch_g[s_])
            dma("pool", Ub[s_][:, :, 0:ncol], w_up_v[:, :, c0:c0 + ncol], ch_u[s_])
            dma("pool", Db[s_][:, 0:nfc, :], w_down_d[c0:c0 + ncol, :].rearrange("(fc p) n -> p fc n", p=128),
                ch_d[s_])

        load_piece(0)
        for i in range(NT):
            dma("sp", Rt[i], x_d[i * 128:(i + 1) * 128, :], ch_x[i % 4])

        flush_pool_mm()
        steps = [(j, qc, sc) for j in range(4) for qc in range(4) for sc in range(16)]
        ocp = [sbv(X0 + 16 * K + i * 2 * K, F32, 512) for i in range(2)]

        def mm1(idx):
            j, qc, sc = steps[idx]
            kh = j // 2
            sb_ = idx % 3
            q0 = qc * 512
            for hb in range(2):
                pr = hb * 64
                mm(psb(2 * sb_ + hb), kdup[pr:pr + 64, kh, sc * 128:(sc + 1) * 128], qT[pr:pr + 64, j, q0:q0 + 512])

        def expo(idx):
            sb_ = idx % 3
            src = ps[:, 2 * sb_:2 * sb_ + 2, :].rearrange("p a b -> p (a b)")
            act(PT[idx % 3], src, AF.Exp, scale=8.0, bias=negc)

        def mm2(idx):
            j, qc, sc = steps[idx]
            kh = j // 2
            for hb in range(2):
                mm(psb(6 + hb), v_aug[:, sc, kh, :], PT[idx % 3][:, hb * 512:(hb + 1) * 512],
                   start=(sc == 0), stop=(sc == 15))

        def onorm(j, qc):
            q0 = qc * 512
            for hb in range(2):
                cp("dve", ocp[hb], psb(6 + hb))
            for hb in range(2):
                pr = hb * 64
                recip(rec[hb][0:64, :], ocp[hb][64:128, :])
                tt("dve", mixT[pr:pr + 64, j, q0:q0 + 512], ocp[hb][0:64, :], rec[hb][0:64, :], ALU.mult)

        mm1(0)
        mm1(1)
        for idx in range(len(steps)):
            if idx + 2 < len(steps):
                mm1(idx + 2)
            expo(idx)
            mm2(idx)
            j, qc, sc = steps[idx]
            if sc == 15:
                onorm(j, qc)
        dbg_out("mixT", mixT, [8, S], BF16)
        if stop <= 3:
            sch.frozen = True
        load_piece(1)

        for i in range(NT + 3):
            if i < NT:
                for n in range(2):
                    bank = nb_()
                    for kc in range(NKC):
                        mm(psb(bank), mixT[:, kc, i * 128:(i + 1) * 128], wout_sb[:, kc, n * 512:(n + 1) * 512],
                           start=(kc == 0), stop=(kc == NKC - 1))
                    tt("dve", Rt[i][:, n * 512:(n + 1) * 512], psb(bank), Rt[i][:, n * 512:(n + 1) * 512], ALU.add)
            if 1 <= i <= NT:
                norm_p1(i - 1, Rt[i - 1], ss2, junk2)
            if 2 <= i <= NT + 1:
                norm_p2(i - 2, Rt[i - 2], ss2, ln2, rstd2, xn2[(i - 2) % 3])
            if i >= 3:
                norm_transpose(i - 3, xn2[(i - 3) % 3], g2v, h2T)
        if debug:
            d = nc.dram_tensor("dbg_x1", [S, D], F32, kind="ExternalOutput").ap()
            dbg_d["x1"] = d
            for i in range(NT):
                dbg_ops.append(dma("sp", d[i * 128:(i + 1) * 128, :], Rt[i], ch_dbg))
        dbg_out("h2T", h2T, [NKC, S], BF16)

        units = [(p, tb) for p in range(len(pieces)) for tb in range(4)]
        out_ops = []
        gu_cnt = [0]

        def gateup(k):
            p, tb = units[k]
            fc0, nfc = pieces[p]
            s_ = p % 2
            slot = k % 3
            cols = slice(tb * 512, (tb + 1) * 512)
            for f in range(nfc):
                pair = gu_cnt[0] % 3
                gu_cnt[0] += 1
                bg, bu = 2 * pair, 2 * pair + 1
                for kc in range(NKC):
                    mm(psb(bg), Gb[s_][:, kc, f * 128:(f + 1) * 128], h2T[:, kc, cols], start=(kc == 0),
                       stop=(kc == NKC - 1))
                for kc in range(NKC):
                    mm(psb(bu), Ub[s_][:, kc, f * 128:(f + 1) * 128], h2T[:, kc, cols], start=(kc == 0),
                       stop=(kc == NKC - 1))
                sgb = sg[gu_cnt[0] % 2]
                act(sgb, psb(bg), AF.Silu)
                tt("dve", actT[slot][:, f, :], psb(bu), sgb, ALU.mult)

        dn_cnt = [0]

        def down(k):
            p, tb = units[k]
            fc0, nfc = pieces[p]
            s_ = p % 2
            slot = k % 3
            for ti in range(4):
                i = tb * 4 + ti
                for n in range(2):
                    bank = 6 + dn_cnt[0] % 2
                    dn_cnt[0] += 1
                    for f in range(nfc):
                        mm(psb(bank), actT[slot][:, f, ti * 128:(ti + 1) * 128], Db[s_][:, f, n * 512:(n + 1) * 512],
                           start=(f == 0), stop=(f == nfc - 1))
                    tt("dve", Rt[i][:, n * 512:(n + 1) * 512], psb(bank), Rt[i][:, n * 512:(n + 1) * 512], ALU.add)
                if p == len(pieces) - 1:
                    out_ops.append(dma("sp", out_d[i * 128:(i + 1) * 128, :], Rt[i], ch_o[i % 4]))
            if tb == 3 and p + 2 < len(pieces):
                load_piece(p + 2)

        gateup(0)
        for k in range(1, len(units)):
            gateup(k)
            down(k - 1)
        down(len(units) - 1)

        sch.op("sp", None, extra_deps=out_ops + dbg_ops, force=True)

        sch.resolve(eng_sems)
        block = es.enter_context(nc.Block())
        block.tensor(sch.runner("pe"))
        block.scalar(sch.runner("act"))
        block.vector(sch.runner("dve"))
        block.gpsimd(sch.runner("pool"))
        block.sync(sch.runner("sp"))
    return nc, dbg_d


def _host_consts():
    f32 = np.float32
    f64 = np.float64
    inv_freq = f64(10000.0) ** (-(np.arange(16, dtype=f64)) / f64(16))
    t = np.arange(S)
    row = (t // 64).astype(f64)
    col = (t % 64).astype(f64)
    ang_row = row[:, None] * inv_freq[None, :]
    ang_col = col[:, None] * inv_freq[None, :]
    ang = np.zeros((64, S), f64)
    for d in range(64):
        a = ang_row if d < 32 else ang_col
        ang[d] = a[:, d % 16]
    cos = np.cos(ang)
    sin = np.sin(ang)
    rope = np.concatenate([np.tile(cos, (2, 1)), np.tile(sin, (2, 1))], axis=1).astype(f32)
    ident = np.eye(128, dtype=f32)
    ones_bd = np.zeros((128, 128), f32)
    ones_bd[:64, :64] = 1
    ones_bd[64:, 64:] = 1
    rr = np.zeros((128, 128), f32)
    perm = np.zeros(128, np.int64)
    for m in range(128):
        if m % 32 < 16:
            rr[m + 16, m] = -1.0
            perm[m] = m + 16
        else:
            rr[m - 16, m] = 1.0
            perm[m] = m - 16
    constb = np.concatenate([ident, ones_bd, rr], axis=1).astype(f32)
    fix = np.ones((4, 16), f64)
    for g, w in enumerate([2, 4, 8, 16]):
        for jj in range(8):
            for side, tt_ in ((0, jj), (1, S - 8 + jj)):
                lo = min(max(tt_ - w // 2, 0), S)
                hi = min(max(tt_ - w // 2 + w, 0), S)
                fix[g, side * 8 + jj] = f64(w) / f64(hi - lo)
    return rope, constb, fix.astype(f32), perm


_NC_CACHE = {}


def _prep_inputs(x, norm1_g, w_in, q_norm_g, k_norm_g, pool_w, pool_b, pool_scale, w_out, norm2_g,
                 w_gate, w_up, w_down):
    f32 = np.float32
    rope, constb, fix, perm = _host_consts()
    constf = np.zeros((128, NF), f32)
    constf[:, 0:8] = np.asarray(norm1_g, f32).reshape(8, 128).T
    constf[:, 8:16] = np.asarray(norm2_g, f32).reshape(8, 128).T
    constf[:, 16] = np.tile(np.asarray(q_norm_g, f32).reshape(64), 2)
    constf[:, 17] = np.tile(np.asarray(k_norm_g, f32).reshape(64), 2)
    constf[:, 18:22] = np.asarray(pool_b, f32).reshape(4, 128).T
    constf[:, 22:26] = np.asarray(pool_scale, f32).reshape(4, 128).T
    constf[:, 26:90] = np.asarray(q_norm_g, f32).reshape(1, 64)
    constf[:, 90:154] = np.asarray(k_norm_g, f32).reshape(1, 64)
    constf[:, 154:218] = fix.reshape(1, 64)
    constf[:, 218] = np.tile(np.asarray(q_norm_g, f32).reshape(64), 2)[perm]
    constf[:, 219] = np.tile(np.asarray(k_norm_g, f32).reshape(64), 2)[perm]
    shared = {
        "w_in": np.ascontiguousarray(np.asarray(w_in, f32)[0]),
        "w_out": np.ascontiguousarray(np.asarray(w_out, f32)[0]),
        "w_gate": np.ascontiguousarray(np.asarray(w_gate, f32)[0]),
        "w_up": np.ascontiguousarray(np.asarray(w_up, f32)[0]),
        "w_down": np.ascontiguousarray(np.asarray(w_down, f32)[0]),
        "pool_w": np.ascontiguousarray(np.asarray(pool_w, f32)[0]),
        "constf": constf,
        "constb": constb,
        "rope": rope,
    }
    xs = np.asarray(x, f32)
    in_maps = []
    for c in range(N_CORES):
        m = dict(shared)
        m["x"] = np.ascontiguousarray(xs[c])
        in_maps.append(m)
    return in_maps


def kernel(x, norm1_g, w_in, q_norm_g, k_norm_g, pool_w, pool_b, pool_scale, w_out, norm2_g,
           w_gate, w_up, w_down):
    in_maps = _prep_inputs(x, norm1_g, w_in, q_norm_g, k_norm_g, pool_w, pool_b, pool_scale, w_out,
                           norm2_g, w_gate, w_up, w_down)
    if "nc" not in _NC_CACHE:
        _NC_CACHE["nc"] = build_nc(debug=False)[0]
    nc = _NC_CACHE["nc"]
    res = run_bass_kernel_spmd(nc, in_maps, core_ids=list(range(N_CORES)))
    out = np.stack([np.asarray(r["out"], np.float32) for r in res.results], axis=0)
    return out
```

```python
import math
from contextlib import ExitStack

import numpy as np
import concourse.bass as bass
import concourse.mybir as mybir
from concourse.bass_utils import run_bass_kernel_spmd

F32 = mybir.dt.float32
BF16 = mybir.dt.bfloat16
ALU = mybir.AluOpType
AF = mybir.ActivationFunctionType
AX = mybir.AxisListType

S = 2048
D = 1024
NT = 16
NKC = 8
DFF = 2816
EPS = 1e-6
N_CORES = 8

K = 1024
R0 = 0
H0 = 64 * K
M0 = 96 * K
Q0 = 128 * K
X0 = 160 * K
ARENA_BYTES = X0 + 47 * K
NF = 224


def _dsize(dt):
    return mybir.dt.size(dt)


class _Op:
    __slots__ = ("eng", "emit", "deps", "signal", "sem", "val", "chan", "waits", "idx")


class _Chan:
    def __init__(self, sem):
        self.sem = sem
        self.count = 0
        self.last = None


class Sched:
    def __init__(self):
        self.ops = []
        self.frozen = False
        self.recs = {"SB": [], "PSUM": []}

    @staticmethod
    def ranges(ap):
        sp = str(ap.space)
        if sp not in ("SB", "PSUM"):
            return None, []
        dims = ap.ap
        pstride = dims[0][0]
        es = _dsize(ap.dtype)
        off = ap.offset % pstride if pstride > 0 else ap.offset
        free = sorted([(s, c) for s, c in dims[1:] if c > 1 and s > 0])
        run = 1
        rest = []
        for s, c in free:
            if s == run and not rest:
                run *= c
            else:
                rest.append((s, c))
        nouter = 1
        for s, c in rest:
            nouter *= c
        out = []
        if nouter <= 64:
            starts = [off]
            for s, c in rest:
                starts = [b + s * k for b in starts for k in range(c)]
            for b in starts:
                out.append((b * es, (b + run) * es))
        else:
            hi = off + run
            for s, c in rest:
                hi += s * (c - 1)
            out.append((off * es, hi * es))
        if sp == "PSUM":
            out = [((lo // 2048) * 2048, ((hi + 2047) // 2048) * 2048) for lo, hi in out]
        out.sort()
        merged = []
        for lo, hi in out:
            if merged and lo <= merged[-1][1]:
                merged[-1] = (merged[-1][0], max(hi, merged[-1][1]))
            else:
                merged.append((lo, hi))
        return sp, merged

    def op(self, eng, emit, reads=(), writes=(), chan=None, extra_deps=(), force=False):
        if self.frozen and not force:
            return None
        extra_deps = [d for d in extra_deps if d is not None]
        o = _Op()
        o.eng = eng
        o.emit = emit
        o.deps = set(extra_deps)
        o.signal = False
        o.sem = None
        o.val = None
        o.chan = chan
        o.waits = []
        o.idx = len(self.ops)
        is_dma = chan is not None
        for ap in reads:
            sp, rs = self.ranges(ap)
            if sp is None:
                continue
            recs = self.recs[sp]
            for lo, hi in rs:
                keep = []
                for r in recs:
                    if r[2] != o.idx and r[0] < hi and lo < r[1]:
                        if r[3]:
                            o.deps.add(r[2])
                        elif sp == "PSUM" and self.ops[r[2]].eng != eng:
                            o.deps.add(r[2])
                        elif (not is_dma) and self.ops[r[2]].chan is None and self.ops[r[2]].eng == eng \
                                and lo <= r[0] and r[1] <= hi:
                            continue
                    keep.append(r)
                keep.append([lo, hi, o.idx, False])
                self.recs[sp] = recs = keep
        for ap in writes:
            sp, rs = self.ranges(ap)
            if sp is None:
                continue
            recs = self.recs[sp]
            for lo, hi in rs:
                keep = []
                for r in recs:
                    if r[0] < hi and lo < r[1]:
                        if r[2] != o.idx:
                            o.deps.add(r[2])
                        if lo <= r[0] and r[1] <= hi:
                            continue
                    keep.append(r)
                keep.append([lo, hi, o.idx, True])
                self.recs[sp] = recs = keep
        if is_dma:
            if chan.last is not None:
                o.deps.add(chan.last)
            chan.last = o.idx
        o.deps.discard(o.idx)
        self.ops.append(o)
        return o.idx

    def resolve(self, eng_sems):
        ops = self.ops

        def skip(o, d):
            return o.eng == "pe" and d.eng == "pe" and o.chan is None and d.chan is None

        for o in ops:
            for di in o.deps:
                d = ops[di]
                if not skip(o, d):
                    d.signal = True
        counters = {e: 0 for e in eng_sems}
        for o in ops:
            if o.chan is not None:
                o.chan.count += 16
                o.sem = o.chan.sem
                o.val = o.chan.count
            elif o.signal:
                counters[o.eng] += 1
                o.sem = eng_sems[o.eng]
                o.val = counters[o.eng]
        waited = {}
        for o in ops:
            w = {}
            wd = waited.setdefault(o.eng, {})
            for di in o.deps:
                d = ops[di]
                if skip(o, d):
                    continue
                key = id(d.sem)
                if wd.get(key, (None, 0))[1] >= d.val:
                    continue
                if key not in w or w[key][1] < d.val:
                    w[key] = (d.sem, d.val)
            for key, sv in w.items():
                wd[key] = sv
            o.waits = list(w.values())

    def runner(self, eng):
        mine = [o for o in self.ops if o.eng == eng]

        def run(e):
            for o in mine:
                for sem, val in o.waits:
                    e.wait_ge(sem, val)
                if o.emit is None:
                    continue
                ins = o.emit(e)
                if o.chan is not None:
                    ins.then_inc(o.sem, 16)
                elif o.signal:
                    ins.then_inc(o.sem, 1)
        return run


def build_nc(debug=False, stop=99, use_ln=True, interleave=True):
    nc = bass.Bass("TRN2", target_bir_lowering=False)
    dr = {}

    def din(name, shape):
        dr[name] = nc.dram_tensor(name, list(shape), F32, kind="ExternalInput").ap()
        return dr[name]

    x_d = din("x", [S, D])
    w_in_d = din("w_in", [D, 1280])
    w_out_d = din("w_out", [D, D])
    w_gate_d = din("w_gate", [D, DFF])
    w_up_d = din("w_up", [D, DFF])
    w_down_d = din("w_down", [DFF, D])
    pool_w_d = din("pool_w", [4, 128, 128])
    constf_d = din("constf", [128, NF])
    constb_d = din("constb", [128, 384])
    rope_d = din("rope", [128, 2 * S])
    out_d = nc.dram_tensor("out", [S, D], F32, kind="ExternalOutput").ap()
    dbg_d = {}

    sch = Sched()

    with ExitStack() as es:
        es.enter_context(nc.allow_low_precision("bf16 matmul operands, fp32 accumulation"))
        arena = es.enter_context(nc.sbuf_tensor("arena", [128, ARENA_BYTES // 4], F32))
        ps = es.enter_context(nc.psum_tensor("ps", [128, 8, 512], F32))
        eng_sems = {e: es.enter_context(nc.semaphore("sem_" + e)) for e in ("pe", "act", "dve", "pool", "sp")}

        def new_chan(name):
            return _Chan(es.enter_context(nc.semaphore("ch_" + name)))

        def sbv(off, dt, *shape):
            n = 1
            for s_ in shape:
                n *= s_
            nb = n * _dsize(dt)
            assert off % 4 == 0 and nb % 4 == 0 and off + nb <= ARENA_BYTES, (off, nb)
            v = arena[:, off // 4:(off + nb) // 4]
            if dt != F32:
                v = v.bitcast(dt)
            if len(shape) == 2:
                v = v.rearrange("p (a b) -> p a b", a=shape[0], b=shape[1])
            elif len(shape) == 3:
                v = v.rearrange("p (a b c) -> p a b c", a=shape[0], b=shape[1], c=shape[2])
            return v

        def psb(bank):
            return ps[:, bank, :]

        def psb_bf(bank):
            return ps[:, bank, :].bitcast(BF16)

        _bank = [0]

        def nb_():
            b = _bank[0]
            _bank[0] = (b + 1) % 8
            return b

        def dma(q, out, in_, chan):
            return sch.op(q, lambda e: e.dma_start(out=out, in_=in_), reads=[in_], writes=[out], chan=chan)

        def mm(out, lhsT, rhs, start=True, stop=True):
            return sch.op("pe", lambda e: e.matmul(out, lhsT, rhs, start=start, stop=stop),
                          reads=[lhsT, rhs], writes=[out])

        def tr(out, in_, ident):
            return sch.op("pe", lambda e: e.transpose(out, in_, ident), reads=[in_, ident], writes=[out])

        def act(out, in_, func, scale=1.0, bias=None, accum=None):
            reads = [in_]
            writes = [out]
            kw = {}
            if not isinstance(scale, (int, float)):
                reads.append(scale)
            if bias is not None:
                kw["bias"] = bias
                if not isinstance(bias, (int, float)):
                    reads.append(bias)
            if accum is not None:
                kw["accum_out"] = accum
                writes.append(accum)
            return sch.op("act", lambda e: e.activation(out, in_, func, scale=scale, **kw),
                          reads=reads, writes=writes)

        def tt(eng, out, in0, in1, op):
            return sch.op(eng, lambda e: e.tensor_tensor(out, in0, in1, op), reads=[in0, in1], writes=[out])

        def ts(eng, out, in0, s1, s2, op0, op1=None):
            reads = [in0] + [s_ for s_ in (s1, s2) if s_ is not None and not isinstance(s_, (int, float))]
            if op1 is None:
                return sch.op(eng, lambda e: e.tensor_scalar(out, in0, s1, None, op0), reads=reads, writes=[out])
            return sch.op(eng, lambda e: e.tensor_scalar(out, in0, s1, s2, op0, op1), reads=reads, writes=[out])

        def cp(eng, out, in_):
            return sch.op(eng, lambda e: e.tensor_copy(out, in_), reads=[in_], writes=[out])

        def recip(out, in_):
            return sch.op("dve", lambda e: e.reciprocal(out, in_), reads=[in_], writes=[out])

        def memset(eng, ap, val):
            return sch.op(eng, lambda e: e.memset(ap, val), writes=[ap])

        C0 = X0 + 38 * K
        constf = sbv(C0, F32, NF)
        constb = sbv(C0 + 1024, BF16, 384)
        pw = sbv(C0 + 2048, BF16, 4, 128)
        stats = sbv(C0 + 3072, F32, 128)
        rec = [sbv(C0 + 3584 + i * 2048, F32, 512) for i in range(2)]
        g1v = constf[:, 0:8]
        g2v = constf[:, 8:16]
        gq = constf[:, 16:17]
        gk = constf[:, 17:18]
        pool_b = constf[:, 18:22]
        pool_sc = constf[:, 22:26]
        gq_row = constf[:, 26:90]
        gk_row = constf[:, 90:154]
        fixtab = constf[:, 154:218]
        gq_perm = constf[:, 218:219]
        gk_perm = constf[:, 219:220]
        ident = constb[:, 0:128]
        ones_bd = constb[:, 128:256]
        rrot = constb[:, 256:384]
        ss1 = stats[:, 0:16]
        ln1 = stats[:, 16:32]
        rstd1 = stats[:, 32:48]
        ss2 = stats[:, 48:64]
        ln2 = stats[:, 64:80]
        rstd2 = stats[:, 80:96]
        mq = stats[:, 96:97]
        mk = stats[:, 97:98]
        negc = stats[:, 98:99]
        epsq = stats[:, 99:100]
        eps1 = stats[:, 100:101]
        pool_bs = stats[:, 104:108]

        hT = sbv(H0, BF16, NKC, S)
        mixT = sbv(M0, BF16, 8, S)
        qT = sbv(Q0, BF16, 4, S)
        kdup = sbv(Q0 + 16 * K, BF16, 2, S)
        v_aug = sbv(Q0 + 24 * K, BF16, NT, 2, 128)
        cosg_q = sbv(X0, F32, S)
        sing_q = sbv(X0 + 8 * K, F32, S)
        cosg_k = sbv(M0, F32, S)
        sing_k = sbv(M0 + 8 * K, F32, S)
        wring = [sbv(X0 + 16 * K + i * 2048, BF16, NKC, 128) for i in range(4)]
        PT = [sbv(X0 + 32 * K + i * 2048, BF16, 1024) for i in range(3)]
        wout_sb = sbv(X0, BF16, NKC, D)
        UPAD = 16
        UW = S + 2 * UPAD
        Ubuf = [sbv(R0 + i * UW * 4, F32, UW) for i in range(2)]
        tmpAB = [sbv(R0 + (2 + i) * UW * 4, F32, UW) for i in range(2)]
        pooled = sbv(R0 + 4 * UW * 4, BF16, S)
        QK0 = R0 + 4 * UW * 4 + 4096

        class _Scr:
            pass

        scr = []
        NSCR = 3
        for i in range(NSCR):
            b = QK0 + i * 8 * K
            s_ = _Scr()
            s_.sq = sbv(b, BF16, 512)
            s_.abf = sbv(b + 1 * K, BF16, 512)
            s_.rs = sbv(b + 2 * K, F32, 512)
            s_.t2 = sbv(b + 4 * K, F32, 512)
            s_.t1 = sbv(b + 6 * K, F32, 512)
            scr.append(s_)
        assert QK0 + NSCR * 8 * K <= R0 + 64 * K
        xring = [sbv(M0 + 16 * K + i * 4 * K, F32, D) for i in range(4)]
        xn = [sbv(X0 + 32 * K + i * 2 * K, BF16, D) for i in range(3)]
        junk = sbv(X0 + 24 * K, BF16, D)
        Rt = [sbv(R0 + i * 4 * K, F32, D) for i in range(NT)]
        h2T = sbv(Q0, BF16, NKC, S)
        Gb = [sbv(H0, BF16, NKC, 512), sbv(H0 + 24 * K, BF16, NKC, 512)]
        Ub = [sbv(H0 + 8 * K, BF16, NKC, 512), sbv(X0 + 16 * K, BF16, NKC, 512)]
        Db = [sbv(H0 + 16 * K, BF16, 4, D), sbv(X0 + 24 * K, BF16, 4, D)]
        actT = [sbv(M0 + i * 4 * K, BF16, 4, 512) for i in range(3)]
        sg = [sbv(M0 + 12 * K + i * 2 * K, F32, 512) for i in range(2)]
        xn2 = [sbv(X0 + 32 * K + i * 2 * K, BF16, D) for i in range(3)]
        junk2 = sbv(X0 + 38 * K + 3584, BF16, D)

        ch_c = [new_chan("c%d" % i) for i in range(6)]
        ch_x = [new_chan("x%d" % i) for i in range(4)]
        ch_w = [new_chan("w%d" % i) for i in range(4)]
        ch_wo = [new_chan("wo%d" % i) for i in range(2)]
        ch_g = [new_chan("g%d" % i) for i in range(2)]
        ch_u = [new_chan("u%d" % i) for i in range(2)]
        ch_d = [new_chan("d%d" % i) for i in range(2)]
        ch_o = [new_chan("o%d" % i) for i in range(4)]
        ch_dbg = new_chan("dbg")

        def dbg_out(name, ap, shape, dt):
            if not debug:
                return
            d = nc.dram_tensor("dbg_" + name, [128] + list(shape), dt, kind="ExternalOutput").ap()
            dbg_d[name] = d
            dbg_ops.append(dma("sp", d, ap, ch_dbg))

        dbg_ops = []

        dma("sp", constf, constf_d, ch_c[0])
        dma("pool", constb, constb_d, ch_c[1])
        dma("pool", pw, pool_w_d.rearrange("g c d -> c g d"), ch_c[2])
        w_in_v = w_in_d.rearrange("(kc p) n -> p kc n", p=128)

        chunks = [("kd", 0), ("kd", 1), ("v", 0), ("u", 0), ("q", 0), ("u", 1), ("q", 1),
                  ("u", 2), ("q", 2), ("u", 3), ("q", 3)]
        NPRE = 4

        def load_chunk(ci):
            kind, j = chunks[ci]
            slot = ci % 4
            if kind == "q":
                dma("pool", wring[slot], w_in_v[:, :, j * 128:(j + 1) * 128], ch_w[slot])
            elif kind == "kd":
                c0 = 512 + j * 64
                dma("pool", wring[slot][:, :, 0:64], w_in_v[:, :, c0:c0 + 64], ch_w[slot])
                dma("pool", wring[slot][:, :, 64:128], w_in_v[:, :, c0:c0 + 64], ch_w[slot])
            elif kind == "v":
                dma("pool", wring[slot], w_in_v[:, :, 640:768], ch_w[slot])
            else:
                c0 = 768 + j * 128
                dma("pool", wring[slot], w_in_v[:, :, c0:c0 + 128], ch_w[slot])

        for ci in range(4):
            load_chunk(ci)

        memset("dve", epsq, 64.0 * EPS)
        memset("dve", eps1, EPS)
        tt("dve", pool_bs, pool_b, pool_sc, ALU.mult)
        memset("pool", v_aug[:, :, :, 64:128], 1.0)
        tt("dve", rec[0][:, 0:64], gq_row, gq_row, ALU.mult)
        sch.op("dve", lambda e: e.reduce_max(out=mq, in_=rec[0][:, 0:64], axis=AX.X),
               reads=[rec[0][:, 0:64]], writes=[mq])
        tt("dve", rec[0][:, 64:128], gk_row, gk_row, ALU.mult)
        sch.op("dve", lambda e: e.reduce_max(out=mk, in_=rec[0][:, 64:128], axis=AX.X),
               reads=[rec[0][:, 64:128]], writes=[mk])
        tt("dve", negc, mq, mk, ALU.mult)
        if use_ln:
            act(negc, negc, AF.Ln)
            act(negc, negc, AF.Exp, scale=0.5)
        else:
            act(negc, negc, AF.Sqrt)
        sch.op("dve", lambda e: e.tensor_scalar(negc, negc, -8.0, None, ALU.mult), reads=[negc], writes=[negc])

        def norm_p1(i, xt, ss, jk):
            act(jk, xt, AF.Square, accum=ss[:, i:i + 1])

        def norm_p2(i, xt, ss, lnv, rstd, xnb, scale_eng="act"):
            act(lnv[:, i:i + 1], ss[:, i:i + 1], AF.Ln, scale=1.0 / D, bias=eps1)
            act(rstd[:, i:i + 1], lnv[:, i:i + 1], AF.Exp, scale=-0.5)
            if scale_eng == "act":
                act(xnb, xt, AF.Copy, scale=rstd[:, i:i + 1])
            else:
                ts(scale_eng, xnb, xt, rstd[:, i:i + 1], None, ALU.mult)

        def norm_transpose(i, xnb, gv, dstT):
            bank = nb_()
            pb = psb_bf(bank)
            for kc in range(NKC):
                tr(pb[:, kc * 128:(kc + 1) * 128], xnb[:, kc * 128:(kc + 1) * 128], ident)
            g3 = gv.rearrange("p (a b) -> p a b", b=1).to_broadcast([128, NKC, 128])
            tt("dve", dstT[:, :, i * 128:(i + 1) * 128], pb.rearrange("p (a b) -> p a b", b=128), g3, ALU.mult)

        unit = [0]
        pending = []
        pool_mm_pending = []
        WIN = [2, 4, 8, 16]

        def proc_chunk_tb(ci, tb):
            kind, j = chunks[ci]
            W = wring[ci % 4]
            cols = slice(tb * 512, (tb + 1) * 512)
            if kind == "v":
                for i in range(tb * 4, tb * 4 + 4):
                    bank = nb_()
                    o_ = psb(bank)[:, 0:128]
                    for kc in range(NKC):
                        mm(o_, hT[:, kc, i * 128:(i + 1) * 128], W[:, kc, :], start=(kc == 0), stop=(kc == NKC - 1))
                    cp("dve", v_aug[:, i, :, 0:64], o_.rearrange("p (a b) -> p a b", a=2, b=64))
                return
            bA = nb_()
            for kc in range(NKC):
                mm(psb(bA), W[:, kc, :], hT[:, kc, cols], start=(kc == 0), stop=(kc == NKC - 1))
            if kind == "u":
                act(Ubuf[j % 2][:, UPAD + tb * 512:UPAD + (tb + 1) * 512], psb(bA), AF.Copy)
                return
            sc_ = scr[unit[0] % NSCR]
            unit[0] += 1
            if kind == "q":
                cg, sgn, dest = cosg_q, sing_q, qT[:, j, cols]
            else:
                cg, sgn, dest = cosg_k, sing_k, kdup[:, j, cols]
            act(sc_.sq, psb(bA), AF.Square)
            act(sc_.abf, psb(bA), AF.Copy)
            tt("dve", sc_.t1, psb(bA), cg[:, cols], ALU.mult)

            def part2(sc_=sc_, sgn=sgn, dest=dest, cols=cols):
                bB = nb_()
                bC = nb_()
                mm(psb(bB), ones_bd, sc_.sq)
                mm(psb(bC), rrot, sc_.abf)
                act(sc_.rs, psb(bB), AF.Ln, bias=epsq)
                act(sc_.rs, sc_.rs, AF.Exp, scale=-0.5)
                tt("dve", sc_.t2, psb(bC), sgn[:, cols], ALU.mult)
                tt("dve", sc_.t1, sc_.t1, sc_.t2, ALU.add)
                tt("dve", dest, sc_.t1, sc_.rs, ALU.mult)

            pending.append(part2)
            while len(pending) > 1:
                pending.pop(0)()

        def flush_pending():
            while pending:
                pending.pop(0)()

        def finish_chunk(ci):
            kind, j = chunks[ci]
            if kind != "u":
                return
            g = j
            U = Ubuf[g % 2]
            prev = U
            exts = [8, 6, 4, 0]
            shifts = [(-1, 0), (-1, 1), (-2, 2), (-4, 4)]
            for l in range(g + 1):
                e_ = exts[l]
                dst = tmpAB[l % 2]
                lo = UPAD - e_
                n_ = S + 2 * e_
                s0, s1 = shifts[l]
                tt("dve", dst[:, lo:lo + n_], prev[:, lo + s0:lo + s0 + n_], prev[:, lo + s1:lo + s1 + n_], ALU.add)
                prev = dst
            Fv = prev
            tt("dve", Fv[:, UPAD:UPAD + 8], Fv[:, UPAD:UPAD + 8], fixtab[:, g * 16:g * 16 + 8], ALU.mult)
            tt("dve", Fv[:, UPAD + S - 8:UPAD + S], Fv[:, UPAD + S - 8:UPAD + S],
               fixtab[:, g * 16 + 8:g * 16 + 16], ALU.mult)
            fin = Fv[:, UPAD:UPAD + S]
            uin = U[:, UPAD:UPAD + S]
            winv = 1.0 / WIN[g]
            sch.op("dve", lambda e, fin=fin, uin=uin, winv=winv: e.scalar_tensor_tensor(
                pooled, fin, winv, uin, ALU.mult, ALU.subtract), reads=[fin, uin], writes=[pooled])

            def pool_mm(g=g, banks=None):
                for tb in range(4):
                    cols = slice(tb * 512, (tb + 1) * 512)
                    bk = banks[tb % len(banks)] if banks else nb_()
                    mm(psb(bk), pw[:, g, :], pooled[:, cols])
                    act(mixT[:, 4 + g, cols], psb(bk), AF.Identity, scale=pool_sc[:, g:g + 1], bias=pool_bs[:, g:g + 1])
            pool_mm_pending.append(pool_mm)

        def flush_pool_mm(banks=None):
            while pool_mm_pending:
                pool_mm_pending.pop(0)(banks=banks)

        NXR = len(xring)
        pre_q = []
        for i in range(NT + 2):
            if i < NT:
                dma("sp", xring[i % NXR], x_d[i * 128:(i + 1) * 128, :], ch_x[i % NXR])
            if i == 1:
                dma("sp", cosg_k, rope_d[:, 0:S], ch_c[3])
                dma("sp", sing_k, rope_d[:, S:2 * S], ch_c[4])
            if i == 4:
                ts("dve", cosg_k, cosg_k, gk, None, ALU.mult)
                ts("dve", sing_k, sing_k, gk_perm, None, ALU.mult)
            if i == 9:
                dma("sp", cosg_q, rope_d[:, 0:S], ch_c[5])
                dma("sp", sing_q, rope_d[:, S:2 * S], ch_c[3])
                ts("dve", cosg_q, cosg_q, gq, None, ALU.mult)
                ts("dve", sing_q, sing_q, gq_perm, None, ALU.mult)
            if i < NT:
                norm_p1(i, xring[i % NXR], ss1, junk)
            if 1 <= i <= NT:
                norm_p2(i - 1, xring[(i - 1) % NXR], ss1, ln1, rstd1, xn[(i - 1) % 3], scale_eng=("act" if (i % 2) else "dve"))
            if i >= 2:
                t_ = i - 2
                norm_transpose(t_, xn[t_ % 3], g1v, hT)
                if interleave and t_ % 4 == 3:
                    pre_q.extend((ci, t_ // 4) for ci in range(NPRE))
            if pre_q:
                proc_chunk_tb(*pre_q.pop(0))
        while pre_q:
            proc_chunk_tb(*pre_q.pop(0))
        if stop <= 0:
            sch.frozen = True
        if not interleave:
            for tb in range(4):
                for ci in range(NPRE):
                    proc_chunk_tb(ci, tb)
        for i in range(2):
            memset("pool", Ubuf[i][:, 0:UPAD], 0.0)
            memset("pool", Ubuf[i][:, UPAD + S:UW], 0.0)
        dbg_out("hT", hT, [NKC, S], BF16)
        if stop <= 1:
            sch.frozen = True
        for ci in range(NPRE):
            finish_chunk(ci)
            if ci + 4 < len(chunks):
                load_chunk(ci + 4)
        groups = [(4, 5), (6, 7), (8, 9), (10,)]
        for grp in groups:
            for tb in range(4):
                for ci in grp:
                    proc_chunk_tb(ci, tb)
                if tb == 3:
                    flush_pool_mm()
            for ci in grp:
                finish_chunk(ci)
                if ci + 4 < len(chunks):
                    load_chunk(ci + 4)
        flush_pending()
        dbg_out("qT", qT, [4, S], BF16)
        dbg_out("kdup", kdup, [2, S], BF16)
        dbg_out("v_aug", v_aug, [NT, 2, 128], BF16)

        if stop <= 2:
            sch.frozen = True
        w_out_v = w_out_d.rearrange("(kc p) n -> p kc n", p=128)
        for hlf in range(2):
            dma("pool", wout_sb[:, hlf * 4:(hlf + 1) * 4, :], w_out_v[:, hlf * 4:(hlf + 1) * 4, :], ch_wo[hlf])
        w_gate_v = w_gate_d.rearrange("(kc p) n -> p kc n", p=128)
        w_up_v = w_up_d.rearrange("(kc p) n -> p kc n", p=128)
        pieces = [(0, 4), (4, 4), (8, 4), (12, 4), (16, 4), (20, 2)]

        def load_piece(p):
            fc0, nfc = pieces[p]
            s_ = p % 2
            c0 = fc0 * 128
            ncol = nfc * 128
            dma("pool", Gb[s_][:, :, 0:ncol], w_gate_v[:, :, c0:c0 + ncol], ch_g[s_])
            dma("pool", Ub[s_][:, :, 0:ncol], w_up_v[:, :, c0:c0 + ncol], ch_u[s_])
            dma("pool", Db[s_][:, 0:nfc, :], w_down_d[c0:c0 + ncol, :].rearrange("(fc p) n -> p fc n", p=128),
                ch_d[s_])

        load_piece(0)
        for i in range(NT):
            dma("sp", Rt[i], x_d[i * 128:(i + 1) * 128, :], ch_x[i % 4])

        flush_pool_mm()
        steps = [(j, qc, sc) for j in range(4) for qc in range(4) for sc in range(16)]
        ocp = [sbv(X0 + 16 * K + i * 2 * K, F32, 512) for i in range(2)]

        def mm1(idx):
            j, qc, sc = steps[idx]
            kh = j // 2
            sb_ = idx % 3
            q0 = qc * 512
            for hb in range(2):
                pr = hb * 64
                mm(psb(2 * sb_ + hb), kdup[pr:pr + 64, kh, sc * 128:(sc + 1) * 128], qT[pr:pr + 64, j, q0:q0 + 512])

        def expo(idx):
            sb_ = idx % 3
            src = ps[:, 2 * sb_:2 * sb_ + 2, :].rearrange("p a b -> p (a b)")
            act(PT[idx % 3], src, AF.Exp, scale=8.0, bias=negc)

        def mm2(idx):
            j, qc, sc = steps[idx]
            kh = j // 2
            for hb in range(2):
                mm(psb(6 + hb), v_aug[:, sc, kh, :], PT[idx % 3][:, hb * 512:(hb + 1) * 512],
                   start=(sc == 0), stop=(sc == 15))

        def onorm(j, qc):
            q0 = qc * 512
            for hb in range(2):
                cp("dve", ocp[hb], psb(6 + hb))
            for hb in range(2):
                pr = hb * 64
                recip(rec[hb][0:64, :], ocp[hb][64:128, :])
                tt("dve", mixT[pr:pr + 64, j, q0:q0 + 512], ocp[hb][0:64, :], rec[hb][0:64, :], ALU.mult)

        mm1(0)
        mm1(1)
        for idx in range(len(steps)):
            if idx + 2 < len(steps):
                mm1(idx + 2)
            expo(idx)
            mm2(idx)
            j, qc, sc = steps[idx]
            if sc == 15:
                onorm(j, qc)
        dbg_out("mixT", mixT, [8, S], BF16)
        if stop <= 3:
            sch.frozen = True
        load_piece(1)

        for i in range(NT + 3):
            if i < NT:
                for n in range(2):
                    bank = nb_()
                    for kc in range(NKC):
                        mm(psb(bank), mixT[:, kc, i * 128:(i + 1) * 128], wout_sb[:, kc, n * 512:(n + 1) * 512],
                           start=(kc == 0), stop=(kc == NKC - 1))
                    tt("dve", Rt[i][:, n * 512:(n + 1) * 512], psb(bank), Rt[i][:, n * 512:(n + 1) * 512], ALU.add)
            if 1 <= i <= NT:
                norm_p1(i - 1, Rt[i - 1], ss2, junk2)
            if 2 <= i <= NT + 1:
                norm_p2(i - 2, Rt[i - 2], ss2, ln2, rstd2, xn2[(i - 2) % 3])
            if i >= 3:
                norm_transpose(i - 3, xn2[(i - 3) % 3], g2v, h2T)
        if debug:
            d = nc.dram_tensor("dbg_x1", [S, D], F32, kind="ExternalOutput").ap()
            dbg_d["x1"] = d
            for i in range(NT):
                dbg_ops.append(dma("sp", d[i * 128:(i + 1) * 128, :], Rt[i], ch_dbg))
        dbg_out("h2T", h2T, [NKC, S], BF16)

        units = [(p, tb) for p in range(len(pieces)) for tb in range(4)]
        out_ops = []
        gu_cnt = [0]

        def gateup(k):
            p, tb = units[k]
            fc0, nfc = pieces[p]
            s_ = p % 2
            slot = k % 3
            cols = slice(tb * 512, (tb + 1) * 512)
            for f in range(nfc):
                pair = gu_cnt[0] % 3
                gu_cnt[0] += 1
                bg, bu = 2 * pair, 2 * pair + 1
                for kc in range(NKC):
                    mm(psb(bg), Gb[s_][:, kc, f * 128:(f + 1) * 128], h2T[:, kc, cols], start=(kc == 0),
                       stop=(kc == NKC - 1))
                for kc in range(NKC):
                    mm(psb(bu), Ub[s_][:, kc, f * 128:(f + 1) * 128], h2T[:, kc, cols], start=(kc == 0),
                       stop=(kc == NKC - 1))
                sgb = sg[gu_cnt[0] % 2]
                act(sgb, psb(bg), AF.Silu)
                tt("dve", actT[slot][:, f, :], psb(bu), sgb, ALU.mult)

        dn_cnt = [0]

        def down(k):
            p, tb = units[k]
            fc0, nfc = pieces[p]
            s_ = p % 2
            slot = k % 3
            for ti in range(4):
                i = tb * 4 + ti
                for n in range(2):
                    bank = 6 + dn_cnt[0] % 2
                    dn_cnt[0] += 1
                    for f in range(nfc):
                        mm(psb(bank), actT[slot][:, f, ti * 128:(ti + 1) * 128], Db[s_][:, f, n * 512:(n + 1) * 512],
                           start=(f == 0), stop=(f == nfc - 1))
                    tt("dve", Rt[i][:, n * 512:(n + 1) * 512], psb(bank), Rt[i][:, n * 512:(n + 1) * 512], ALU.add)
                if p == len(pieces) - 1:
                    out_ops.append(dma("sp", out_d[i * 128:(i + 1) * 128, :], Rt[i], ch_o[i % 4]))
            if tb == 3 and p + 2 < len(pieces):
                load_piece(p + 2)

        gateup(0)
        for k in range(1, len(units)):
            gateup(k)
            down(k - 1)
        down(len(units) - 1)

        sch.op("sp", None, extra_deps=out_ops + dbg_ops, force=True)

        sch.resolve(eng_sems)
        block = es.enter_context(nc.Block())
        block.tensor(sch.runner("pe"))
        block.scalar(sch.runner("act"))
        block.vector(sch.runner("dve"))
        block.gpsimd(sch.runner("pool"))
        block.sync(sch.runner("sp"))
    return nc, dbg_d


def _host_consts():
    f32 = np.float32
    f64 = np.float64
    inv_freq = f64(10000.0) ** (-(np.arange(16, dtype=f64)) / f64(16))
    t = np.arange(S)
    row = (t // 64).astype(f64)
    col = (t % 64).astype(f64)
    ang_row = row[:, None] * inv_freq[None, :]
    ang_col = col[:, None] * inv_freq[None, :]
    ang = np.zeros((64, S), f64)
    for d in range(64):
        a = ang_row if d < 32 else ang_col
        ang[d] = a[:, d % 16]
    cos = np.cos(ang)
    sin = np.sin(ang)
    rope = np.concatenate([np.tile(cos, (2, 1)), np.tile(sin, (2, 1))], axis=1).astype(f32)
    ident = np.eye(128, dtype=f32)
    ones_bd = np.zeros((128, 128), f32)
    ones_bd[:64, :64] = 1
    ones_bd[64:, 64:] = 1
    rr = np.zeros((128, 128), f32)
    perm = np.zeros(128, np.int64)
    for m in range(128):
        if m % 32 < 16:
            rr[m + 16, m] = -1.0
            perm[m] = m + 16
        else:
            rr[m - 16, m] = 1.0
            perm[m] = m - 16
    constb = np.concatenate([ident, ones_bd, rr], axis=1).astype(f32)
    fix = np.ones((4, 16), f64)
    for g, w in enumerate([2, 4, 8, 16]):
        for jj in range(8):
            for side, tt_ in ((0, jj), (1, S - 8 + jj)):
                lo = min(max(tt_ - w // 2, 0), S)
                hi = min(max(tt_ - w // 2 + w, 0), S)
                fix[g, side * 8 + jj] = f64(w) / f64(hi - lo)
    return rope, constb, fix.astype(f32), perm


_NC_CACHE = {}


def _prep_inputs(x, norm1_g, w_in, q_norm_g, k_norm_g, pool_w, pool_b, pool_scale, w_out, norm2_g,
                 w_gate, w_up, w_down):
    f32 = np.float32
    rope, constb, fix, perm = _host_consts()
    constf = np.zeros((128, NF), f32)
    constf[:, 0:8] = np.asarray(norm1_g, f32).reshape(8, 128).T
    constf[:, 8:16] = np.asarray(norm2_g, f32).reshape(8, 128).T
    constf[:, 16] = np.tile(np.asarray(q_norm_g, f32).reshape(64), 2)
    constf[:, 17] = np.tile(np.asarray(k_norm_g, f32).reshape(64), 2)
    constf[:, 18:22] = np.asarray(pool_b, f32).reshape(4, 128).T
    constf[:, 22:26] = np.asarray(pool_scale, f32).reshape(4, 128).T
    constf[:, 26:90] = np.asarray(q_norm_g, f32).reshape(1, 64)
    constf[:, 90:154] = np.asarray(k_norm_g, f32).reshape(1, 64)
    constf[:, 154:218] = fix.reshape(1, 64)
    constf[:, 218] = np.tile(np.asarray(q_norm_g, f32).reshape(64), 2)[perm]
    constf[:, 219] = np.tile(np.asarray(k_norm_g, f32).reshape(64), 2)[perm]
    shared = {
        "w_in": np.ascontiguousarray(np.asarray(w_in, f32)[0]),
        "w_out": np.ascontiguousarray(np.asarray(w_out, f32)[0]),
        "w_gate": np.ascontiguousarray(np.asarray(w_gate, f32)[0]),
        "w_up": np.ascontiguousarray(np.asarray(w_up, f32)[0]),
        "w_down": np.ascontiguousarray(np.asarray(w_down, f32)[0]),
        "pool_w": np.ascontiguousarray(np.asarray(pool_w, f32)[0]),
        "constf": constf,
        "constb": constb,
        "rope": rope,
    }
    xs = np.asarray(x, f32)
    in_maps = []
    for c in range(N_CORES):
        m = dict(shared)
        m["x"] = np.ascontiguousarray(xs[c])
        in_maps.append(m)
    return in_maps


def kernel(x, norm1_g, w_in, q_norm_g, k_norm_g, pool_w, pool_b, pool_scale, w_out, norm2_g,
           w_gate, w_up, w_down):
    in_maps = _prep_inputs(x, norm1_g, w_in, q_norm_g, k_norm_g, pool_w, pool_b, pool_scale, w_out,
                           norm2_g, w_gate, w_up, w_down)
    if "nc" not in _NC_CACHE:
        _NC_CACHE["nc"] = build_nc(debug=False)[0]
    nc = _NC_CACHE["nc"]
    res = run_bass_kernel_spmd(nc, in_maps, core_ids=list(range(N_CORES)))
    out = np.stack([np.asarray(r["out"], np.float32) for r in res.results], axis=0)
    return out
```
